# Optimizing a Trainium2 kernel written in Bass

```python
import math
import jax, jax.numpy as jnp
from jax import lax
import numpy as np

D_MODEL = 2048
BATCH = 1
SEQ = 16384
DEPTH = 1
DEC_BATCH = 8
DEC_SEQ = 4096
PAST_LEN = 128

HEAD_DIM = 128
A_HEADS = 8
A_KV_HEADS = 2
A_GROUP = A_HEADS // A_KV_HEADS
WINDOW = 128
BLOCK = 128
ROPE_THETA = 500000.0
ROPE_DIM = HEAD_DIM // 4
B_HEADS = 4
GRID_W = 64
NA_ROWS = 8
NA_COLS = 16
M_HEADS = 4
N_MEM = 256
A_Q_W = A_HEADS * HEAD_DIM
A_KV_W = A_KV_HEADS * HEAD_DIM
B_W = B_HEADS * HEAD_DIM
M_W = M_HEADS * HEAD_DIM
N_BRANCH = 3
IN_SIZES = (A_Q_W, A_KV_W, A_KV_W, B_W, B_W, B_W, M_W, N_BRANCH * D_MODEL)
IN_W = A_Q_W + 2 * A_KV_W + 3 * B_W + M_W + N_BRANCH * D_MODEL
PEER_HEADS = 8
PEER_QDIM = 256
PEER_HALF = PEER_QDIM // 2
N_KEYS = 128
N_EXPERTS = N_KEYS * N_KEYS
PEER_TOPK = 16
PEER_BLOCK = 128
DEEPNORM_ALPHA = (2.0 * DEPTH) ** 0.25
DEEPNORM_BETA = (8.0 * DEPTH) ** -0.25
LN_EPS = 1e-5
NEG_INF = -1e30

kernel_name = "hybrid_gated_window_natten_peer_encoder"


def layer_norm(x, g, b):
    xf = x.astype(jnp.float32)
    mu = jnp.mean(xf, axis=-1, keepdims=True)
    xc = xf - mu
    var = jnp.mean(xc * xc, axis=-1, keepdims=True)
    y = xc * lax.rsqrt(var + LN_EPS) * g.astype(jnp.float32) + b.astype(jnp.float32)
    return y.astype(x.dtype)


def rope_partial(x, pos):
    half = ROPE_DIM // 2
    inv_freq = ROPE_THETA ** (-jnp.arange(half, dtype=jnp.float32) * 2.0 / ROPE_DIM)
    ang = pos.astype(jnp.float32)[:, None] * inv_freq[None, :]
    cos = jnp.cos(ang)[None, :, None, :]
    sin = jnp.sin(ang)[None, :, None, :]
    xf = x.astype(jnp.float32)
    x1 = xf[..., :half]
    x2 = xf[..., half:ROPE_DIM]
    out = jnp.concatenate([x1 * cos - x2 * sin, x2 * cos + x1 * sin, xf[..., ROPE_DIM:]], axis=-1)
    return out.astype(x.dtype)


def windowed_gqa(q, k, v, sink):
    Bsz, T = q.shape[0], q.shape[1]
    nb = T // BLOCK
    scale = HEAD_DIM ** -0.5
    qb = q.reshape(Bsz, nb, BLOCK, A_KV_HEADS, A_GROUP, HEAD_DIM)
    pad = ((0, 0), (BLOCK, BLOCK), (0, 0), (0, 0))
    kp = jnp.pad(k, pad).reshape(Bsz, nb + 2, BLOCK, A_KV_HEADS, HEAD_DIM)
    vp = jnp.pad(v, pad).reshape(Bsz, nb + 2, BLOCK, A_KV_HEADS, HEAD_DIM)
    kw = jnp.concatenate([kp[:, :-2], kp[:, 1:-1], kp[:, 2:]], axis=2)
    vw = jnp.concatenate([vp[:, :-2], vp[:, 1:-1], vp[:, 2:]], axis=2)
    s = jnp.einsum('bnqhgd,bnkhd->bnhgqk', qb, kw,
                   preferred_element_type=jnp.float32) * scale
    qi = jnp.arange(BLOCK)[:, None]
    kj = jnp.arange(3 * BLOCK)[None, :]
    in_band = jnp.abs(kj - BLOCK - qi) <= WINDOW
    kpos = (jnp.arange(nb)[:, None] - 1) * BLOCK + jnp.arange(3 * BLOCK)[None, :]
    in_seq = (kpos >= 0) & (kpos < T)
    mask = in_band[None] & in_seq[:, None, :]
    s = jnp.where(mask[None, :, None, None], s, NEG_INF)
    sink_l = sink.astype(jnp.float32).reshape(A_KV_HEADS, A_GROUP)[None, None, :, :, None]
    lse = jnp.logaddexp(jax.nn.logsumexp(s, axis=-1), sink_l)
    p = jnp.exp(s - lse[..., None])
    o = jnp.einsum('bnhgqk,bnkhd->bnqhgd', p.astype(v.dtype), vw)
    return o.reshape(Bsz, T, A_Q_W)


def neighbourhood_attn(q, k, v, rpb):
    Bsz, T = q.shape[0], q.shape[1]
    rows = T // GRID_W
    wr = min(NA_ROWS, rows)
    scale = HEAD_DIM ** -0.5
    r = jnp.arange(rows)
    rs = jnp.clip(r - wr // 2, 0, rows - wr)
    key_rows = rs[:, None] + jnp.arange(wr)[None, :]
    c = jnp.arange(GRID_W)
    cs = jnp.clip(c - NA_COLS // 2, 0, GRID_W - NA_COLS)
    col_ok = (c[None, :] >= cs[:, None]) & (c[None, :] < cs[:, None] + NA_COLS)
    dr = key_rows - r[:, None]
    dc = jnp.clip(c[None, :] - c[:, None], -(NA_COLS - 1), NA_COLS - 1)
    bias = rpb[:, (dr + NA_ROWS - 1)[:, None, :, None], (dc + NA_COLS - 1)[None, :, None, :]]
    bias = jnp.where(col_ok[None, None, :, None, :], bias.astype(jnp.float32), NEG_INF)
    bias = bias.reshape(B_HEADS, rows, GRID_W, wr * GRID_W).transpose(1, 0, 2, 3)
    qg = q.reshape(Bsz, rows, GRID_W, B_HEADS, HEAD_DIM)
    kg = k.reshape(Bsz, rows, GRID_W, B_HEADS, HEAD_DIM)[:, key_rows].reshape(
        Bsz, rows, wr * GRID_W, B_HEADS, HEAD_DIM)
    vg = v.reshape(Bsz, rows, GRID_W, B_HEADS, HEAD_DIM)[:, key_rows].reshape(
        Bsz, rows, wr * GRID_W, B_HEADS, HEAD_DIM)
    s = jnp.einsum('brqhd,brkhd->brhqk', qg, kg,
                   preferred_element_type=jnp.float32) * scale + bias[None]
    p = jax.nn.softmax(s, axis=-1)
    o = jnp.einsum('brhqk,brkhd->brqhd', p.astype(v.dtype), vg)
    return o.reshape(Bsz, T, B_W)


def memory_attn(q, mem, w_mem_kv):
    Bsz, T = q.shape[0], q.shape[1]
    kv = jnp.einsum('bmd,de->bme', mem, w_mem_kv)
    km = kv[..., :M_W].reshape(Bsz, -1, M_HEADS, HEAD_DIM)
    vm = kv[..., M_W:].reshape(Bsz, -1, M_HEADS, HEAD_DIM)
    s = jnp.einsum('bthd,bmhd->bhtm', q, km, preferred_element_type=jnp.float32) * HEAD_DIM ** -0.5
    p = jax.nn.softmax(s, axis=-1)
    o = jnp.einsum('bhtm,bmhd->bthd', p.astype(vm.dtype), vm)
    return o.reshape(Bsz, T, M_W)


def peer_ffn(x, w_peer_q, peer_keys1, peer_keys2, peer_u, peer_v):
    Bsz, T, D = x.shape
    q = jnp.einsum('btd,de->bte', x, w_peer_q).reshape(Bsz, T, PEER_HEADS, 2, PEER_HALF)
    s1 = jnp.einsum('bthc,nc->bthn', q[..., 0, :], peer_keys1, preferred_element_type=jnp.float32)
    s2 = jnp.einsum('bthc,nc->bthn', q[..., 1, :], peer_keys2, preferred_element_type=jnp.float32)
    t1, i1 = lax.top_k(s1, PEER_TOPK)
    t2, i2 = lax.top_k(s2, PEER_TOPK)
    cand = (t1[..., :, None] + t2[..., None, :]).reshape(Bsz, T, PEER_HEADS, PEER_TOPK * PEER_TOPK)
    cand_idx = (i1[..., :, None] * N_KEYS + i2[..., None, :]).reshape(
        Bsz, T, PEER_HEADS, PEER_TOPK * PEER_TOPK)
    top, pos = lax.top_k(cand, PEER_TOPK)
    idx = jnp.take_along_axis(cand_idx, pos, axis=-1)
    g = jax.nn.softmax(top, axis=-1)
    n_blk = (Bsz * T) // PEER_BLOCK
    xs = x.reshape(n_blk, PEER_BLOCK, D)
    idxs = idx.reshape(n_blk, PEER_BLOCK, PEER_HEADS, PEER_TOPK)
    gs = g.reshape(n_blk, PEER_BLOCK, PEER_HEADS, PEER_TOPK)

    def expert_block(args):
        xb, ib, gb = args
        ub = peer_u[ib]
        a = jnp.einsum('td,thkd->thk', xb, ub, preferred_element_type=jnp.float32)
        w = gb * jax.nn.gelu(a, approximate=False)
        vb = peer_v[ib]
        return jnp.einsum('thk,thkd->td', w.astype(vb.dtype), vb)

    out = lax.map(expert_block, (xs, idxs, gs))
    return out.reshape(Bsz, T, D)


def encoder_layer(x, mem, w_in, b_gate, a_sink, na_rpb, w_mem_kv, w_proj_a, w_proj_b, w_proj_m,
                  w_out, ln1_g, ln1_b, w_peer_q, peer_keys1, peer_keys2, peer_u, peer_v,
                  ln2_g, ln2_b):
    Bsz, T, _ = x.shape
    proj = jnp.einsum('btd,de->bte', x, w_in)
    pieces = []
    off = 0
    for width in IN_SIZES:
        pieces.append(proj[..., off:off + width])
        off += width
    q_a, k_a, v_a, q_b, k_b, v_b, q_m, gate_logits = pieces
    pos = jnp.arange(T)
    q_a = rope_partial(q_a.reshape(Bsz, T, A_HEADS, HEAD_DIM), pos)
    k_a = rope_partial(k_a.reshape(Bsz, T, A_KV_HEADS, HEAD_DIM), pos)
    o_a = windowed_gqa(q_a, k_a, v_a.reshape(Bsz, T, A_KV_HEADS, HEAD_DIM), a_sink)
    o_b = neighbourhood_attn(q_b.reshape(Bsz, T, B_HEADS, HEAD_DIM),
                             k_b.reshape(Bsz, T, B_HEADS, HEAD_DIM),
                             v_b.reshape(Bsz, T, B_HEADS, HEAD_DIM), na_rpb)
    o_m = memory_attn(q_m.reshape(Bsz, T, M_HEADS, HEAD_DIM), mem, w_mem_kv)
    gates = jax.nn.sigmoid(gate_logits.astype(jnp.float32) + b_gate.astype(jnp.float32))
    gates = gates.reshape(Bsz, T, N_BRANCH, D_MODEL)
    merged = (gates[:, :, 0] * (o_a @ w_proj_a)
              + gates[:, :, 1] * (o_b @ w_proj_b)
              + gates[:, :, 2] * (o_m @ w_proj_m))
    mixed = merged.astype(x.dtype) @ w_out
    x = layer_norm(DEEPNORM_ALPHA * x + mixed, ln1_g, ln1_b)
    ff = peer_ffn(x, w_peer_q, peer_keys1, peer_keys2, peer_u, peer_v)
    x = layer_norm(DEEPNORM_ALPHA * x + ff.astype(x.dtype), ln2_g, ln2_b)
    return x


def setup_inputs(seed: int = 0) -> dict:
    key = jax.random.key(seed)
    ks = jax.random.split(key, 24)

    def nrm(k, shape, scale):
        return jax.random.normal(k, shape, dtype=jnp.float32) * scale

    L = DEPTH
    col_scale = jnp.concatenate([
        jnp.ones((A_Q_W + A_KV_W,), jnp.float32),
        jnp.full((A_KV_W,), DEEPNORM_BETA, jnp.float32),
        jnp.ones((2 * B_W,), jnp.float32),
        jnp.full((B_W,), DEEPNORM_BETA, jnp.float32),
        jnp.ones((M_W + N_BRANCH * D_MODEL,), jnp.float32)])
    mem_scale = jnp.concatenate([jnp.ones((M_W,), jnp.float32),
                                 jnp.full((M_W,), DEEPNORM_BETA, jnp.float32)])
    return {
        "x_prompt": nrm(ks[0], (BATCH, SEQ, D_MODEL), 1.0),
        "x_sample": nrm(ks[1], (DEC_BATCH, DEC_SEQ, D_MODEL), 1.0),
        "mem_prompt": nrm(ks[2], (BATCH, N_MEM, D_MODEL), 1.0),
        "mem_sample": nrm(ks[3], (DEC_BATCH, N_MEM, D_MODEL), 1.0),
        "w_in": nrm(ks[4], (L, D_MODEL, IN_W), D_MODEL ** -0.5) * col_scale,
        "b_gate": nrm(ks[5], (L, N_BRANCH * D_MODEL), 0.1),
        "a_sink": nrm(ks[6], (L, A_HEADS), 1.0),
        "na_rpb": nrm(ks[7], (L, B_HEADS, 2 * NA_ROWS - 1, 2 * NA_COLS - 1), 0.5),
        "w_mem_kv": nrm(ks[8], (L, D_MODEL, 2 * M_W), D_MODEL ** -0.5) * mem_scale,
        "w_proj_a": nrm(ks[9], (L, A_Q_W, D_MODEL), DEEPNORM_BETA * A_Q_W ** -0.5),
        "w_proj_b": nrm(ks[10], (L, B_W, D_MODEL), DEEPNORM_BETA * B_W ** -0.5),
        "w_proj_m": nrm(ks[11], (L, M_W, D_MODEL), DEEPNORM_BETA * M_W ** -0.5),
        "w_out": nrm(ks[12], (L, D_MODEL, D_MODEL), DEEPNORM_BETA * D_MODEL ** -0.5),
        "ln1_g": 1.0 + nrm(ks[13], (L, D_MODEL), 0.02),
        "ln1_b": nrm(ks[14], (L, D_MODEL), 0.02),
        "w_peer_q": nrm(ks[15], (L, D_MODEL, PEER_HEADS * PEER_QDIM), D_MODEL ** -0.5),
        "peer_keys1": nrm(ks[16], (L, N_KEYS, PEER_HALF), PEER_HALF ** -0.5),
        "peer_keys2": nrm(ks[17], (L, N_KEYS, PEER_HALF), PEER_HALF ** -0.5),
        "peer_u": nrm(ks[18], (L, N_EXPERTS, D_MODEL), D_MODEL ** -0.5),
        "peer_v": nrm(ks[19], (L, N_EXPERTS, D_MODEL), DEEPNORM_BETA * PEER_HEADS ** -0.5),
        "ln2_g": 1.0 + nrm(ks[20], (L, D_MODEL), 0.02),
        "ln2_b": nrm(ks[21], (L, D_MODEL), 0.02),
    }


def reference(x_prompt, x_sample, mem_prompt, mem_sample, w_in, b_gate, a_sink, na_rpb, w_mem_kv,
              w_proj_a, w_proj_b, w_proj_m, w_out, ln1_g, ln1_b, w_peer_q, peer_keys1, peer_keys2,
              peer_u, peer_v, ln2_g, ln2_b):
    y_prompt = x_prompt
    y_sample = x_sample
    for l in range(DEPTH):
        layer_params = (w_in[l], b_gate[l], a_sink[l], na_rpb[l], w_mem_kv[l], w_proj_a[l],
                        w_proj_b[l], w_proj_m[l], w_out[l], ln1_g[l], ln1_b[l], w_peer_q[l],
                        peer_keys1[l], peer_keys2[l], peer_u[l], peer_v[l], ln2_g[l], ln2_b[l])
        y_prompt = encoder_layer(y_prompt, mem_prompt, *layer_params)
        y_sample = encoder_layer(y_sample, mem_sample, *layer_params)
    return (y_prompt, y_sample)
```

```python
import math
from contextlib import ExitStack

import numpy as np
import concourse.bass as bass
import concourse.mybir as mybir
from concourse.bass_utils import run_bass_kernel_spmd

F32 = mybir.dt.float32
U32 = mybir.dt.uint32
ALU = mybir.AluOpType
ACT = mybir.ActivationFunctionType
AX = mybir.AxisListType

D = 2048
KC = 16
HD = 128
IN_W = 9728
N_EXP = 16384
ALPHA = 2.0 ** 0.25
LN_EPS = 1e-5
SCALE = HD ** -0.5
NEG = -30000.0
ROPE_THETA = 500000.0

NW = 5
NG = 6
KVR = 6
LOOK = 3
STOP = 0
WQ = "pool"
P2_STEPS = 2


class Tile:
    def __init__(self, name, ap=None):
        self.name = name
        self.ap = ap
        self.writer = None
        self.readers = []
        self.chan = None
        self.excl = False


class Op:
    __slots__ = ("eng", "fn", "deps", "dma", "chan", "needed", "val", "ninst")

    def __init__(self, eng, fn, dma=False, chan=None):
        self.eng = eng
        self.fn = fn
        self.deps = []
        self.dma = dma
        self.chan = chan
        self.needed = False
        self.val = None
        self.ninst = 1


class Sched:
    ENGS = ("pe", "act", "dve", "pool", "sp")

    def __init__(self):
        self.prog = {e: [] for e in self.ENGS}
        self.all_ops = []
        self.nchan = 0
        self.store_ops = []

    def new_chan(self):
        self.nchan += 1
        return self.nchan - 1

    def op(self, eng, fn, reads=(), writes=(), dma=False, chan=None, ninst=1):
        o = Op(eng, fn, dma, chan)
        o.ninst = ninst
        deps = set()
        for t in reads:
            if t.writer is not None:
                deps.add(t.writer)
            if t.excl:
                for r in t.readers:
                    if r.eng != eng:
                        deps.add(r)
        for t in writes:
            if t.writer is not None:
                deps.add(t.writer)
            for r in t.readers:
                deps.add(r)
        deps.discard(o)
        o.deps = list(deps)
        for t in reads:
            t.readers.append(o)
        for t in writes:
            t.writer = o
            t.readers = []
        self.prog[eng].append(o)
        self.all_ops.append(o)
        return o

    def emit(self, nc, es):
        for e in self.ENGS:
            for o in self.prog[e]:
                for d in o.deps:
                    if d.eng == "pe" and o.eng == "pe" and not d.dma:
                        continue
                    d.needed = True
        esem = {e: es.enter_context(nc.semaphore("sem_" + e)) for e in ("pe", "act", "dve", "pool")}
        csem = [es.enter_context(nc.semaphore("ch%d" % i)) for i in range(self.nchan)]
        cnt = {e: 0 for e in esem}
        ccnt = [0] * self.nchan
        for o in self.all_ops:
            if o.dma:
                ccnt[o.chan] += 16 * o.ninst
                o.val = (csem[o.chan], ccnt[o.chan])
        for e in ("pe", "act", "dve", "pool"):
            for o in self.prog[e]:
                if not o.dma and o.needed:
                    cnt[e] += 1
                    o.val = (esem[e], cnt[e])
        final = [(csem[o.chan], o.val[1]) for o in self.store_ops]
        engobj = {"pe": "tensor", "act": "scalar", "dve": "vector", "pool": "gpsimd", "sp": "sync"}

        with nc.Block() as block:
            def body(ename):
                def run(eng):
                    waited = {}
                    for o in self.prog[ename]:
                        need = {}
                        for d in o.deps:
                            if d.eng == "pe" and ename == "pe" and not d.dma:
                                continue
                            s, v = d.val
                            k = id(s)
                            if k not in need or need[k][1] < v:
                                need[k] = (s, v)
                        for k, (s, v) in need.items():
                            if waited.get(k, 0) >= v:
                                continue
                            eng.wait_ge(s, v)
                            waited[k] = v
                        insts = o.fn(eng)
                        if o.dma:
                            if not isinstance(insts, (list, tuple)):
                                insts = [insts]
                            assert len(insts) == o.ninst
                            for ins in insts:
                                ins.then_inc(o.val[0], 16)
                        elif o.needed:
                            insts.then_inc(o.val[0], 1)
                    if ename == "sp":
                        for s, v in final:
                            eng.wait_ge(s, v)
                return run

            block.tensor(body("pe"))
            block.scalar(body("act"))
            block.vector(body("dve"))
            block.gpsimd(body("pool"))
            block.sync(body("sp"))


def sb_ap(t, offset, dims):
    full = t.ap
    return bass.AP(t.tensor, t.offset + offset, [[full[0][0], full[0][1]]] + [list(d) for d in dims])


def build_program(nbs):
    nc = bass.Bass("TRN2", target_bir_lowering=False)
    S = Sched()
    es = ExitStack()
    nseg = len(nbs)

    def din(name, shape, dt=F32):
        return nc.dram_tensor(name, list(shape), dt, kind="ExternalInput").ap()

    d_xT = [din("xT%d" % s, [nbs[s] + 4, 128, D]) for s in range(nseg)]
    d_xtok = [din("xtok%d" % s, [nbs[s], 128, D]) for s in range(nseg)]
    d_rope = [din("rope%d" % s, [nbs[s] + 4, 128, 256]) for s in range(nseg)]
    d_memT = din("memT", [nseg, 128, KC * 256])
    d_btab = din("btab", [nseg * 5, 128, 4 * 7 * 128])
    d_amask = din("amask", [128, nseg * 3 * 384])
    d_win = din("win_t", [76, 128, D])
    d_wpa = din("wpa_t", [16, 128, 1024])
    d_wpb = din("wpb_t", [16, 128, 512])
    d_wpm = din("wpm_t", [16, 128, 512])
    d_wout = din("wout_t", [16, 128, D])
    d_wpq = din("wpq_t", [16, 128, D])
    d_wmkv = din("wmkv_t", [8, 128, D])
    d_bg = din("bgate", [128, 48])
    d_sink = din("sink", [1, 8])
    d_ln = din("lnp", [4, D])
    d_ident = din("ident", [128, 128])
    d_identS = din("identS", [128, 128])
    d_iota = din("iota16", [128, 16])
    d_keysT = din("keysT", [128, 256])
    d_pu = din("peer_u", [N_EXP if STOP < 10 else 128, D])
    d_pv = din("peer_v", [N_EXP if STOP < 10 else 128, D])
    d_y = [nc.dram_tensor("y%d" % s, [nbs[s], 128, D], F32, kind="ExternalOutput").ap()
           for s in range(nseg)]

    def sb(name, shape, dt=F32, dma=False):
        t = es.enter_context(nc.sbuf_tensor("sb_" + name, list(shape), dt))
        tl = Tile(name, t[:])
        if dma:
            tl.chan = S.new_chan()
        return tl

    B1 = sb("B1", [128, D], dma=True)
    B2 = sb("B2", [128, D], dma=True)
    B3 = sb("B3", [128, D], dma=True)
    B4 = sb("B4", [128, D], dma=True)
    xTkv, xTq, QT, oT, mT = B1, B2, B3, B3, B1
    gbuf = [sb("gbuf%d" % i, [128, D], dma=True) for i in range(NG)]
    gchan = {id(t): S.new_chan() for t in gbuf}
    xtok = [sb("xtok%d" % i, [128, D], dma=True) for i in range(2)]
    xtok_st = [S.new_chan() for _ in range(2)]
    wring = [sb("w%d" % i, [128, D], dma=True) for i in range(NW)]
    kTa = [sb("kTa%d" % i, [128, 2 * 128]) for i in range(KVR)]
    va = [sb("va%d" % i, [128, 2 * 129]) for i in range(KVR)]
    kTb = [sb("kTb%d" % i, [128, 4 * 128]) for i in range(KVR)]
    vb = [sb("vb%d" % i, [128, 4 * 129]) for i in range(KVR)]
    kTm = sb("kTm", [128, 4 * 256])
    vm = sb("vm", [128, 2 * 4 * 129])
    identS = sb("identS", [128, 128], dma=True)
    dens = sb("dens", [128, 32])
    amask = sb("amask", [128, 3 * 384], dma=True)
    ropekv = sb("ropekv", [128, 256], dma=True)
    ropeq = sb("ropeq", [128, 256], dma=True)
    ident = sb("ident", [128, 128], dma=True)
    iota16 = sb("iota16", [128, 16], dma=True)
    keysT = sb("keysT", [128, 256], dma=True)
    bgate = sb("bgate", [128, 48], dma=True)
    sink = sb("sink", [128, 8], dma=True)
    sinke = sb("sinke", [128, 8])
    tok_tmp = B4
    o_tok = B1
    p_sb = [sb("p_sb%d" % i, [128, 512]) for i in range(2)]
    rtmp = p_sb[0]
    g_sb = sb("g_sb", [128, 384])
    gm_sb = sb("gm_sb", [128, 384])
    small = sb("small", [128, 64])
    stats = sb("stats", [128, 32])
    stats2 = sb("stats2", [128, 32])
    tv = sb("tv", [128, 256])
    ti = sb("ti", [128, 256], U32)
    tif = sb("tif", [128, 256])
    s2 = gm_sb
    topv = sb("topv", [128, 128])
    topi = sb("topi", [128, 128], U32)
    ai = sb("ai", [128, 128], U32)
    af = sb("af", [128, 128])
    bf = sb("bf", [128, 128])
    c4 = sb("c4", [128, 2], U32)
    c15 = sb("c15", [128, 2], U32)
    isel1 = sb("isel1", [128, 128])
    isel2 = sb("isel2", [128, 128])
    idxf = sb("idxf", [128, 128])
    idxu2 = [sb("idxu%d" % i, [128, 128], U32) for i in range(2)]
    gts2 = [sb("gts%d" % i, [128, 128]) for i in range(2)]
    a_sb = sb("a_sb", [128, 128])
    w_sb = sb("w_sb", [128, 128])
    gsum = sb("gsum", [128, 16])

    psum = []
    for i in range(8):
        t = es.enter_context(nc.psum_tensor("ps%d" % i, [128, 512], F32))
        psum.append(Tile("ps%d" % i, t[:]))
        psum[-1].excl = True
    ps_mm = psum[0:4]
    ps_s = psum[4:6]
    ps_o = psum[6:8]
    rr = {"mm": 0, "s": 0, "o": 0, "p": 0, "w": 0, "bt": 0}

    def nxt(key, lst):
        v = lst[rr[key] % len(lst)]
        rr[key] += 1
        return v

    def dma_load(dst, dst_ap, src_ap, reads=()):
        return S.op("sp", lambda e, a=dst_ap, b=src_ap: e.dma_start(out=a, in_=b),
                    reads=reads, writes=[dst], dma=True, chan=dst.chan)

    def bcast_rows(dram, row, ncols):
        return bass.AP(dram.tensor, dram.offset + row * ncols, [[0, 128], [1, ncols]])

    def mm(ps, out_ap, pairs, reads):
        n = len(pairs)
        last = None
        for k, (l, r) in enumerate(pairs):
            last = S.op("pe", lambda e, o=out_ap, l=l, r=r, st=(k == 0), sp=(k == n - 1):
                        e.matmul(o, l, r, start=st, stop=sp),
                        reads=reads, writes=[ps])
        return last

    def transpose(ps, out_ap, in_ap, reads):
        return S.op("pe", lambda e, o=out_ap, i=in_ap: e.transpose(o, i, ident.ap),
                    reads=list(reads) + [ident], writes=[ps])

    def act(out_t, out_ap, in_t, in_ap, func, extra_reads=(), **kw):
        return S.op("act", lambda e, o=out_ap, i=in_ap: e.activation(o, i, func, **kw),
                    reads=[in_t] + list(extra_reads), writes=[out_t])

    def dve(fn, reads, writes):
        return S.op("dve", fn, reads=reads, writes=writes)

    wseq = []
    wstate = {"issued": 0, "used": 0}

    def w_issue_upto(k):
        while wstate["issued"] < min(k, len(wseq)):
            i = wstate["issued"]
            slot = wring[i % NW]
            src = wseq[i]
            n = src.shape[-1]
            S.op(WQ, lambda e, a=slot.ap[:, 0:n], b=src: e.dma_start(out=a, in_=b),
                 reads=[], writes=[slot], dma=True, chan=slot.chan)
            wstate["issued"] += 1

    def w_next(expect):
        i = wstate["used"]
        if STOP:
            wseq.append(expect)
        assert wseq[i].offset == expect.offset and wseq[i].tensor.name == expect.tensor.name, (i,)
        w_issue_upto(i + NW)
        wstate["used"] += 1
        return wring[i % NW]

    def seq_kv():
        return [d_win[t] for t in (8, 9, 10, 11, 20, 21, 22, 23, 16, 17, 18, 19)]

    def seq_main():
        l = [d_win[t] for t in list(range(0, 8)) + [12, 13, 14, 15, 24, 25, 26, 27]]
        for ec in range(16):
            l += [d_win[28 + b * 16 + ec] for b in range(3)]
            l += [d_wpa[ec], d_wpb[ec], d_wpm[ec]]
        l += [d_wout[ct] for ct in range(16)]
        l += [d_wpq[ct] for ct in range(16)]
        return l

    for s in range(nseg):
        wseq += [d_wmkv[t] for t in range(8)]
        for j in range(min(LOOK + 2, nbs[s] + 4)):
            wseq += seq_kv()
        for i in range(nbs[s]):
            if i + 2 + LOOK < nbs[s] + 4:
                wseq += seq_kv()
            wseq += seq_main()

    if STOP:
        wseq.clear()
    dma_load(ident, ident.ap, d_ident)
    dma_load(identS, identS.ap, d_identS)
    dma_load(iota16, iota16.ap, d_iota)
    dma_load(keysT, keysT.ap, d_keysT)
    dma_load(bgate, bgate.ap, d_bg)
    dma_load(sink, sink.ap, bcast_rows(d_sink, 0, 8))
    act(sinke, sinke.ap, sink, sink.ap, ACT.Exp)
    dve(lambda e: e.memset(c4.ap, 4), [], [c4])
    dve(lambda e: e.memset(c15.ap, 15), [], [c15])
    for r in range(KVR):
        dve(lambda e, t=va[r]: e.memset(t.ap, 1.0), [], [va[r]])
        dve(lambda e, t=vb[r]: e.memset(t.ap, 1.0), [], [vb[r]])
    dve(lambda e: e.memset(vm.ap, 1.0), [], [vm])

    def rope(nh, rtab):
        x1 = sb_ap(tok_tmp.ap, 0, [[128, nh], [1, 16]])
        x2 = sb_ap(tok_tmp.ap, 16, [[128, nh], [1, 16]])
        cs = sb_ap(rtab.ap, 0, [[16, nh], [1, 16]])
        sn = sb_ap(rtab.ap, 128, [[16, nh], [1, 16]])
        t = [sb_ap(rtmp.ap, k * 128, [[16, nh], [1, 16]]) for k in range(4)]
        dve(lambda e: e.tensor_tensor(t[0], x1, cs, ALU.mult), [tok_tmp, rtab], [rtmp])
        dve(lambda e: e.tensor_tensor(t[1], x2, sn, ALU.mult), [tok_tmp, rtab], [rtmp])
        dve(lambda e: e.tensor_tensor(t[2], x2, cs, ALU.mult), [tok_tmp, rtab], [rtmp])
        dve(lambda e: e.tensor_tensor(t[3], x1, sn, ALU.mult), [tok_tmp, rtab], [rtmp])
        dve(lambda e: e.tensor_tensor(x1, t[0], t[1], ALU.subtract), [rtmp], [tok_tmp])
        dve(lambda e: e.tensor_tensor(x2, t[2], t[3], ALU.add), [rtmp], [tok_tmp])

    def tokmajor(xT_t, wtiles_dram, ps):
        for q, wd in enumerate(wtiles_dram):
            wt = w_next(wd)
            pairs = [(xT_t.ap[:, kc * 128:(kc + 1) * 128], wt.ap[:, kc * 128:(kc + 1) * 128])
                     for kc in range(KC)]
            mm(ps, ps.ap[:, q * 128:(q + 1) * 128], pairs, [xT_t, wt])

    def featmajor(xT_t, wtiles_dram, ps, kcs=KC, xoff=0):
        for q, wd in enumerate(wtiles_dram):
            wt = w_next(wd)
            pairs = [(wt.ap[:, kc * 128:(kc + 1) * 128],
                      xT_t.ap[:, (xoff + kc) * 128:(xoff + kc + 1) * 128]) for kc in range(kcs)]
            mm(ps, ps.ap[:, q * 128:(q + 1) * 128], pairs, [xT_t, wt])

    def kv_stage(s, j):
        if STOP in (10, 20):
            return
        slot = j % KVR
        dma_load(xTkv, xTkv.ap, d_xT[s][j])
        dma_load(ropekv, ropekv.ap, d_rope[s][j])
        ps = nxt("mm", ps_mm)
        tokmajor(xTkv, [d_win[8], d_win[9], d_win[10], d_win[11]], ps)
        act(tok_tmp, tok_tmp.ap[:, 0:256], ps, ps.ap[:, 0:256], ACT.Copy)
        dve(lambda e, o=sb_ap(va[slot].ap, 0, [[129, 2], [1, 128]]),
            i=sb_ap(ps.ap, 256, [[128, 2], [1, 128]]): e.tensor_copy(o, i), [ps], [va[slot]])
        if STOP == 21:
            return
        rope(2, ropekv)
        ps2 = nxt("mm", ps_mm)
        for h in range(2):
            transpose(ps2, ps2.ap[:, h * 128:(h + 1) * 128], tok_tmp.ap[:, h * 128:(h + 1) * 128],
                      [tok_tmp])
        act(kTa[slot], kTa[slot].ap, ps2, ps2.ap[:, 0:256], ACT.Copy)
        if STOP == 22:
            return
        ps = nxt("mm", ps_mm)
        tokmajor(xTkv, [d_win[20 + q] for q in range(4)], ps)
        dve(lambda e, o=sb_ap(vb[slot].ap, 0, [[129, 4], [1, 128]]),
            i=sb_ap(ps.ap, 0, [[128, 4], [1, 128]]): e.tensor_copy(o, i), [ps], [vb[slot]])
        if STOP == 23:
            return
        ps = nxt("mm", ps_mm)
        featmajor(xTkv, [d_win[16 + q] for q in range(4)], ps)
        act(kTb[slot], kTb[slot].ap, ps, ps.ap, ACT.Copy)

    def attn_all(heads):
        chunks = []
        for hd in heads:
            nk = len(hd["keys"])
            done = 0
            first = True
            while done < nk:
                n = min(3, nk - done)
                chunks.append((hd, done, n, first, done + n == nk))
                first = False
                done += n
        state = {}

        def emit_S(ck):
            hd, j0, n, first, last = ck
            if first and hd.get("pre") is not None:
                hd["pre"]()
            pss = nxt("s", ps_s)
            for jj in range(n):
                kT_ap, kt_t, _, _ = hd["keys"][j0 + jj]
                o = pss.ap[:, jj * 128:(jj + 1) * 128]
                if hd["mask"] is None:
                    mm(pss, o, [(kT_ap, hd["q_ap"])], [kt_t, QT])
                else:
                    m_ap, m_t = hd["mask"](j0 + jj)
                    S.op("pe", lambda e, o=o, l=kT_ap, r=hd["q_ap"]: e.matmul(o, l, r, start=True, stop=False),
                         reads=[kt_t, QT], writes=[pss])
                    S.op("pe", lambda e, o=o, r=m_ap: e.matmul(o, identS.ap, r, start=False, stop=True),
                         reads=[identS, m_t], writes=[pss])
            p = nxt("p", p_sb)
            act(p, p.ap[:, 0:n * 128], pss, pss.ap[:, 0:n * 128], ACT.Exp, scale=SCALE)
            state[id(ck)] = p

        def emit_PV(ck):
            hd, j0, n, first, last = ck
            p = state.pop(id(ck))
            if first:
                hd["po"] = nxt("o", ps_o)
            po = hd["po"]
            for jj in range(n):
                _, _, v_ap, v_t = hd["keys"][j0 + jj]
                S.op("pe", lambda e, o=po.ap[:, 0:129], l=p.ap[:, jj * 128:(jj + 1) * 128], r=v_ap,
                     st=(first and jj == 0), sp=(last and jj == n - 1): e.matmul(o, l, r, start=st, stop=sp),
                     reads=[p, v_t], writes=[po])
            if last:
                c = hd["col"]
                act(o_tok, o_tok.ap[:, c * 128:(c + 1) * 128], po, po.ap[:, 0:128], ACT.Copy)
                act(dens, dens.ap[:, c:c + 1], po, po.ap[:, 128:129], ACT.Copy)

        emit_S(chunks[0])
        for k, ck in enumerate(chunks):
            if k + 1 < len(chunks):
                emit_S(chunks[k + 1])
            emit_PV(ck)
            if ck[4]:
                yield

    def attn_finish():
        dve(lambda e: e.tensor_tensor(dens.ap[:, 0:8], dens.ap[:, 0:8], sinke.ap, ALU.add), [dens, sinke], [dens])
        dve(lambda e: e.reciprocal(dens.ap[:, 16:32], dens.ap[:, 0:16]), [dens], [dens])
        dve(lambda e: e.tensor_tensor(sb_ap(o_tok.ap, 0, [[128, 16], [1, 128]]),
                                      sb_ap(o_tok.ap, 0, [[128, 16], [1, 128]]),
                                      sb_ap(dens.ap, 16, [[1, 16], [0, 128]]), ALU.mult), [o_tok, dens], [o_tok])

    def layernorm(xt, grow, brow, lnA, lnB, stats):
        dma_load(lnA, lnA.ap, bcast_rows(d_ln, grow, D))
        dma_load(lnB, lnB.ap, bcast_rows(d_ln, brow, D))
        for q in range(4):
            dve(lambda e, q=q: e.bn_stats(stats.ap[:, q * 6:(q + 1) * 6], xt.ap[:, q * 512:(q + 1) * 512]),
                [xt], [stats])
        dve(lambda e: e.bn_aggr(stats.ap[:, 24:26], stats.ap[:, 0:24]), [stats], [stats])
        dve(lambda e: e.tensor_scalar(stats.ap[:, 26:27], stats.ap[:, 25:26], LN_EPS, None, ALU.add),
            [stats], [stats])
        act(stats, stats.ap[:, 27:28], stats, stats.ap[:, 26:27], ACT.Sqrt)
        dve(lambda e: e.reciprocal(stats.ap[:, 28:29], stats.ap[:, 27:28]), [stats], [stats])
        dve(lambda e: e.tensor_scalar(xt.ap, xt.ap, stats.ap[:, 24:25], stats.ap[:, 28:29],
                                      ALU.subtract, ALU.mult), [xt, stats], [xt])
        dve(lambda e: e.tensor_tensor(xt.ap, xt.ap, lnA.ap, ALU.mult), [xt, lnA], [xt])
        dve(lambda e: e.tensor_tensor(xt.ap, xt.ap, lnB.ap, ALU.add), [xt, lnB], [xt])

    def top16(src_t, src_ap, n, scratch_ap, outv_t, outv_ap, outi_t, outi_ap):
        dve(lambda e: e.max(outv_ap[:, 0:8], src_ap), [src_t], [outv_t])
        dve(lambda e: e.max_index(outi_ap[:, 0:8], outv_ap[:, 0:8], src_ap), [src_t, outv_t], [outi_t])
        dve(lambda e: e.match_replace(scratch_ap, outv_ap[:, 0:8], src_ap, -3.0e38),
            [src_t, outv_t], [s2])
        dve(lambda e: e.max(outv_ap[:, 8:16], scratch_ap), [s2], [outv_t])
        dve(lambda e: e.max_index(outi_ap[:, 8:16], outv_ap[:, 8:16], scratch_ap), [s2, outv_t], [outi_t])

    def seg_prologue(s):
        nb = nbs[s]
        halves = (xTkv, xTq)
        dma_load(amask, amask.ap, d_amask[:, s * 1152:(s + 1) * 1152])
        for mb in range(2):
            src = bass.AP(d_memT.tensor, d_memT.offset + s * 128 * KC * 256 + mb * 128,
                          [[KC * 256, 128], [256, KC], [1, 128]])
            dma_load(halves[mb], sb_ap(halves[mb].ap, 0, [[128, KC], [1, 128]]), src)
        for t in range(8):
            wt = w_next(d_wmkv[t])
            if t < 4:
                ps = nxt("mm", ps_mm)
                for mb in range(2):
                    pairs = [(wt.ap[:, kc * 128:(kc + 1) * 128], halves[mb].ap[:, kc * 128:(kc + 1) * 128])
                             for kc in range(KC)]
                    mm(ps, ps.ap[:, mb * 128:(mb + 1) * 128], pairs, [halves[mb], wt])
                act(kTm, kTm.ap[:, t * 256:(t + 1) * 256], ps, ps.ap[:, 0:256], ACT.Copy)
            else:
                h = t - 4
                ps = nxt("mm", ps_mm)
                for mb in range(2):
                    pairs = [(halves[mb].ap[:, kc * 128:(kc + 1) * 128], wt.ap[:, kc * 128:(kc + 1) * 128])
                             for kc in range(KC)]
                    mm(ps, ps.ap[:, mb * 128:(mb + 1) * 128], pairs, [halves[mb], wt])
                for mb in range(2):
                    o = vm.ap[:, (mb * 4 + h) * 129:(mb * 4 + h) * 129 + 128]
                    act(vm, o, ps, ps.ap[:, mb * 128:(mb + 1) * 128], ACT.Copy)
            yield
        for j in range(min(LOOK + 2, nb + 4)):
            kv_stage(s, j)
            yield

    def phase1(s, i, gblk):
        nb = nbs[s]
        j = i + 2
        xt = xtok[gblk % 2]
        idxu = idxu2[gblk % 2]
        gts = gts2[gblk % 2]
        if i == 0:
            yield from seg_prologue(s)
        jn = i + 2 + LOOK
        if jn < nb + 4:
            kv_stage(s, jn)
            yield
        dma_load(xTq, xTq.ap, d_xT[s][j])
        dma_load(ropeq, ropeq.ap, d_rope[s][j])
        S.op("sp", lambda e, a=xt.ap, b=d_xtok[s][i]: e.dma_start(out=a, in_=b),
             reads=[], writes=[xt], dma=True, chan=xt.chan)
        for half in range(2):
            ps = nxt("mm", ps_mm)
            tokmajor(xTq, [d_win[half * 4 + q] for q in range(4)], ps)
            act(tok_tmp, tok_tmp.ap[:, half * 512:(half + 1) * 512], ps, ps.ap, ACT.Copy)
            yield
        rope(8, ropeq)
        for half in range(2):
            ps = nxt("mm", ps_mm)
            for q in range(4):
                h = half * 4 + q
                transpose(ps, ps.ap[:, q * 128:(q + 1) * 128], tok_tmp.ap[:, h * 128:(h + 1) * 128],
                          [tok_tmp])
            act(QT, QT.ap[:, half * 512:(half + 1) * 512], ps, ps.ap, ACT.Copy)
        yield
        for grp, t0 in ((2, 12), (3, 24)):
            ps = nxt("mm", ps_mm)
            featmajor(xTq, [d_win[t0 + q] for q in range(4)], ps)
            act(QT, QT.ap[:, grp * 512:(grp + 1) * 512], ps, ps.ap, ACT.Copy)
            yield
        aslot = 0 if i == 0 else (2 if i == nb - 1 else 1)
        am_off = aslot * 384
        heads = []
        for h in range(8):
            hk = h // 4
            kl = []
            for d in (-1, 0, 1):
                sl = (j + d) % KVR
                kl.append((kTa[sl].ap[:, hk * 128:(hk + 1) * 128], kTa[sl],
                           va[sl].ap[:, hk * 129:(hk + 1) * 129], va[sl]))
            heads.append(dict(q_ap=QT.ap[:, h * 128:(h + 1) * 128], keys=kl, col=h,
                              mask=lambda jj, o=am_off: (amask.ap[:, o + jj * 128:o + (jj + 1) * 128], amask)))
        if i == 0:
            bslot, dl = 0, list(range(-2, 4))
        elif i == nb - 1:
            bslot, dl = 4, list(range(-3, 3))
        else:
            bslot = 1 if i == 1 else (3 if i == nb - 2 else 2)
            dl = list(range(-2, 3))
        d0 = dl[0] + 3
        for h in range(4):
            kl = []
            for d in dl:
                sl = (j + d) % KVR
                kl.append((kTb[sl].ap[:, h * 128:(h + 1) * 128], kTb[sl],
                           vb[sl].ap[:, h * 129:(h + 1) * 129], vb[sl]))
            bo = (h % 2) * 1024
            heads.append(dict(q_ap=QT.ap[:, (8 + h) * 128:(9 + h) * 128], keys=kl, col=8 + h,
                              pre=lambda bo=bo, h=h: dma_load(B4, B4.ap[:, bo:bo + 896],
                                                              d_btab[s * 5 + bslot][:, h * 896:(h + 1) * 896]),
                              mask=lambda jj, bo=bo, d0=d0: (B4.ap[:, bo + (d0 + jj) * 128:bo + (d0 + jj + 1) * 128], B4)))
        for h in range(4):
            kl = []
            for mb in range(2):
                kl.append((kTm.ap[:, h * 256 + mb * 128:h * 256 + (mb + 1) * 128], kTm,
                           vm.ap[:, (mb * 4 + h) * 129:(mb * 4 + h + 1) * 129], vm))
            heads.append(dict(q_ap=QT.ap[:, (12 + h) * 128:(13 + h) * 128], keys=kl, col=12 + h, mask=None))
        yield from attn_all(heads)
        attn_finish()
        for grp in range(4):
            ps = nxt("mm", ps_mm)
            for q in range(4):
                c = grp * 4 + q
                transpose(ps, ps.ap[:, q * 128:(q + 1) * 128], o_tok.ap[:, c * 128:(c + 1) * 128], [o_tok])
            act(oT, oT.ap[:, grp * 512:(grp + 1) * 512], ps, ps.ap, ACT.Copy)
        yield
        for ec in range(16):
            psg = nxt("mm", ps_mm)
            featmajor(xTq, [d_win[28 + b * 16 + ec] for b in range(3)], psg)
            for b in range(3):
                act(g_sb, g_sb.ap[:, b * 128:(b + 1) * 128], psg, psg.ap[:, b * 128:(b + 1) * 128],
                    ACT.Sigmoid, extra_reads=[bgate], bias=bgate.ap[:, b * 16 + ec:b * 16 + ec + 1])
            psp = nxt("mm", ps_mm)
            for b, (wd, kcs, xoff) in enumerate(((d_wpa[ec], 8, 0), (d_wpb[ec], 4, 8), (d_wpm[ec], 4, 12))):
                wt = w_next(wd)
                pairs = [(wt.ap[:, kc * 128:(kc + 1) * 128], oT.ap[:, (xoff + kc) * 128:(xoff + kc + 1) * 128])
                         for kc in range(kcs)]
                mm(psp, psp.ap[:, b * 128:(b + 1) * 128], pairs, [oT, wt])
            dve(lambda e, pp=psp: e.tensor_tensor(gm_sb.ap, pp.ap[:, 0:384], g_sb.ap, ALU.mult),
                [psp, g_sb], [gm_sb])
            mo = mT.ap[:, ec * 128:(ec + 1) * 128]
            dve(lambda e, mo=mo: e.tensor_reduce(mo, sb_ap(gm_sb.ap, 0, [[1, 128], [128, 3]]), AX.X, ALU.add),
                [gm_sb], [mT])
            yield
        for grp in range(4):
            ps = nxt("mm", ps_mm)
            tokmajor(mT, [d_wout[grp * 4 + q] for q in range(4)], ps)
            xs = xt.ap[:, grp * 512:(grp + 1) * 512]
            dve(lambda e, xs=xs, ps=ps: e.scalar_tensor_tensor(xs, xs, ALPHA, ps.ap, ALU.mult, ALU.add),
                [xt, ps], [xt])
            yield
        layernorm(xt, 0, 1, B4, B1, stats)
        yield
        x1T = xTq
        for grp in range(4):
            ps = nxt("mm", ps_mm)
            for q in range(4):
                c = grp * 4 + q
                transpose(ps, ps.ap[:, q * 128:(q + 1) * 128], xt.ap[:, c * 128:(c + 1) * 128], [xt])
            act(x1T, x1T.ap[:, grp * 512:(grp + 1) * 512], ps, ps.ap, ACT.Copy)
        yield
        qTp = QT
        for grp in range(4):
            ps = nxt("mm", ps_mm)
            featmajor(x1T, [d_wpq[grp * 4 + q] for q in range(4)], ps)
            act(qTp, qTp.ap[:, grp * 512:(grp + 1) * 512], ps, ps.ap, ACT.Copy)
            yield
        s_sb = B4
        for grp in range(4):
            ps = nxt("mm", ps_mm)
            for q in range(4):
                g = grp * 4 + q
                side = g % 2
                mm(ps, ps.ap[:, q * 128:(q + 1) * 128],
                   [(qTp.ap[:, g * 128:(g + 1) * 128], keysT.ap[:, side * 128:(side + 1) * 128])],
                   [qTp, keysT])
            act(s_sb, s_sb.ap[:, grp * 512:(grp + 1) * 512], ps, ps.ap, ACT.Copy)
        yield
        for g in range(16):
            top16(s_sb, s_sb.ap[:, g * 128:(g + 1) * 128], 128, s2.ap[:, 0:128],
                  tv, tv.ap[:, g * 16:(g + 1) * 16], ti, ti.ap[:, g * 16:(g + 1) * 16])
            if g % 2 == 1:
                yield
        cand = B1
        dve(lambda e: e.tensor_tensor(sb_ap(cand.ap, 0, [[256, 8], [16, 16], [1, 16]]),
                                      sb_ap(tv.ap, 0, [[32, 8], [1, 16], [0, 16]]),
                                      sb_ap(tv.ap, 16, [[32, 8], [0, 16], [1, 16]]), ALU.add),
            [tv], [cand])
        for h in range(8):
            top16(cand, cand.ap[:, h * 256:(h + 1) * 256], 256, s2.ap[:, 0:256],
                  topv, topv.ap[:, h * 16:(h + 1) * 16], topi, topi.ap[:, h * 16:(h + 1) * 16])
            if h % 2 == 1:
                yield
        dve(lambda e: e.tensor_tensor(ai.ap, topi.ap, sb_ap(c4.ap, 0, [[0, 128]]), ALU.logical_shift_right),
            [topi, c4], [ai])
        dve(lambda e: e.tensor_copy(af.ap, ai.ap), [ai], [af])
        dve(lambda e: e.tensor_tensor(ai.ap, topi.ap, sb_ap(c15.ap, 0, [[0, 128]]), ALU.bitwise_and),
            [topi, c15], [ai])
        dve(lambda e: e.tensor_copy(bf.ap, ai.ap), [ai], [bf])
        dve(lambda e: e.tensor_copy(tif.ap, ti.ap), [ti], [tif])
        yield
        oh = B4
        for (pf, off, dst) in ((af, 0, isel1), (bf, 16, isel2)):
            dve(lambda e, pf=pf: e.tensor_tensor(sb_ap(oh.ap, 0, [[16, 128], [1, 16]]),
                                                 sb_ap(pf.ap, 0, [[1, 128], [0, 16]]),
                                                 sb_ap(iota16.ap, 0, [[0, 128], [1, 16]]), ALU.is_equal),
                [pf, iota16], [oh])
            dve(lambda e, off=off: e.tensor_tensor(sb_ap(oh.ap, 0, [[256, 8], [16, 16], [1, 16]]),
                                                   sb_ap(oh.ap, 0, [[256, 8], [16, 16], [1, 16]]),
                                                   sb_ap(tif.ap, off, [[32, 8], [0, 16], [1, 16]]), ALU.mult),
                [oh, tif], [oh])
            dve(lambda e, dst=dst: e.tensor_reduce(dst.ap, sb_ap(oh.ap, 0, [[16, 128], [1, 16]]),
                                                   AX.X, ALU.add), [oh], [dst])
            yield
        dve(lambda e: e.scalar_tensor_tensor(idxf.ap, isel1.ap, 128.0, isel2.ap, ALU.mult, ALU.add),
            [isel1, isel2], [idxf])
        dve(lambda e: e.tensor_copy(idxu.ap, idxf.ap), [idxf], [idxu])
        dve(lambda e: e.tensor_tensor(sb_ap(gts.ap, 0, [[16, 8], [1, 16]]),
                                      sb_ap(topv.ap, 0, [[16, 8], [1, 16]]),
                                      sb_ap(topv.ap, 0, [[16, 8], [0, 16]]), ALU.subtract), [topv], [gts])
        act(gts, gts.ap, gts, gts.ap, ACT.Exp)
        dve(lambda e: e.tensor_reduce(gsum.ap[:, 0:8], sb_ap(gts.ap, 0, [[16, 8], [1, 16]]), AX.X, ALU.add),
            [gts], [gsum])
        dve(lambda e: e.reciprocal(gsum.ap[:, 8:16], gsum.ap[:, 0:8]), [gsum], [gsum])
        dve(lambda e: e.tensor_tensor(sb_ap(gts.ap, 0, [[16, 8], [1, 16]]),
                                      sb_ap(gts.ap, 0, [[16, 8], [1, 16]]),
                                      sb_ap(gsum.ap, 8, [[1, 8], [0, 16]]), ALU.mult), [gts, gsum], [gts])
        yield

    def phase2(s, i, gblk):
        xt = xtok[gblk % 2]
        idxu = idxu2[gblk % 2]
        gts = gts2[gblk % 2]

        def gather(buf, table, col):
            return S.op("pool", lambda e, b=buf, c=col: e.indirect_dma_start(
                out=b.ap, out_offset=None, in_=table,
                in_offset=bass.IndirectOffsetOnAxis(ap=idxu.ap[:, c:c + 1], axis=0)),
                reads=[idxu], writes=[buf], dma=True, chan=gchan[id(buf)])

        for c in range(128):
            ub = gbuf[c % NG]
            gather(ub, d_pu, c)
            dve(lambda e, ub=ub, c=c: e.scalar_tensor_tensor(
                ub.ap, ub.ap, 1.0, xt.ap, ALU.mult, ALU.mult, accum_out=a_sb.ap[:, c:c + 1]),
                [ub, xt], [ub, a_sb])
            yield
        act(w_sb, w_sb.ap, a_sb, a_sb.ap, ACT.Gelu)
        dve(lambda e: e.tensor_tensor(w_sb.ap, w_sb.ap, gts.ap, ALU.mult), [w_sb, gts], [w_sb])
        dve(lambda e: e.tensor_scalar(xt.ap, xt.ap, ALPHA, None, ALU.mult), [xt], [xt])
        for c in range(128):
            vbf = gbuf[(128 + c) % NG]
            gather(vbf, d_pv, c)
            dve(lambda e, vbf=vbf, c=c: e.scalar_tensor_tensor(
                xt.ap, vbf.ap, w_sb.ap[:, c:c + 1], xt.ap, ALU.mult, ALU.add),
                [vbf, w_sb, xt], [xt])
            yield
        layernorm(xt, 2, 3, gbuf[0], gbuf[1], stats2)
        st = S.op("sp", lambda e, a=d_y[s][i], b=xt.ap: e.dma_start(out=a, in_=b),
                  reads=[xt], writes=[], dma=True, chan=xtok_st[gblk % 2])
        S.store_ops.append(st)
        yield

    blocks = [(s, i) for s in range(nseg) for i in range(nbs[s])]
    for _ in phase1(blocks[0][0], blocks[0][1], 0):
        pass
    for g, (s, i) in enumerate(blocks):
        p2 = phase2(s, i, g)
        p1 = phase1(blocks[g + 1][0], blocks[g + 1][1], g + 1) if g + 1 < len(blocks) else None
        alive2, alive1 = True, p1 is not None
        while alive2 or alive1:
            for _ in range(P2_STEPS):
                if alive2 and next(p2, "done") == "done":
                    alive2 = False
            if alive1 and next(p1, "done") == "done":
                alive1 = False
    assert wstate["used"] == len(wseq), (wstate, len(wseq))

    S.emit(nc, es)
    es.close()
    return nc


def _tile_w(W, kcn):
    K, N = W.shape
    assert K == kcn * 128
    return np.ascontiguousarray(W.reshape(kcn, 128, N // 128, 128).transpose(2, 1, 0, 3)).reshape(
        N // 128, 128, kcn * 128)


def _btab(rpb, gi, nbt):
    rows = 2 * nbt
    wr = min(8, rows)
    k = np.arange(128)
    q = np.arange(128)
    kr2, kc = k // 64, k % 64
    qr2, qc = q // 64, q % 64
    cs = np.clip(qc - 8, 0, 64 - 16)
    col_ok = (kc[:, None] >= cs[None, :]) & (kc[:, None] < cs[None, :] + 16)
    dc = np.clip(kc[:, None] - qc[None, :], -15, 15) + 15
    out = np.full((128, 4, 7, 128), NEG, np.float32)
    for d in range(-3, 4):
        gk = gi + d
        krg = 2 * gk + kr2
        r = 2 * gi + qr2
        rs = np.clip(r - wr // 2, 0, rows - wr)
        ok = (krg[:, None] >= 0) & (krg[:, None] < rows) & (krg[:, None] >= rs[None, :]) \
            & (krg[:, None] < rs[None, :] + wr) & col_ok
        dr = np.clip(krg[:, None] - r[None, :] + 7, 0, 14)
        vals = rpb[:, dr, dc]
        out[:, :, d + 3, :] = np.where(ok[:, None, :], vals.transpose(1, 0, 2), np.float32(NEG))
    return out.reshape(128, 4 * 7 * 128)


def _amask(gi, nbt):
    k = np.arange(128)[:, None]
    q = np.arange(128)[None, :]
    out = np.zeros((128, 3, 128), np.float32)
    for jj, d in enumerate((-1, 0, 1)):
        ok = (np.abs(d * 128 + k - q) <= 128) & (0 <= gi + d < nbt)
        out[:, jj, :] = np.where(ok, np.float32(0.0), np.float32(NEG))
    return out.reshape(128, 384)


def _rope_tab(pos):
    half = 16
    inv = (np.float32(ROPE_THETA) ** (-np.arange(half, dtype=np.float32) * np.float32(2.0) / np.float32(32))).astype(np.float32)
    ang = pos.astype(np.float32)[:, None] * inv[None, :]
    c = np.cos(ang).astype(np.float32)
    s = np.sin(ang).astype(np.float32)
    return np.concatenate([np.tile(c, (1, 8)), np.tile(s, (1, 8))], axis=1)


def prep_shared(inp):
    w_in = np.asarray(inp["w_in"][0], np.float32)
    sh = {
        "win_t": _tile_w(w_in, 16),
        "wpa_t": _tile_w(np.asarray(inp["w_proj_a"][0], np.float32), 8),
        "wpb_t": _tile_w(np.asarray(inp["w_proj_b"][0], np.float32), 4),
        "wpm_t": _tile_w(np.asarray(inp["w_proj_m"][0], np.float32), 4),
        "wout_t": _tile_w(np.asarray(inp["w_out"][0], np.float32), 16),
        "wpq_t": _tile_w(np.asarray(inp["w_peer_q"][0], np.float32), 16),
        "wmkv_t": _tile_w(np.asarray(inp["w_mem_kv"][0], np.float32), 16),
        "bgate": np.ascontiguousarray(np.asarray(inp["b_gate"][0], np.float32).reshape(48, 128).T),
        "sink": np.asarray(inp["a_sink"], np.float32).reshape(1, 8),
        "lnp": np.stack([np.asarray(inp[k][0], np.float32) for k in ("ln1_g", "ln1_b", "ln2_g", "ln2_b")]),
        "ident": np.eye(128, dtype=np.float32),
        "identS": (np.eye(128, dtype=np.float32) * np.float32(1.0 / SCALE)).astype(np.float32),
        "iota16": np.tile(np.arange(16, dtype=np.float32)[None, :], (128, 1)),
        "keysT": np.ascontiguousarray(np.concatenate(
            [np.asarray(inp["peer_keys1"][0], np.float32).T, np.asarray(inp["peer_keys2"][0], np.float32).T], axis=1)),
        "peer_u": np.asarray(inp["peer_u"][0], np.float32),
        "peer_v": np.asarray(inp["peer_v"][0], np.float32),
    }
    return sh


def prep_segment(xseq, tok0, ntok, mem, rpb):
    T = xseq.shape[0]
    nbt = T // 128
    nb = ntok // 128
    g0 = tok0 // 128
    xpad = np.zeros(((nb + 4) * 128, D), np.float32)
    lo, hi = tok0 - 256, tok0 + ntok + 256
    slo, shi = max(lo, 0), min(hi, T)
    xpad[slo - lo:shi - lo] = xseq[slo:shi]
    xT = np.ascontiguousarray(xpad.reshape(nb + 4, 128, KC, 128).transpose(0, 3, 2, 1)).reshape(nb + 4, 128, D)
    xtok = np.ascontiguousarray(xseq[tok0:tok0 + ntok].reshape(nb, 128, D))
    pos = np.arange(lo, hi)
    rope = _rope_tab(np.maximum(pos, 0)).reshape(nb + 4, 128, 256)
    memT = np.ascontiguousarray(mem.reshape(256, KC, 128).transpose(2, 1, 0)).reshape(128, KC * 256)
    bslots = [g0, g0 + 1, g0 + 2, g0 + nb - 2, g0 + nb - 1]
    btab = np.stack([_btab(rpb, gi, nbt) for gi in bslots])
    aslots = [g0, g0 + 1, g0 + nb - 1]
    am = np.concatenate([_amask(gi, nbt) for gi in aslots], axis=1)
    return xT, xtok, rope, memT, btab, am


def make_in_map(shared, segs):
    m = dict(shared)
    for s, (xT, xtok, rope, memT, btab, am) in enumerate(segs):
        m["xT%d" % s] = xT
        m["xtok%d" % s] = xtok
        m["rope%d" % s] = rope
    m["memT"] = np.stack([sg[3] for sg in segs])
    m["btab"] = np.concatenate([sg[4] for sg in segs], axis=0)
    m["amask"] = np.ascontiguousarray(np.concatenate([sg[5] for sg in segs], axis=1))
    return m


_NC_CACHE = {}


def kernel(x_prompt, x_sample, mem_prompt, mem_sample, w_in, b_gate, a_sink, na_rpb, w_mem_kv,
           w_proj_a, w_proj_b, w_proj_m, w_out, ln1_g, ln1_b, w_peer_q, peer_keys1, peer_keys2,
           peer_u, peer_v, ln2_g, ln2_b):
    inp = dict(w_in=w_in, b_gate=b_gate, a_sink=a_sink, w_mem_kv=w_mem_kv, w_proj_a=w_proj_a,
               w_proj_b=w_proj_b, w_proj_m=w_proj_m, w_out=w_out, ln1_g=ln1_g, ln1_b=ln1_b,
               w_peer_q=w_peer_q, peer_keys1=peer_keys1, peer_keys2=peer_keys2, peer_u=peer_u,
               peer_v=peer_v, ln2_g=ln2_g, ln2_b=ln2_b)
    n = 8
    shared = prep_shared(inp)
    rpb = np.asarray(na_rpb[0], np.float32)
    xs = np.asarray(x_sample, np.float32)
    xp = np.asarray(x_prompt, np.float32)[0]
    ms = np.asarray(mem_sample, np.float32)
    mp = np.asarray(mem_prompt, np.float32)[0]
    TP = xp.shape[0] // n
    in_maps = []
    for c in range(n):
        seg0 = prep_segment(xs[c], 0, xs.shape[1], ms[c], rpb)
        seg1 = prep_segment(xp, c * TP, TP, mp, rpb)
        in_maps.append(make_in_map(shared, [seg0, seg1]))
    nbs = (xs.shape[1] // 128, TP // 128)
    nc = build_program(list(nbs))
    res = run_bass_kernel_spmd(nc, in_maps, core_ids=list(range(n)))
    y_s = np.stack([np.asarray(res.results[c]["y0"]).reshape(-1, D) for c in range(n)]).astype(np.float32)
    y_p = np.concatenate([np.asarray(res.results[c]["y1"]).reshape(-1, D) for c in range(n)], axis=0)[None]
    return (y_p.astype(np.float32), y_s)
```

```python
import math
from contextlib import ExitStack

import numpy as np
import concourse.bass as bass
import concourse.mybir as mybir
from concourse.bass_utils import run_bass_kernel_spmd

F32 = mybir.dt.float32
U32 = mybir.dt.uint32
ALU = mybir.AluOpType
ACT = mybir.ActivationFunctionType
AX = mybir.AxisListType

D = 2048
KC = 16
HD = 128
IN_W = 9728
N_EXP = 16384
ALPHA = 2.0 ** 0.25
LN_EPS = 1e-5
SCALE = HD ** -0.5
NEG = -30000.0
ROPE_THETA = 500000.0

NW = 5
NG = 6
KVR = 6
LOOK = 3
STOP = 0
WQ = "pool"
P2_STEPS = 2


class Tile:
    def __init__(self, name, ap=None):
        self.name = name
        self.ap = ap
        self.writer = None
        self.readers = []
        self.chan = None
        self.excl = False


class Op:
    __slots__ = ("eng", "fn", "deps", "dma", "chan", "needed", "val", "ninst")

    def __init__(self, eng, fn, dma=False, chan=None):
        self.eng = eng
        self.fn = fn
        self.deps = []
        self.dma = dma
        self.chan = chan
        self.needed = False
        self.val = None
        self.ninst = 1


class Sched:
    ENGS = ("pe", "act", "dve", "pool", "sp")

    def __init__(self):
        self.prog = {e: [] for e in self.ENGS}
        self.all_ops = []
        self.nchan = 0
        self.store_ops = []

    def new_chan(self):
        self.nchan += 1
        return self.nchan - 1

    def op(self, eng, fn, reads=(), writes=(), dma=False, chan=None, ninst=1):
        o = Op(eng, fn, dma, chan)
        o.ninst = ninst
        deps = set()
        for t in reads:
            if t.writer is not None:
                deps.add(t.writer)
            if t.excl:
                for r in t.readers:
                    if r.eng != eng:
                        deps.add(r)
        for t in writes:
            if t.writer is not None:
                deps.add(t.writer)
            for r in t.readers:
                deps.add(r)
        deps.discard(o)
        o.deps = list(deps)
        for t in reads:
            t.readers.append(o)
        for t in writes:
            t.writer = o
            t.readers = []
        self.prog[eng].append(o)
        self.all_ops.append(o)
        return o

    def emit(self, nc, es):
        for e in self.ENGS:
            for o in self.prog[e]:
                for d in o.deps:
                    if d.eng == "pe" and o.eng == "pe" and not d.dma:
                        continue
                    d.needed = True
        esem = {e: es.enter_context(nc.semaphore("sem_" + e)) for e in ("pe", "act", "dve", "pool")}
        csem = [es.enter_context(nc.semaphore("ch%d" % i)) for i in range(self.nchan)]
        cnt = {e: 0 for e in esem}
        ccnt = [0] * self.nchan
        for o in self.all_ops:
            if o.dma:
                ccnt[o.chan] += 16 * o.ninst
                o.val = (csem[o.chan], ccnt[o.chan])
        for e in ("pe", "act", "dve", "pool"):
            for o in self.prog[e]:
                if not o.dma and o.needed:
                    cnt[e] += 1
                    o.val = (esem[e], cnt[e])
        final = [(csem[o.chan], o.val[1]) for o in self.store_ops]
        engobj = {"pe": "tensor", "act": "scalar", "dve": "vector", "pool": "gpsimd", "sp": "sync"}

        with nc.Block() as block:
            def body(ename):
                def run(eng):
                    waited = {}
                    for o in self.prog[ename]:
                        need = {}
                        for d in o.deps:
                            if d.eng == "pe" and ename == "pe" and not d.dma:
                                continue
                            s, v = d.val
                            k = id(s)
                            if k not in need or need[k][1] < v:
                                need[k] = (s, v)
                        for k, (s, v) in need.items():
                            if waited.get(k, 0) >= v:
                                continue
                            eng.wait_ge(s, v)
                            waited[k] = v
                        insts = o.fn(eng)
                        if o.dma:
                            if not isinstance(insts, (list, tuple)):
                                insts = [insts]
                            assert len(insts) == o.ninst
                            for ins in insts:
                                ins.then_inc(o.val[0], 16)
                        elif o.needed:
                            insts.then_inc(o.val[0], 1)
                    if ename == "sp":
                        for s, v in final:
                            eng.wait_ge(s, v)
                return run

            block.tensor(body("pe"))
            block.scalar(body("act"))
            block.vector(body("dve"))
            block.gpsimd(body("pool"))
            block.sync(body("sp"))


def sb_ap(t, offset, dims):
    full = t.ap
    return bass.AP(t.tensor, t.offset + offset, [[full[0][0], full[0][1]]] + [list(d) for d in dims])


def build_program(nbs):
    nc = bass.Bass("TRN2", target_bir_lowering=False)
    S = Sched()
    es = ExitStack()
    nseg = len(nbs)

    def din(name, shape, dt=F32):
        return nc.dram_tensor(name, list(shape), dt, kind="ExternalInput").ap()

    d_xT = [din("xT%d" % s, [nbs[s] + 4, 128, D]) for s in range(nseg)]
    d_xtok = [din("xtok%d" % s, [nbs[s], 128, D]) for s in range(nseg)]
    d_rope = [din("rope%d" % s, [nbs[s] + 4, 128, 256]) for s in range(nseg)]
    d_memT = din("memT", [nseg, 128, KC * 256])
    d_btab = din("btab", [nseg * 5, 128, 4 * 7 * 128])
    d_amask = din("amask", [128, nseg * 3 * 384])
    d_win = din("win_t", [76, 128, D])
    d_wprj = din("wprj_t", [16, 128, D])
    d_wout = din("wout_t", [16, 128, D])
    d_wpq = din("wpq_t", [16, 128, D])
    d_wmkv = din("wmkv_t", [8, 128, D])
    d_bg = din("bgate", [128, 48])
    d_sink = din("sink", [1, 8])
    d_ln = din("lnp", [4, D])
    d_ident = din("ident", [128, 128])
    d_identS = din("identS", [128, 128])
    d_iota = din("iota16", [128, 16])
    d_keysT = din("keysT", [128, 256])
    d_pu = din("peer_u", [N_EXP if STOP < 10 else 128, D])
    d_pv = din("peer_v", [N_EXP if STOP < 10 else 128, D])
    d_y = [nc.dram_tensor("y%d" % s, [nbs[s], 128, D], F32, kind="ExternalOutput").ap()
           for s in range(nseg)]

    def sb(name, shape, dt=F32, dma=False):
        t = es.enter_context(nc.sbuf_tensor("sb_" + name, list(shape), dt))
        tl = Tile(name, t[:])
        if dma:
            tl.chan = S.new_chan()
        return tl

    B1 = sb("B1", [128, D], dma=True)
    B2 = sb("B2", [128, D], dma=True)
    B3 = sb("B3", [128, D], dma=True)
    B4 = sb("B4", [128, D], dma=True)
    xTkv, xTq, QT, oT, mT = B1, B2, B3, B3, B1
    gbuf = [sb("gbuf%d" % i, [128, D], dma=True) for i in range(NG)]
    gchan = {id(t): S.new_chan() for t in gbuf}
    xtok = [sb("xtok%d" % i, [128, D], dma=True) for i in range(2)]
    xtok_st = [S.new_chan() for _ in range(2)]
    wring = [sb("w%d" % i, [128, D], dma=True) for i in range(NW)]
    kTa = [sb("kTa%d" % i, [128, 2 * 128]) for i in range(KVR)]
    va = [sb("va%d" % i, [128, 2 * 129]) for i in range(KVR)]
    kTb = [sb("kTb%d" % i, [128, 4 * 128]) for i in range(KVR)]
    vb = [sb("vb%d" % i, [128, 4 * 129]) for i in range(KVR)]
    kTm = sb("kTm", [128, 4 * 256])
    vm = sb("vm", [128, 2 * 4 * 129])
    identS = sb("identS", [128, 128], dma=True)
    dens = sb("dens", [128, 32])
    amask = sb("amask", [128, 3 * 384], dma=True)
    ropekv = sb("ropekv", [128, 256], dma=True)
    ropeq = sb("ropeq", [128, 256], dma=True)
    ident = sb("ident", [128, 128], dma=True)
    iota16 = sb("iota16", [128, 16], dma=True)
    keysT = sb("keysT", [128, 256], dma=True)
    bgate = sb("bgate", [128, 48], dma=True)
    sink = sb("sink", [128, 8], dma=True)
    sinke = sb("sinke", [128, 8])
    tok_tmp = B4
    o_tok = B1
    p_sb = [sb("p_sb%d" % i, [128, 512]) for i in range(2)]
    rtmp = p_sb[0]
    g_sb = sb("g_sb", [128, 384])
    gm_sb = sb("gm_sb", [128, 384])
    small = sb("small", [128, 64])
    stats = sb("stats", [128, 32])
    stats2 = sb("stats2", [128, 32])
    tv = sb("tv", [128, 256])
    ti = sb("ti", [128, 256], U32)
    tif = sb("tif", [128, 256])
    s2 = gm_sb
    topv = sb("topv", [128, 128])
    topi = sb("topi", [128, 128], U32)
    ai = sb("ai", [128, 128], U32)
    af = sb("af", [128, 128])
    bf = sb("bf", [128, 128])
    c4 = sb("c4", [128, 2], U32)
    c15 = sb("c15", [128, 2], U32)
    isel1 = sb("isel1", [128, 128])
    isel2 = sb("isel2", [128, 128])
    idxf = sb("idxf", [128, 128])
    idxu2 = [sb("idxu%d" % i, [128, 128], U32) for i in range(2)]
    gts2 = [sb("gts%d" % i, [128, 128]) for i in range(2)]
    a_sb = sb("a_sb", [128, 128])
    w_sb = sb("w_sb", [128, 128])
    gsum = sb("gsum", [128, 16])

    psum = []
    for i in range(8):
        t = es.enter_context(nc.psum_tensor("ps%d" % i, [128, 512], F32))
        psum.append(Tile("ps%d" % i, t[:]))
        psum[-1].excl = True
    ps_mm = psum[0:4]
    ps_s = psum[4:6]
    ps_o = psum[6:8]
    rr = {"mm": 0, "s": 0, "o": 0, "p": 0, "w": 0, "bt": 0}

    def nxt(key, lst):
        v = lst[rr[key] % len(lst)]
        rr[key] += 1
        return v

    def dma_load(dst, dst_ap, src_ap, reads=()):
        return S.op(WQ, lambda e, a=dst_ap, b=src_ap: e.dma_start(out=a, in_=b),
                    reads=reads, writes=[dst], dma=True, chan=dst.chan)

    def bcast_rows(dram, row, ncols):
        return bass.AP(dram.tensor, dram.offset + row * ncols, [[0, 128], [1, ncols]])

    def mm(ps, out_ap, pairs, reads):
        n = len(pairs)
        last = None
        for k, (l, r) in enumerate(pairs):
            last = S.op("pe", lambda e, o=out_ap, l=l, r=r, st=(k == 0), sp=(k == n - 1):
                        e.matmul(o, l, r, start=st, stop=sp),
                        reads=reads, writes=[ps])
        return last

    def transpose(ps, out_ap, in_ap, reads):
        return S.op("pe", lambda e, o=out_ap, i=in_ap: e.transpose(o, i, ident.ap),
                    reads=list(reads) + [ident], writes=[ps])

    def act(out_t, out_ap, in_t, in_ap, func, extra_reads=(), **kw):
        return S.op("act", lambda e, o=out_ap, i=in_ap: e.activation(o, i, func, **kw),
                    reads=[in_t] + list(extra_reads), writes=[out_t])

    def dve(fn, reads, writes):
        return S.op("dve", fn, reads=reads, writes=writes)

    wseq = []
    wstate = {"issued": 0, "used": 0}

    def w_issue_upto(k):
        while wstate["issued"] < min(k, len(wseq)):
            i = wstate["issued"]
            slot = wring[i % NW]
            src = wseq[i]
            n = src.shape[-1]
            S.op(WQ, lambda e, a=slot.ap[:, 0:n], b=src: e.dma_start(out=a, in_=b),
                 reads=[], writes=[slot], dma=True, chan=slot.chan)
            wstate["issued"] += 1

    def w_next(expect):
        i = wstate["used"]
        if STOP:
            wseq.append(expect)
        assert wseq[i].offset == expect.offset and wseq[i].tensor.name == expect.tensor.name, (i,)
        w_issue_upto(i + NW)
        wstate["used"] += 1
        return wring[i % NW]

    def seq_kv():
        return [d_win[t] for t in (8, 9, 10, 11, 20, 21, 22, 23, 16, 17, 18, 19)]

    def seq_main():
        l = [d_win[t] for t in list(range(0, 8)) + [12, 13, 14, 15, 24, 25, 26, 27]]
        for ec in range(16):
            l += [d_win[28 + b * 16 + ec] for b in range(3)]
            l += [d_wprj[ec]]
        l += [d_wout[ct] for ct in range(16)]
        l += [d_wpq[ct] for ct in range(16)]
        return l

    for s in range(nseg):
        wseq += [d_wmkv[t] for t in range(8)]
        for j in range(min(LOOK + 2, nbs[s] + 4)):
            wseq += seq_kv()
        for i in range(nbs[s]):
            if i + 2 + LOOK < nbs[s] + 4:
                wseq += seq_kv()
            wseq += seq_main()

    if STOP:
        wseq.clear()
    dma_load(ident, ident.ap, d_ident)
    dma_load(identS, identS.ap, d_identS)
    dma_load(iota16, iota16.ap, d_iota)
    dma_load(keysT, keysT.ap, d_keysT)
    dma_load(bgate, bgate.ap, d_bg)
    dma_load(sink, sink.ap, bcast_rows(d_sink, 0, 8))
    act(sinke, sinke.ap, sink, sink.ap, ACT.Exp)
    dve(lambda e: e.memset(c4.ap, 4), [], [c4])
    dve(lambda e: e.memset(c15.ap, 15), [], [c15])
    for r in range(KVR):
        dve(lambda e, t=va[r]: e.memset(t.ap, 1.0), [], [va[r]])
        dve(lambda e, t=vb[r]: e.memset(t.ap, 1.0), [], [vb[r]])
    dve(lambda e: e.memset(vm.ap, 1.0), [], [vm])

    def rope(nh, rtab):
        x1 = sb_ap(tok_tmp.ap, 0, [[128, nh], [1, 16]])
        x2 = sb_ap(tok_tmp.ap, 16, [[128, nh], [1, 16]])
        cs = sb_ap(rtab.ap, 0, [[16, nh], [1, 16]])
        sn = sb_ap(rtab.ap, 128, [[16, nh], [1, 16]])
        t = [sb_ap(rtmp.ap, k * 128, [[16, nh], [1, 16]]) for k in range(4)]
        dve(lambda e: e.tensor_tensor(t[0], x1, cs, ALU.mult), [tok_tmp, rtab], [rtmp])
        dve(lambda e: e.tensor_tensor(t[1], x2, sn, ALU.mult), [tok_tmp, rtab], [rtmp])
        dve(lambda e: e.tensor_tensor(t[2], x2, cs, ALU.mult), [tok_tmp, rtab], [rtmp])
        dve(lambda e: e.tensor_tensor(t[3], x1, sn, ALU.mult), [tok_tmp, rtab], [rtmp])
        dve(lambda e: e.tensor_tensor(x1, t[0], t[1], ALU.subtract), [rtmp], [tok_tmp])
        dve(lambda e: e.tensor_tensor(x2, t[2], t[3], ALU.add), [rtmp], [tok_tmp])

    def tokmajor(xT_t, wtiles_dram, ps):
        for q, wd in enumerate(wtiles_dram):
            wt = w_next(wd)
            pairs = [(xT_t.ap[:, kc * 128:(kc + 1) * 128], wt.ap[:, kc * 128:(kc + 1) * 128])
                     for kc in range(KC)]
            mm(ps, ps.ap[:, q * 128:(q + 1) * 128], pairs, [xT_t, wt])

    def featmajor(xT_t, wtiles_dram, ps, kcs=KC, xoff=0):
        for q, wd in enumerate(wtiles_dram):
            wt = w_next(wd)
            pairs = [(wt.ap[:, kc * 128:(kc + 1) * 128],
                      xT_t.ap[:, (xoff + kc) * 128:(xoff + kc + 1) * 128]) for kc in range(kcs)]
            mm(ps, ps.ap[:, q * 128:(q + 1) * 128], pairs, [xT_t, wt])

    def kv_stage(s, j):
        if STOP in (10, 20):
            return
        slot = j % KVR
        dma_load(xTkv, xTkv.ap, d_xT[s][j])
        dma_load(ropekv, ropekv.ap, d_rope[s][j])
        ps = nxt("mm", ps_mm)
        tokmajor(xTkv, [d_win[8], d_win[9], d_win[10], d_win[11]], ps)
        act(tok_tmp, tok_tmp.ap[:, 0:256], ps, ps.ap[:, 0:256], ACT.Copy)
        dve(lambda e, o=sb_ap(va[slot].ap, 0, [[129, 2], [1, 128]]),
            i=sb_ap(ps.ap, 256, [[128, 2], [1, 128]]): e.tensor_copy(o, i), [ps], [va[slot]])
        if STOP == 21:
            return
        rope(2, ropekv)
        ps2 = nxt("mm", ps_mm)
        for h in range(2):
            transpose(ps2, ps2.ap[:, h * 128:(h + 1) * 128], tok_tmp.ap[:, h * 128:(h + 1) * 128],
                      [tok_tmp])
        act(kTa[slot], kTa[slot].ap, ps2, ps2.ap[:, 0:256], ACT.Copy)
        if STOP == 22:
            return
        ps = nxt("mm", ps_mm)
        tokmajor(xTkv, [d_win[20 + q] for q in range(4)], ps)
        dve(lambda e, o=sb_ap(vb[slot].ap, 0, [[129, 4], [1, 128]]),
            i=sb_ap(ps.ap, 0, [[128, 4], [1, 128]]): e.tensor_copy(o, i), [ps], [vb[slot]])
        if STOP == 23:
            return
        ps = nxt("mm", ps_mm)
        featmajor(xTkv, [d_win[16 + q] for q in range(4)], ps)
        act(kTb[slot], kTb[slot].ap, ps, ps.ap, ACT.Copy)

    def attn_all(heads):
        chunks = []
        for hd in heads:
            nk = len(hd["keys"])
            done = 0
            first = True
            while done < nk:
                n = min(3, nk - done)
                chunks.append((hd, done, n, first, done + n == nk))
                first = False
                done += n
        state = {}

        def emit_S(ck):
            hd, j0, n, first, last = ck
            if first and hd.get("pre") is not None:
                hd["pre"]()
            pss = nxt("s", ps_s)
            for jj in range(n):
                kT_ap, kt_t, _, _ = hd["keys"][j0 + jj]
                o = pss.ap[:, jj * 128:(jj + 1) * 128]
                if hd["mask"] is None:
                    mm(pss, o, [(kT_ap, hd["q_ap"])], [kt_t, QT])
                else:
                    m_ap, m_t = hd["mask"](j0 + jj)
                    S.op("pe", lambda e, o=o, l=kT_ap, r=hd["q_ap"]: e.matmul(o, l, r, start=True, stop=False),
                         reads=[kt_t, QT], writes=[pss])
                    S.op("pe", lambda e, o=o, r=m_ap: e.matmul(o, identS.ap, r, start=False, stop=True),
                         reads=[identS, m_t], writes=[pss])
            p = nxt("p", p_sb)
            act(p, p.ap[:, 0:n * 128], pss, pss.ap[:, 0:n * 128], ACT.Exp, scale=SCALE)
            state[id(ck)] = p

        def emit_PV(ck):
            hd, j0, n, first, last = ck
            p = state.pop(id(ck))
            if first:
                hd["po"] = nxt("o", ps_o)
            po = hd["po"]
            for jj in range(n):
                _, _, v_ap, v_t = hd["keys"][j0 + jj]
                S.op("pe", lambda e, o=po.ap[:, 0:129], l=p.ap[:, jj * 128:(jj + 1) * 128], r=v_ap,
                     st=(first and jj == 0), sp=(last and jj == n - 1): e.matmul(o, l, r, start=st, stop=sp),
                     reads=[p, v_t], writes=[po])
            if last:
                c = hd["col"]
                act(o_tok, o_tok.ap[:, c * 128:(c + 1) * 128], po, po.ap[:, 0:128], ACT.Copy)
                act(dens, dens.ap[:, c:c + 1], po, po.ap[:, 128:129], ACT.Copy)

        emit_S(chunks[0])
        for k, ck in enumerate(chunks):
            if k + 1 < len(chunks):
                emit_S(chunks[k + 1])
            emit_PV(ck)
            if ck[4]:
                yield

    def attn_finish():
        dve(lambda e: e.tensor_tensor(dens.ap[:, 0:8], dens.ap[:, 0:8], sinke.ap, ALU.add), [dens, sinke], [dens])
        dve(lambda e: e.reciprocal(dens.ap[:, 16:32], dens.ap[:, 0:16]), [dens], [dens])
        dve(lambda e: e.tensor_tensor(sb_ap(o_tok.ap, 0, [[128, 16], [1, 128]]),
                                      sb_ap(o_tok.ap, 0, [[128, 16], [1, 128]]),
                                      sb_ap(dens.ap, 16, [[1, 16], [0, 128]]), ALU.mult), [o_tok, dens], [o_tok])

    def layernorm(xt, grow, brow, lnA, lnB, stats):
        dma_load(lnA, lnA.ap, bcast_rows(d_ln, grow, D))
        dma_load(lnB, lnB.ap, bcast_rows(d_ln, brow, D))
        for q in range(4):
            dve(lambda e, q=q: e.bn_stats(stats.ap[:, q * 6:(q + 1) * 6], xt.ap[:, q * 512:(q + 1) * 512]),
                [xt], [stats])
        dve(lambda e: e.bn_aggr(stats.ap[:, 24:26], stats.ap[:, 0:24]), [stats], [stats])
        dve(lambda e: e.tensor_scalar(stats.ap[:, 26:27], stats.ap[:, 25:26], LN_EPS, None, ALU.add),
            [stats], [stats])
        act(stats, stats.ap[:, 27:28], stats, stats.ap[:, 26:27], ACT.Sqrt)
        dve(lambda e: e.reciprocal(stats.ap[:, 28:29], stats.ap[:, 27:28]), [stats], [stats])
        dve(lambda e: e.tensor_scalar(xt.ap, xt.ap, stats.ap[:, 24:25], stats.ap[:, 28:29],
                                      ALU.subtract, ALU.mult), [xt, stats], [xt])
        dve(lambda e: e.tensor_tensor(xt.ap, xt.ap, lnA.ap, ALU.mult), [xt, lnA], [xt])
        dve(lambda e: e.tensor_tensor(xt.ap, xt.ap, lnB.ap, ALU.add), [xt, lnB], [xt])

    def top16(src_t, src_ap, n, scratch_ap, outv_t, outv_ap, outi_t, outi_ap):
        dve(lambda e: e.max(outv_ap[:, 0:8], src_ap), [src_t], [outv_t])
        dve(lambda e: e.max_index(outi_ap[:, 0:8], outv_ap[:, 0:8], src_ap), [src_t, outv_t], [outi_t])
        dve(lambda e: e.match_replace(scratch_ap, outv_ap[:, 0:8], src_ap, -3.0e38),
            [src_t, outv_t], [s2])
        dve(lambda e: e.max(outv_ap[:, 8:16], scratch_ap), [s2], [outv_t])
        dve(lambda e: e.max_index(outi_ap[:, 8:16], outv_ap[:, 8:16], scratch_ap), [s2, outv_t], [outi_t])

    def seg_prologue(s):
        nb = nbs[s]
        halves = (xTkv, xTq)
        dma_load(amask, amask.ap, d_amask[:, s * 1152:(s + 1) * 1152])
        for mb in range(2):
            src = bass.AP(d_memT.tensor, d_memT.offset + s * 128 * KC * 256 + mb * 128,
                          [[KC * 256, 128], [256, KC], [1, 128]])
            dma_load(halves[mb], sb_ap(halves[mb].ap, 0, [[128, KC], [1, 128]]), src)
        for t in range(8):
            wt = w_next(d_wmkv[t])
            if t < 4:
                ps = nxt("mm", ps_mm)
                for mb in range(2):
                    pairs = [(wt.ap[:, kc * 128:(kc + 1) * 128], halves[mb].ap[:, kc * 128:(kc + 1) * 128])
                             for kc in range(KC)]
                    mm(ps, ps.ap[:, mb * 128:(mb + 1) * 128], pairs, [halves[mb], wt])
                act(kTm, kTm.ap[:, t * 256:(t + 1) * 256], ps, ps.ap[:, 0:256], ACT.Copy)
            else:
                h = t - 4
                ps = nxt("mm", ps_mm)
                for mb in range(2):
                    pairs = [(halves[mb].ap[:, kc * 128:(kc + 1) * 128], wt.ap[:, kc * 128:(kc + 1) * 128])
                             for kc in range(KC)]
                    mm(ps, ps.ap[:, mb * 128:(mb + 1) * 128], pairs, [halves[mb], wt])
                for mb in range(2):
                    o = vm.ap[:, (mb * 4 + h) * 129:(mb * 4 + h) * 129 + 128]
                    act(vm, o, ps, ps.ap[:, mb * 128:(mb + 1) * 128], ACT.Copy)
            yield
        for j in range(min(LOOK + 2, nb + 4)):
            kv_stage(s, j)
            yield

    def phase1(s, i, gblk):
        nb = nbs[s]
        j = i + 2
        xt = xtok[gblk % 2]
        idxu = idxu2[gblk % 2]
        gts = gts2[gblk % 2]
        if i == 0:
            yield from seg_prologue(s)
        jn = i + 2 + LOOK
        if jn < nb + 4:
            kv_stage(s, jn)
            yield
        dma_load(xTq, xTq.ap, d_xT[s][j])
        dma_load(ropeq, ropeq.ap, d_rope[s][j])
        S.op(WQ, lambda e, a=xt.ap, b=d_xtok[s][i]: e.dma_start(out=a, in_=b),
             reads=[], writes=[xt], dma=True, chan=xt.chan)
        for half in range(2):
            ps = nxt("mm", ps_mm)
            tokmajor(xTq, [d_win[half * 4 + q] for q in range(4)], ps)
            act(tok_tmp, tok_tmp.ap[:, half * 512:(half + 1) * 512], ps, ps.ap, ACT.Copy)
            yield
        rope(8, ropeq)
        for half in range(2):
            ps = nxt("mm", ps_mm)
            for q in range(4):
                h = half * 4 + q
                transpose(ps, ps.ap[:, q * 128:(q + 1) * 128], tok_tmp.ap[:, h * 128:(h + 1) * 128],
                          [tok_tmp])
            act(QT, QT.ap[:, half * 512:(half + 1) * 512], ps, ps.ap, ACT.Copy)
        yield
        for grp, t0 in ((2, 12), (3, 24)):
            ps = nxt("mm", ps_mm)
            featmajor(xTq, [d_win[t0 + q] for q in range(4)], ps)
            act(QT, QT.ap[:, grp * 512:(grp + 1) * 512], ps, ps.ap, ACT.Copy)
            yield
        aslot = 0 if i == 0 else (2 if i == nb - 1 else 1)
        am_off = aslot * 384
        heads = []
        for h in range(8):
            hk = h // 4
            kl = []
            for d in (-1, 0, 1):
                sl = (j + d) % KVR
                kl.append((kTa[sl].ap[:, hk * 128:(hk + 1) * 128], kTa[sl],
                           va[sl].ap[:, hk * 129:(hk + 1) * 129], va[sl]))
            heads.append(dict(q_ap=QT.ap[:, h * 128:(h + 1) * 128], keys=kl, col=h,
                              mask=lambda jj, o=am_off: (amask.ap[:, o + jj * 128:o + (jj + 1) * 128], amask)))
        if i == 0:
            bslot, dl = 0, list(range(-2, 4))
        elif i == nb - 1:
            bslot, dl = 4, list(range(-3, 3))
        else:
            bslot = 1 if i == 1 else (3 if i == nb - 2 else 2)
            dl = list(range(-2, 3))
        d0 = dl[0] + 3
        for h in range(4):
            kl = []
            for d in dl:
                sl = (j + d) % KVR
                kl.append((kTb[sl].ap[:, h * 128:(h + 1) * 128], kTb[sl],
                           vb[sl].ap[:, h * 129:(h + 1) * 129], vb[sl]))
            bo = (h % 2) * 1024
            heads.append(dict(q_ap=QT.ap[:, (8 + h) * 128:(9 + h) * 128], keys=kl, col=8 + h,
                              pre=lambda bo=bo, h=h: dma_load(B4, B4.ap[:, bo:bo + 896],
                                                              d_btab[s * 5 + bslot][:, h * 896:(h + 1) * 896]),
                              mask=lambda jj, bo=bo, d0=d0: (B4.ap[:, bo + (d0 + jj) * 128:bo + (d0 + jj + 1) * 128], B4)))
        for h in range(4):
            kl = []
            for mb in range(2):
                kl.append((kTm.ap[:, h * 256 + mb * 128:h * 256 + (mb + 1) * 128], kTm,
                           vm.ap[:, (mb * 4 + h) * 129:(mb * 4 + h + 1) * 129], vm))
            heads.append(dict(q_ap=QT.ap[:, (12 + h) * 128:(13 + h) * 128], keys=kl, col=12 + h, mask=None))
        yield from attn_all(heads)
        attn_finish()
        for grp in range(4):
            ps = nxt("mm", ps_mm)
            for q in range(4):
                c = grp * 4 + q
                transpose(ps, ps.ap[:, q * 128:(q + 1) * 128], o_tok.ap[:, c * 128:(c + 1) * 128], [o_tok])
            act(oT, oT.ap[:, grp * 512:(grp + 1) * 512], ps, ps.ap, ACT.Copy)
        yield
        for ec in range(16):
            psg = nxt("mm", ps_mm)
            featmajor(xTq, [d_win[28 + b * 16 + ec] for b in range(3)], psg)
            for b in range(3):
                act(g_sb, g_sb.ap[:, b * 128:(b + 1) * 128], psg, psg.ap[:, b * 128:(b + 1) * 128],
                    ACT.Sigmoid, extra_reads=[bgate], bias=bgate.ap[:, b * 16 + ec:b * 16 + ec + 1])
            psp = nxt("mm", ps_mm)
            wt = w_next(d_wprj[ec])
            for b, (kcs, xoff) in enumerate(((8, 0), (4, 8), (4, 12))):
                pairs = [(wt.ap[:, (xoff + kc) * 128:(xoff + kc + 1) * 128],
                          oT.ap[:, (xoff + kc) * 128:(xoff + kc + 1) * 128]) for kc in range(kcs)]
                mm(psp, psp.ap[:, b * 128:(b + 1) * 128], pairs, [oT, wt])
            dve(lambda e, pp=psp: e.tensor_tensor(gm_sb.ap, pp.ap[:, 0:384], g_sb.ap, ALU.mult),
                [psp, g_sb], [gm_sb])
            mo = mT.ap[:, ec * 128:(ec + 1) * 128]
            dve(lambda e, mo=mo: e.tensor_reduce(mo, sb_ap(gm_sb.ap, 0, [[1, 128], [128, 3]]), AX.X, ALU.add),
                [gm_sb], [mT])
            yield
        for grp in range(4):
            ps = nxt("mm", ps_mm)
            tokmajor(mT, [d_wout[grp * 4 + q] for q in range(4)], ps)
            xs = xt.ap[:, grp * 512:(grp + 1) * 512]
            dve(lambda e, xs=xs, ps=ps: e.scalar_tensor_tensor(xs, xs, ALPHA, ps.ap, ALU.mult, ALU.add),
                [xt, ps], [xt])
            yield
        layernorm(xt, 0, 1, B4, B1, stats)
        yield
        x1T = xTq
        for grp in range(4):
            ps = nxt("mm", ps_mm)
            for q in range(4):
                c = grp * 4 + q
                transpose(ps, ps.ap[:, q * 128:(q + 1) * 128], xt.ap[:, c * 128:(c + 1) * 128], [xt])
            act(x1T, x1T.ap[:, grp * 512:(grp + 1) * 512], ps, ps.ap, ACT.Copy)
        yield
        qTp = QT
        for grp in range(4):
            ps = nxt("mm", ps_mm)
            featmajor(x1T, [d_wpq[grp * 4 + q] for q in range(4)], ps)
            act(qTp, qTp.ap[:, grp * 512:(grp + 1) * 512], ps, ps.ap, ACT.Copy)
            yield
        s_sb = B4
        for grp in range(4):
            ps = nxt("mm", ps_mm)
            for q in range(4):
                g = grp * 4 + q
                side = g % 2
                mm(ps, ps.ap[:, q * 128:(q + 1) * 128],
                   [(qTp.ap[:, g * 128:(g + 1) * 128], keysT.ap[:, side * 128:(side + 1) * 128])],
                   [qTp, keysT])
            act(s_sb, s_sb.ap[:, grp * 512:(grp + 1) * 512], ps, ps.ap, ACT.Copy)
        yield
        for g in range(16):
            top16(s_sb, s_sb.ap[:, g * 128:(g + 1) * 128], 128, s2.ap[:, 0:128],
                  tv, tv.ap[:, g * 16:(g + 1) * 16], ti, ti.ap[:, g * 16:(g + 1) * 16])
            if g % 2 == 1:
                yield
        cand = B1
        dve(lambda e: e.tensor_tensor(sb_ap(cand.ap, 0, [[256, 8], [16, 16], [1, 16]]),
                                      sb_ap(tv.ap, 0, [[32, 8], [1, 16], [0, 16]]),
                                      sb_ap(tv.ap, 16, [[32, 8], [0, 16], [1, 16]]), ALU.add),
            [tv], [cand])
        for h in range(8):
            top16(cand, cand.ap[:, h * 256:(h + 1) * 256], 256, s2.ap[:, 0:256],
                  topv, topv.ap[:, h * 16:(h + 1) * 16], topi, topi.ap[:, h * 16:(h + 1) * 16])
            if h % 2 == 1:
                yield
        dve(lambda e: e.tensor_tensor(ai.ap, topi.ap, sb_ap(c4.ap, 0, [[0, 128]]), ALU.logical_shift_right),
            [topi, c4], [ai])
        dve(lambda e: e.tensor_copy(af.ap, ai.ap), [ai], [af])
        dve(lambda e: e.tensor_tensor(ai.ap, topi.ap, sb_ap(c15.ap, 0, [[0, 128]]), ALU.bitwise_and),
            [topi, c15], [ai])
        dve(lambda e: e.tensor_copy(bf.ap, ai.ap), [ai], [bf])
        dve(lambda e: e.tensor_copy(tif.ap, ti.ap), [ti], [tif])
        yield
        oh = B4
        for (pf, off, dst) in ((af, 0, isel1), (bf, 16, isel2)):
            dve(lambda e, pf=pf: e.tensor_tensor(sb_ap(oh.ap, 0, [[16, 128], [1, 16]]),
                                                 sb_ap(pf.ap, 0, [[1, 128], [0, 16]]),
                                                 sb_ap(iota16.ap, 0, [[0, 128], [1, 16]]), ALU.is_equal),
                [pf, iota16], [oh])
            dve(lambda e, off=off: e.tensor_tensor(sb_ap(oh.ap, 0, [[256, 8], [16, 16], [1, 16]]),
                                                   sb_ap(oh.ap, 0, [[256, 8], [16, 16], [1, 16]]),
                                                   sb_ap(tif.ap, off, [[32, 8], [0, 16], [1, 16]]), ALU.mult),
                [oh, tif], [oh])
            dve(lambda e, dst=dst: e.tensor_reduce(dst.ap, sb_ap(oh.ap, 0, [[16, 128], [1, 16]]),
                                                   AX.X, ALU.add), [oh], [dst])
            yield
        dve(lambda e: e.scalar_tensor_tensor(idxf.ap, isel1.ap, 128.0, isel2.ap, ALU.mult, ALU.add),
            [isel1, isel2], [idxf])
        dve(lambda e: e.tensor_copy(idxu.ap, idxf.ap), [idxf], [idxu])
        dve(lambda e: e.tensor_tensor(sb_ap(gts.ap, 0, [[16, 8], [1, 16]]),
                                      sb_ap(topv.ap, 0, [[16, 8], [1, 16]]),
                                      sb_ap(topv.ap, 0, [[16, 8], [0, 16]]), ALU.subtract), [topv], [gts])
        act(gts, gts.ap, gts, gts.ap, ACT.Exp)
        dve(lambda e: e.tensor_reduce(gsum.ap[:, 0:8], sb_ap(gts.ap, 0, [[16, 8], [1, 16]]), AX.X, ALU.add),
            [gts], [gsum])
        dve(lambda e: e.reciprocal(gsum.ap[:, 8:16], gsum.ap[:, 0:8]), [gsum], [gsum])
        dve(lambda e: e.tensor_tensor(sb_ap(gts.ap, 0, [[16, 8], [1, 16]]),
                                      sb_ap(gts.ap, 0, [[16, 8], [1, 16]]),
                                      sb_ap(gsum.ap, 8, [[1, 8], [0, 16]]), ALU.mult), [gts, gsum], [gts])
        yield

    def phase2(s, i, gblk):
        xt = xtok[gblk % 2]
        idxu = idxu2[gblk % 2]
        gts = gts2[gblk % 2]

        def gather(buf, table, col):
            return S.op("pool", lambda e, b=buf, c=col: e.indirect_dma_start(
                out=b.ap, out_offset=None, in_=table,
                in_offset=bass.IndirectOffsetOnAxis(ap=idxu.ap[:, c:c + 1], axis=0)),
                reads=[idxu], writes=[buf], dma=True, chan=gchan[id(buf)])

        for c in range(128):
            ub = gbuf[c % NG]
            gather(ub, d_pu, c)
            dve(lambda e, ub=ub, c=c: e.scalar_tensor_tensor(
                ub.ap, ub.ap, 1.0, xt.ap, ALU.mult, ALU.mult, accum_out=a_sb.ap[:, c:c + 1]),
                [ub, xt], [ub, a_sb])
            yield
        act(w_sb, w_sb.ap, a_sb, a_sb.ap, ACT.Gelu)
        dve(lambda e: e.tensor_tensor(w_sb.ap, w_sb.ap, gts.ap, ALU.mult), [w_sb, gts], [w_sb])
        dve(lambda e: e.tensor_scalar(xt.ap, xt.ap, ALPHA, None, ALU.mult), [xt], [xt])
        for c in range(128):
            vbf = gbuf[(128 + c) % NG]
            gather(vbf, d_pv, c)
            dve(lambda e, vbf=vbf, c=c: e.scalar_tensor_tensor(
                xt.ap, vbf.ap, w_sb.ap[:, c:c + 1], xt.ap, ALU.mult, ALU.add),
                [vbf, w_sb, xt], [xt])
            yield
        layernorm(xt, 2, 3, gbuf[0], gbuf[1], stats2)
        st = S.op("sp", lambda e, a=d_y[s][i], b=xt.ap: e.dma_start(out=a, in_=b),
                  reads=[xt], writes=[], dma=True, chan=xtok_st[gblk % 2])
        S.store_ops.append(st)
        yield

    blocks = [(s, i) for s in range(nseg) for i in range(nbs[s])]
    for _ in phase1(blocks[0][0], blocks[0][1], 0):
        pass
    for g, (s, i) in enumerate(blocks):
        p2 = phase2(s, i, g)
        p1 = phase1(blocks[g + 1][0], blocks[g + 1][1], g + 1) if g + 1 < len(blocks) else None
        alive2, alive1 = True, p1 is not None
        while alive2 or alive1:
            for _ in range(P2_STEPS):
                if alive2 and next(p2, "done") == "done":
                    alive2 = False
            if alive1 and next(p1, "done") == "done":
                alive1 = False
    assert wstate["used"] == len(wseq), (wstate, len(wseq))

    S.emit(nc, es)
    es.close()
    return nc


def _tile_w(W, kcn):
    K, N = W.shape
    assert K == kcn * 128
    return np.ascontiguousarray(W.reshape(kcn, 128, N // 128, 128).transpose(2, 1, 0, 3)).reshape(
        N // 128, 128, kcn * 128)


def _btab(rpb, gi, nbt):
    rows = 2 * nbt
    wr = min(8, rows)
    k = np.arange(128)
    q = np.arange(128)
    kr2, kc = k // 64, k % 64
    qr2, qc = q // 64, q % 64
    cs = np.clip(qc - 8, 0, 64 - 16)
    col_ok = (kc[:, None] >= cs[None, :]) & (kc[:, None] < cs[None, :] + 16)
    dc = np.clip(kc[:, None] - qc[None, :], -15, 15) + 15
    out = np.full((128, 4, 7, 128), NEG, np.float32)
    for d in range(-3, 4):
        gk = gi + d
        krg = 2 * gk + kr2
        r = 2 * gi + qr2
        rs = np.clip(r - wr // 2, 0, rows - wr)
        ok = (krg[:, None] >= 0) & (krg[:, None] < rows) & (krg[:, None] >= rs[None, :]) \
            & (krg[:, None] < rs[None, :] + wr) & col_ok
        dr = np.clip(krg[:, None] - r[None, :] + 7, 0, 14)
        vals = rpb[:, dr, dc]
        out[:, :, d + 3, :] = np.where(ok[:, None, :], vals.transpose(1, 0, 2), np.float32(NEG))
    return out.reshape(128, 4 * 7 * 128)


def _amask(gi, nbt):
    k = np.arange(128)[:, None]
    q = np.arange(128)[None, :]
    out = np.zeros((128, 3, 128), np.float32)
    for jj, d in enumerate((-1, 0, 1)):
        ok = (np.abs(d * 128 + k - q) <= 128) & (0 <= gi + d < nbt)
        out[:, jj, :] = np.where(ok, np.float32(0.0), np.float32(NEG))
    return out.reshape(128, 384)


def _rope_tab(pos):
    half = 16
    inv = (np.float32(ROPE_THETA) ** (-np.arange(half, dtype=np.float32) * np.float32(2.0) / np.float32(32))).astype(np.float32)
    ang = pos.astype(np.float32)[:, None] * inv[None, :]
    c = np.cos(ang).astype(np.float32)
    s = np.sin(ang).astype(np.float32)
    return np.concatenate([np.tile(c, (1, 8)), np.tile(s, (1, 8))], axis=1)


def prep_shared(inp):
    w_in = np.asarray(inp["w_in"][0], np.float32)
    sh = {
        "win_t": _tile_w(w_in, 16),
        "wprj_t": np.ascontiguousarray(np.concatenate(
            [_tile_w(np.asarray(inp["w_proj_a"][0], np.float32), 8),
             _tile_w(np.asarray(inp["w_proj_b"][0], np.float32), 4),
             _tile_w(np.asarray(inp["w_proj_m"][0], np.float32), 4)], axis=2)),
        "wout_t": _tile_w(np.asarray(inp["w_out"][0], np.float32), 16),
        "wpq_t": _tile_w(np.asarray(inp["w_peer_q"][0], np.float32), 16),
        "wmkv_t": _tile_w(np.asarray(inp["w_mem_kv"][0], np.float32), 16),
        "bgate": np.ascontiguousarray(np.asarray(inp["b_gate"][0], np.float32).reshape(48, 128).T),
        "sink": np.asarray(inp["a_sink"], np.float32).reshape(1, 8),
        "lnp": np.stack([np.asarray(inp[k][0], np.float32) for k in ("ln1_g", "ln1_b", "ln2_g", "ln2_b")]),
        "ident": np.eye(128, dtype=np.float32),
        "identS": (np.eye(128, dtype=np.float32) * np.float32(1.0 / SCALE)).astype(np.float32),
        "iota16": np.tile(np.arange(16, dtype=np.float32)[None, :], (128, 1)),
        "keysT": np.ascontiguousarray(np.concatenate(
            [np.asarray(inp["peer_keys1"][0], np.float32).T, np.asarray(inp["peer_keys2"][0], np.float32).T], axis=1)),
        "peer_u": np.asarray(inp["peer_u"][0], np.float32),
        "peer_v": np.asarray(inp["peer_v"][0], np.float32),
    }
    return sh


def prep_segment(xseq, tok0, ntok, mem, rpb):
    T = xseq.shape[0]
    nbt = T // 128
    nb = ntok // 128
    g0 = tok0 // 128
    xpad = np.zeros(((nb + 4) * 128, D), np.float32)
    lo, hi = tok0 - 256, tok0 + ntok + 256
    slo, shi = max(lo, 0), min(hi, T)
    xpad[slo - lo:shi - lo] = xseq[slo:shi]
    xT = np.ascontiguousarray(xpad.reshape(nb + 4, 128, KC, 128).transpose(0, 3, 2, 1)).reshape(nb + 4, 128, D)
    xtok = np.ascontiguousarray(xseq[tok0:tok0 + ntok].reshape(nb, 128, D))
    pos = np.arange(lo, hi)
    rope = _rope_tab(np.maximum(pos, 0)).reshape(nb + 4, 128, 256)
    memT = np.ascontiguousarray(mem.reshape(256, KC, 128).transpose(2, 1, 0)).reshape(128, KC * 256)
    bslots = [g0, g0 + 1, g0 + 2, g0 + nb - 2, g0 + nb - 1]
    btab = np.stack([_btab(rpb, gi, nbt) for gi in bslots])
    aslots = [g0, g0 + 1, g0 + nb - 1]
    am = np.concatenate([_amask(gi, nbt) for gi in aslots], axis=1)
    return xT, xtok, rope, memT, btab, am


def make_in_map(shared, segs):
    m = dict(shared)
    for s, (xT, xtok, rope, memT, btab, am) in enumerate(segs):
        m["xT%d" % s] = xT
        m["xtok%d" % s] = xtok
        m["rope%d" % s] = rope
    m["memT"] = np.stack([sg[3] for sg in segs])
    m["btab"] = np.concatenate([sg[4] for sg in segs], axis=0)
    m["amask"] = np.ascontiguousarray(np.concatenate([sg[5] for sg in segs], axis=1))
    return m


_NC_CACHE = {}


def kernel(x_prompt, x_sample, mem_prompt, mem_sample, w_in, b_gate, a_sink, na_rpb, w_mem_kv,
           w_proj_a, w_proj_b, w_proj_m, w_out, ln1_g, ln1_b, w_peer_q, peer_keys1, peer_keys2,
           peer_u, peer_v, ln2_g, ln2_b):
    inp = dict(w_in=w_in, b_gate=b_gate, a_sink=a_sink, w_mem_kv=w_mem_kv, w_proj_a=w_proj_a,
               w_proj_b=w_proj_b, w_proj_m=w_proj_m, w_out=w_out, ln1_g=ln1_g, ln1_b=ln1_b,
               w_peer_q=w_peer_q, peer_keys1=peer_keys1, peer_keys2=peer_keys2, peer_u=peer_u,
               peer_v=peer_v, ln2_g=ln2_g, ln2_b=ln2_b)
    n = 8
    shared = prep_shared(inp)
    rpb = np.asarray(na_rpb[0], np.float32)
    xs = np.asarray(x_sample, np.float32)
    xp = np.asarray(x_prompt, np.float32)[0]
    ms = np.asarray(mem_sample, np.float32)
    mp = np.asarray(mem_prompt, np.float32)[0]
    TP = xp.shape[0] // n
    in_maps = []
    for c in range(n):
        seg0 = prep_segment(xs[c], 0, xs.shape[1], ms[c], rpb)
        seg1 = prep_segment(xp, c * TP, TP, mp, rpb)
        in_maps.append(make_in_map(shared, [seg0, seg1]))
    nbs = (xs.shape[1] // 128, TP // 128)
    nc = build_program(list(nbs))
    res = run_bass_kernel_spmd(nc, in_maps, core_ids=list(range(n)))
    y_s = np.stack([np.asarray(res.results[c]["y0"]).reshape(-1, D) for c in range(n)]).astype(np.float32)
    y_p = np.concatenate([np.asarray(res.results[c]["y1"]).reshape(-1, D) for c in range(n)], axis=0)[None]
    return (y_p.astype(np.float32), y_s)
```

```python
import math
from contextlib import ExitStack

import numpy as np
import concourse.bass as bass
import concourse.mybir as mybir
from concourse.bass_utils import run_bass_kernel_spmd

F32 = mybir.dt.float32
U32 = mybir.dt.uint32
ALU = mybir.AluOpType
ACT = mybir.ActivationFunctionType
AX = mybir.AxisListType

D = 2048
KC = 16
HD = 128
IN_W = 9728
N_EXP = 16384
ALPHA = 2.0 ** 0.25
LN_EPS = 1e-5
SCALE = HD ** -0.5
NEG = -30000.0
ROPE_THETA = 500000.0

NW = 5
NG = 6
KVR = 6
LOOK = 3
STOP = 0
WQ = "pool"
P2_STEPS = 3


class Tile:
    def __init__(self, name, ap=None):
        self.name = name
        self.ap = ap
        self.writer = None
        self.readers = []
        self.chan = None
        self.excl = False


class Op:
    __slots__ = ("eng", "fn", "deps", "dma", "chan", "needed", "val", "ninst")

    def __init__(self, eng, fn, dma=False, chan=None):
        self.eng = eng
        self.fn = fn
        self.deps = []
        self.dma = dma
        self.chan = chan
        self.needed = False
        self.val = None
        self.ninst = 1


class Sched:
    ENGS = ("pe", "act", "dve", "pool", "sp")

    def __init__(self):
        self.prog = {e: [] for e in self.ENGS}
        self.all_ops = []
        self.nchan = 0
        self.store_ops = []

    def new_chan(self):
        self.nchan += 1
        return self.nchan - 1

    def op(self, eng, fn, reads=(), writes=(), dma=False, chan=None, ninst=1):
        o = Op(eng, fn, dma, chan)
        o.ninst = ninst
        deps = set()
        for t in reads:
            if t.writer is not None:
                deps.add(t.writer)
            if t.excl:
                for r in t.readers:
                    if r.eng != eng:
                        deps.add(r)
        for t in writes:
            if t.writer is not None:
                deps.add(t.writer)
            for r in t.readers:
                deps.add(r)
        deps.discard(o)
        o.deps = list(deps)
        for t in reads:
            t.readers.append(o)
        for t in writes:
            t.writer = o
            t.readers = []
        self.prog[eng].append(o)
        self.all_ops.append(o)
        return o

    def emit(self, nc, es):
        for e in self.ENGS:
            for o in self.prog[e]:
                for d in o.deps:
                    if d.eng == "pe" and o.eng == "pe" and not d.dma:
                        continue
                    d.needed = True
        esem = {e: es.enter_context(nc.semaphore("sem_" + e)) for e in ("pe", "act", "dve", "pool")}
        csem = [es.enter_context(nc.semaphore("ch%d" % i)) for i in range(self.nchan)]
        cnt = {e: 0 for e in esem}
        ccnt = [0] * self.nchan
        for o in self.all_ops:
            if o.dma:
                ccnt[o.chan] += 16 * o.ninst
                o.val = (csem[o.chan], ccnt[o.chan])
        for e in ("pe", "act", "dve", "pool"):
            for o in self.prog[e]:
                if not o.dma and o.needed:
                    cnt[e] += 1
                    o.val = (esem[e], cnt[e])
        final = [(csem[o.chan], o.val[1]) for o in self.store_ops]
        engobj = {"pe": "tensor", "act": "scalar", "dve": "vector", "pool": "gpsimd", "sp": "sync"}

        with nc.Block() as block:
            def body(ename):
                def run(eng):
                    waited = {}
                    for o in self.prog[ename]:
                        need = {}
                        for d in o.deps:
                            if d.eng == "pe" and ename == "pe" and not d.dma:
                                continue
                            s, v = d.val
                            k = id(s)
                            if k not in need or need[k][1] < v:
                                need[k] = (s, v)
                        for k, (s, v) in need.items():
                            if waited.get(k, 0) >= v:
                                continue
                            eng.wait_ge(s, v)
                            waited[k] = v
                        insts = o.fn(eng)
                        if o.dma:
                            if not isinstance(insts, (list, tuple)):
                                insts = [insts]
                            assert len(insts) == o.ninst
                            for ins in insts:
                                ins.then_inc(o.val[0], 16)
                        elif o.needed:
                            insts.then_inc(o.val[0], 1)
                    if ename == "sp":
                        for s, v in final:
                            eng.wait_ge(s, v)
                return run

            block.tensor(body("pe"))
            block.scalar(body("act"))
            block.vector(body("dve"))
            block.gpsimd(body("pool"))
            block.sync(body("sp"))


def sb_ap(t, offset, dims):
    full = t.ap
    return bass.AP(t.tensor, t.offset + offset, [[full[0][0], full[0][1]]] + [list(d) for d in dims])


def build_program(nbs):
    nc = bass.Bass("TRN2", target_bir_lowering=False)
    S = Sched()
    es = ExitStack()
    nseg = len(nbs)

    def din(name, shape, dt=F32):
        return nc.dram_tensor(name, list(shape), dt, kind="ExternalInput").ap()

    d_xT = [din("xT%d" % s, [nbs[s] + 4, 128, D]) for s in range(nseg)]
    d_xtok = [din("xtok%d" % s, [nbs[s], 128, D]) for s in range(nseg)]
    d_rope = [din("rope%d" % s, [nbs[s] + 4, 128, 256]) for s in range(nseg)]
    d_memT = din("memT", [nseg, 128, KC * 256])
    d_btab = din("btab", [nseg * 5, 128, 4 * 7 * 128])
    d_amask = din("amask", [128, nseg * 3 * 384])
    d_win = din("win_t", [76, 128, D])
    d_wpa = din("wpa_t", [16, 128, 1024])
    d_wpb = din("wpb_t", [16, 128, 512])
    d_wpm = din("wpm_t", [16, 128, 512])
    d_wout = din("wout_t", [16, 128, D])
    d_wpq = din("wpq_t", [16, 128, D])
    d_wmkv = din("wmkv_t", [8, 128, D])
    d_bg = din("bgate", [128, 48])
    d_sink = din("sink", [1, 8])
    d_ln = din("lnp", [4, D])
    d_ident = din("ident", [128, 128])
    d_identS = din("identS", [128, 128])
    d_iota = din("iota16", [128, 16])
    d_keysT = din("keysT", [128, 256])
    d_pu = din("peer_u", [N_EXP if STOP < 10 else 128, D])
    d_pv = din("peer_v", [N_EXP if STOP < 10 else 128, D])
    d_y = [nc.dram_tensor("y%d" % s, [nbs[s], 128, D], F32, kind="ExternalOutput").ap()
           for s in range(nseg)]

    def sb(name, shape, dt=F32, dma=False):
        t = es.enter_context(nc.sbuf_tensor("sb_" + name, list(shape), dt))
        tl = Tile(name, t[:])
        if dma:
            tl.chan = S.new_chan()
        return tl

    B1 = sb("B1", [128, D], dma=True)
    B2 = sb("B2", [128, D], dma=True)
    B3 = sb("B3", [128, D], dma=True)
    B4 = sb("B4", [128, D], dma=True)
    xTkv, xTq, QT, oT, mT = B1, B2, B3, B3, B1
    gbuf = [sb("gbuf%d" % i, [128, D], dma=True) for i in range(NG)]
    gchan = {id(t): S.new_chan() for t in gbuf}
    xtok = [sb("xtok%d" % i, [128, D], dma=True) for i in range(2)]
    xtok_st = [S.new_chan() for _ in range(2)]
    wring = [sb("w%d" % i, [128, D], dma=True) for i in range(NW)]
    kTa = [sb("kTa%d" % i, [128, 2 * 128]) for i in range(KVR)]
    va = [sb("va%d" % i, [128, 2 * 129]) for i in range(KVR)]
    kTb = [sb("kTb%d" % i, [128, 4 * 128]) for i in range(KVR)]
    vb = [sb("vb%d" % i, [128, 4 * 129]) for i in range(KVR)]
    kTm = sb("kTm", [128, 4 * 256])
    vm = sb("vm", [128, 2 * 4 * 129])
    identS = sb("identS", [128, 128], dma=True)
    dens = sb("dens", [128, 32])
    amask = sb("amask", [128, 3 * 384], dma=True)
    ropekv = sb("ropekv", [128, 256], dma=True)
    ropeq = sb("ropeq", [128, 256], dma=True)
    ident = sb("ident", [128, 128], dma=True)
    iota16 = sb("iota16", [128, 16], dma=True)
    keysT = sb("keysT", [128, 256], dma=True)
    bgate = sb("bgate", [128, 48], dma=True)
    sink = sb("sink", [128, 8], dma=True)
    sinke = sb("sinke", [128, 8])
    tok_tmp = B4
    o_tok = B1
    p_sb = [sb("p_sb%d" % i, [128, 512]) for i in range(2)]
    rtmp = p_sb[0]
    g_sb = sb("g_sb", [128, 384])
    gm_sb = sb("gm_sb", [128, 384])
    small = sb("small", [128, 64])
    stats = sb("stats", [128, 32])
    stats2 = sb("stats2", [128, 32])
    tv = sb("tv", [128, 256])
    ti = sb("ti", [128, 256], U32)
    tif = sb("tif", [128, 256])
    s2 = gm_sb
    topv = sb("topv", [128, 128])
    topi = sb("topi", [128, 128], U32)
    ai = sb("ai", [128, 128], U32)
    af = sb("af", [128, 128])
    bf = sb("bf", [128, 128])
    c4 = sb("c4", [128, 2], U32)
    c15 = sb("c15", [128, 2], U32)
    isel1 = sb("isel1", [128, 128])
    isel2 = sb("isel2", [128, 128])
    idxf = sb("idxf", [128, 128])
    idxu2 = [sb("idxu%d" % i, [128, 128], U32) for i in range(2)]
    gts2 = [sb("gts%d" % i, [128, 128]) for i in range(2)]
    a_sb = sb("a_sb", [128, 128])
    w_sb = sb("w_sb", [128, 128])
    gsum = sb("gsum", [128, 16])

    psum = []
    for i in range(8):
        t = es.enter_context(nc.psum_tensor("ps%d" % i, [128, 512], F32))
        psum.append(Tile("ps%d" % i, t[:]))
        psum[-1].excl = True
    ps_mm = psum[0:4]
    ps_s = psum[4:6]
    ps_o = psum[6:8]
    rr = {"mm": 0, "s": 0, "o": 0, "p": 0, "w": 0, "bt": 0}

    def nxt(key, lst):
        v = lst[rr[key] % len(lst)]
        rr[key] += 1
        return v

    def dma_load(dst, dst_ap, src_ap, reads=()):
        return S.op("sp", lambda e, a=dst_ap, b=src_ap: e.dma_start(out=a, in_=b),
                    reads=reads, writes=[dst], dma=True, chan=dst.chan)

    def bcast_rows(dram, row, ncols):
        return bass.AP(dram.tensor, dram.offset + row * ncols, [[0, 128], [1, ncols]])

    def mm(ps, out_ap, pairs, reads):
        n = len(pairs)
        last = None
        for k, (l, r) in enumerate(pairs):
            last = S.op("pe", lambda e, o=out_ap, l=l, r=r, st=(k == 0), sp=(k == n - 1):
                        e.matmul(o, l, r, start=st, stop=sp),
                        reads=reads, writes=[ps])
        return last

    def transpose(ps, out_ap, in_ap, reads):
        return S.op("pe", lambda e, o=out_ap, i=in_ap: e.transpose(o, i, ident.ap),
                    reads=list(reads) + [ident], writes=[ps])

    def act(out_t, out_ap, in_t, in_ap, func, extra_reads=(), **kw):
        return S.op("act", lambda e, o=out_ap, i=in_ap: e.activation(o, i, func, **kw),
                    reads=[in_t] + list(extra_reads), writes=[out_t])

    def dve(fn, reads, writes):
        return S.op("dve", fn, reads=reads, writes=writes)

    wseq = []
    wstate = {"issued": 0, "used": 0}

    def w_issue_upto(k):
        while wstate["issued"] < min(k, len(wseq)):
            i = wstate["issued"]
            slot = wring[i % NW]
            src = wseq[i]
            n = src.shape[-1]
            S.op(WQ, lambda e, a=slot.ap[:, 0:n], b=src: e.dma_start(out=a, in_=b),
                 reads=[], writes=[slot], dma=True, chan=slot.chan)
            wstate["issued"] += 1

    def w_next(expect):
        i = wstate["used"]
        if STOP:
            wseq.append(expect)
        assert wseq[i].offset == expect.offset and wseq[i].tensor.name == expect.tensor.name, (i,)
        w_issue_upto(i + NW)
        wstate["used"] += 1
        return wring[i % NW]

    def seq_kv():
        return [d_win[t] for t in (8, 9, 10, 11, 20, 21, 22, 23, 16, 17, 18, 19)]

    def seq_main():
        l = [d_win[t] for t in list(range(0, 8)) + [12, 13, 14, 15, 24, 25, 26, 27]]
        for ec in range(16):
            l += [d_win[28 + b * 16 + ec] for b in range(3)]
            l += [d_wpa[ec], d_wpb[ec], d_wpm[ec]]
        l += [d_wout[ct] for ct in range(16)]
        l += [d_wpq[ct] for ct in range(16)]
        return l

    for s in range(nseg):
        wseq += [d_wmkv[t] for t in range(8)]
        for j in range(min(LOOK + 2, nbs[s] + 4)):
            wseq += seq_kv()
        for i in range(nbs[s]):
            if i + 2 + LOOK < nbs[s] + 4:
                wseq += seq_kv()
            wseq += seq_main()

    if STOP:
        wseq.clear()
    dma_load(ident, ident.ap, d_ident)
    dma_load(identS, identS.ap, d_identS)
    dma_load(iota16, iota16.ap, d_iota)
    dma_load(keysT, keysT.ap, d_keysT)
    dma_load(bgate, bgate.ap, d_bg)
    dma_load(sink, sink.ap, bcast_rows(d_sink, 0, 8))
    act(sinke, sinke.ap, sink, sink.ap, ACT.Exp)
    dve(lambda e: e.memset(c4.ap, 4), [], [c4])
    dve(lambda e: e.memset(c15.ap, 15), [], [c15])
    for r in range(KVR):
        dve(lambda e, t=va[r]: e.memset(t.ap, 1.0), [], [va[r]])
        dve(lambda e, t=vb[r]: e.memset(t.ap, 1.0), [], [vb[r]])
    dve(lambda e: e.memset(vm.ap, 1.0), [], [vm])

    def rope(nh, rtab):
        x1 = sb_ap(tok_tmp.ap, 0, [[128, nh], [1, 16]])
        x2 = sb_ap(tok_tmp.ap, 16, [[128, nh], [1, 16]])
        cs = sb_ap(rtab.ap, 0, [[16, nh], [1, 16]])
        sn = sb_ap(rtab.ap, 128, [[16, nh], [1, 16]])
        t = [sb_ap(rtmp.ap, k * 128, [[16, nh], [1, 16]]) for k in range(4)]
        dve(lambda e: e.tensor_tensor(t[0], x1, cs, ALU.mult), [tok_tmp, rtab], [rtmp])
        dve(lambda e: e.tensor_tensor(t[1], x2, sn, ALU.mult), [tok_tmp, rtab], [rtmp])
        dve(lambda e: e.tensor_tensor(t[2], x2, cs, ALU.mult), [tok_tmp, rtab], [rtmp])
        dve(lambda e: e.tensor_tensor(t[3], x1, sn, ALU.mult), [tok_tmp, rtab], [rtmp])
        dve(lambda e: e.tensor_tensor(x1, t[0], t[1], ALU.subtract), [rtmp], [tok_tmp])
        dve(lambda e: e.tensor_tensor(x2, t[2], t[3], ALU.add), [rtmp], [tok_tmp])

    def tokmajor(xT_t, wtiles_dram, ps):
        for q, wd in enumerate(wtiles_dram):
            wt = w_next(wd)
            pairs = [(xT_t.ap[:, kc * 128:(kc + 1) * 128], wt.ap[:, kc * 128:(kc + 1) * 128])
                     for kc in range(KC)]
            mm(ps, ps.ap[:, q * 128:(q + 1) * 128], pairs, [xT_t, wt])

    def featmajor(xT_t, wtiles_dram, ps, kcs=KC, xoff=0):
        for q, wd in enumerate(wtiles_dram):
            wt = w_next(wd)
            pairs = [(wt.ap[:, kc * 128:(kc + 1) * 128],
                      xT_t.ap[:, (xoff + kc) * 128:(xoff + kc + 1) * 128]) for kc in range(kcs)]
            mm(ps, ps.ap[:, q * 128:(q + 1) * 128], pairs, [xT_t, wt])

    def kv_stage(s, j):
        if STOP in (10, 20):
            return
        slot = j % KVR
        dma_load(xTkv, xTkv.ap, d_xT[s][j])
        dma_load(ropekv, ropekv.ap, d_rope[s][j])
        ps = nxt("mm", ps_mm)
        tokmajor(xTkv, [d_win[8], d_win[9], d_win[10], d_win[11]], ps)
        act(tok_tmp, tok_tmp.ap[:, 0:256], ps, ps.ap[:, 0:256], ACT.Copy)
        dve(lambda e, o=sb_ap(va[slot].ap, 0, [[129, 2], [1, 128]]),
            i=sb_ap(ps.ap, 256, [[128, 2], [1, 128]]): e.tensor_copy(o, i), [ps], [va[slot]])
        if STOP == 21:
            return
        rope(2, ropekv)
        ps2 = nxt("mm", ps_mm)
        for h in range(2):
            transpose(ps2, ps2.ap[:, h * 128:(h + 1) * 128], tok_tmp.ap[:, h * 128:(h + 1) * 128],
                      [tok_tmp])
        act(kTa[slot], kTa[slot].ap, ps2, ps2.ap[:, 0:256], ACT.Copy)
        if STOP == 22:
            return
        ps = nxt("mm", ps_mm)
        tokmajor(xTkv, [d_win[20 + q] for q in range(4)], ps)
        dve(lambda e, o=sb_ap(vb[slot].ap, 0, [[129, 4], [1, 128]]),
            i=sb_ap(ps.ap, 0, [[128, 4], [1, 128]]): e.tensor_copy(o, i), [ps], [vb[slot]])
        if STOP == 23:
            return
        ps = nxt("mm", ps_mm)
        featmajor(xTkv, [d_win[16 + q] for q in range(4)], ps)
        act(kTb[slot], kTb[slot].ap, ps, ps.ap, ACT.Copy)

    def attn_all(heads):
        chunks = []
        for hd in heads:
            nk = len(hd["keys"])
            done = 0
            first = True
            while done < nk:
                n = min(3, nk - done)
                chunks.append((hd, done, n, first, done + n == nk))
                first = False
                done += n
        state = {}

        def emit_S(ck):
            hd, j0, n, first, last = ck
            if first and hd.get("pre") is not None:
                hd["pre"]()
            pss = nxt("s", ps_s)
            for jj in range(n):
                kT_ap, kt_t, _, _ = hd["keys"][j0 + jj]
                o = pss.ap[:, jj * 128:(jj + 1) * 128]
                if hd["mask"] is None:
                    mm(pss, o, [(kT_ap, hd["q_ap"])], [kt_t, QT])
                else:
                    m_ap, m_t = hd["mask"](j0 + jj)
                    S.op("pe", lambda e, o=o, l=kT_ap, r=hd["q_ap"]: e.matmul(o, l, r, start=True, stop=False),
                         reads=[kt_t, QT], writes=[pss])
                    S.op("pe", lambda e, o=o, r=m_ap: e.matmul(o, identS.ap, r, start=False, stop=True),
                         reads=[identS, m_t], writes=[pss])
            p = nxt("p", p_sb)
            act(p, p.ap[:, 0:n * 128], pss, pss.ap[:, 0:n * 128], ACT.Exp, scale=SCALE)
            state[id(ck)] = p

        def emit_PV(ck):
            hd, j0, n, first, last = ck
            p = state.pop(id(ck))
            if first:
                hd["po"] = nxt("o", ps_o)
            po = hd["po"]
            for jj in range(n):
                _, _, v_ap, v_t = hd["keys"][j0 + jj]
                S.op("pe", lambda e, o=po.ap[:, 0:129], l=p.ap[:, jj * 128:(jj + 1) * 128], r=v_ap,
                     st=(first and jj == 0), sp=(last and jj == n - 1): e.matmul(o, l, r, start=st, stop=sp),
                     reads=[p, v_t], writes=[po])
            if last:
                c = hd["col"]
                act(o_tok, o_tok.ap[:, c * 128:(c + 1) * 128], po, po.ap[:, 0:128], ACT.Copy)
                act(dens, dens.ap[:, c:c + 1], po, po.ap[:, 128:129], ACT.Copy)

        emit_S(chunks[0])
        for k, ck in enumerate(chunks):
            if k + 1 < len(chunks):
                emit_S(chunks[k + 1])
            emit_PV(ck)
            if ck[4]:
                yield

    def attn_finish():
        dve(lambda e: e.tensor_tensor(dens.ap[:, 0:8], dens.ap[:, 0:8], sinke.ap, ALU.add), [dens, sinke], [dens])
        dve(lambda e: e.reciprocal(dens.ap[:, 16:32], dens.ap[:, 0:16]), [dens], [dens])
        dve(lambda e: e.tensor_tensor(sb_ap(o_tok.ap, 0, [[128, 16], [1, 128]]),
                                      sb_ap(o_tok.ap, 0, [[128, 16], [1, 128]]),
                                      sb_ap(dens.ap, 16, [[1, 16], [0, 128]]), ALU.mult), [o_tok, dens], [o_tok])

    def layernorm(xt, grow, brow, lnA, lnB, stats):
        dma_load(lnA, lnA.ap, bcast_rows(d_ln, grow, D))
        dma_load(lnB, lnB.ap, bcast_rows(d_ln, brow, D))
        for q in range(4):
            dve(lambda e, q=q: e.bn_stats(stats.ap[:, q * 6:(q + 1) * 6], xt.ap[:, q * 512:(q + 1) * 512]),
                [xt], [stats])
        dve(lambda e: e.bn_aggr(stats.ap[:, 24:26], stats.ap[:, 0:24]), [stats], [stats])
        dve(lambda e: e.tensor_scalar(stats.ap[:, 26:27], stats.ap[:, 25:26], LN_EPS, None, ALU.add),
            [stats], [stats])
        act(stats, stats.ap[:, 27:28], stats, stats.ap[:, 26:27], ACT.Sqrt)
        dve(lambda e: e.reciprocal(stats.ap[:, 28:29], stats.ap[:, 27:28]), [stats], [stats])
        dve(lambda e: e.tensor_scalar(xt.ap, xt.ap, stats.ap[:, 24:25], stats.ap[:, 28:29],
                                      ALU.subtract, ALU.mult), [xt, stats], [xt])
        dve(lambda e: e.tensor_tensor(xt.ap, xt.ap, lnA.ap, ALU.mult), [xt, lnA], [xt])
        dve(lambda e: e.tensor_tensor(xt.ap, xt.ap, lnB.ap, ALU.add), [xt, lnB], [xt])

    def top16(src_t, src_ap, n, scratch_ap, outv_t, outv_ap, outi_t, outi_ap):
        dve(lambda e: e.max(outv_ap[:, 0:8], src_ap), [src_t], [outv_t])
        dve(lambda e: e.max_index(outi_ap[:, 0:8], outv_ap[:, 0:8], src_ap), [src_t, outv_t], [outi_t])
        dve(lambda e: e.match_replace(scratch_ap, outv_ap[:, 0:8], src_ap, -3.0e38),
            [src_t, outv_t], [s2])
        dve(lambda e: e.max(outv_ap[:, 8:16], scratch_ap), [s2], [outv_t])
        dve(lambda e: e.max_index(outi_ap[:, 8:16], outv_ap[:, 8:16], scratch_ap), [s2, outv_t], [outi_t])

    def seg_prologue(s):
        nb = nbs[s]
        halves = (xTkv, xTq)
        dma_load(amask, amask.ap, d_amask[:, s * 1152:(s + 1) * 1152])
        for mb in range(2):
            src = bass.AP(d_memT.tensor, d_memT.offset + s * 128 * KC * 256 + mb * 128,
                          [[KC * 256, 128], [256, KC], [1, 128]])
            dma_load(halves[mb], sb_ap(halves[mb].ap, 0, [[128, KC], [1, 128]]), src)
        for t in range(8):
            wt = w_next(d_wmkv[t])
            if t < 4:
                ps = nxt("mm", ps_mm)
                for mb in range(2):
                    pairs = [(wt.ap[:, kc * 128:(kc + 1) * 128], halves[mb].ap[:, kc * 128:(kc + 1) * 128])
                             for kc in range(KC)]
                    mm(ps, ps.ap[:, mb * 128:(mb + 1) * 128], pairs, [halves[mb], wt])
                act(kTm, kTm.ap[:, t * 256:(t + 1) * 256], ps, ps.ap[:, 0:256], ACT.Copy)
            else:
                h = t - 4
                ps = nxt("mm", ps_mm)
                for mb in range(2):
                    pairs = [(halves[mb].ap[:, kc * 128:(kc + 1) * 128], wt.ap[:, kc * 128:(kc + 1) * 128])
                             for kc in range(KC)]
                    mm(ps, ps.ap[:, mb * 128:(mb + 1) * 128], pairs, [halves[mb], wt])
                for mb in range(2):
                    o = vm.ap[:, (mb * 4 + h) * 129:(mb * 4 + h) * 129 + 128]
                    act(vm, o, ps, ps.ap[:, mb * 128:(mb + 1) * 128], ACT.Copy)
            yield
        for j in range(min(LOOK + 2, nb + 4)):
            kv_stage(s, j)
            yield

    def phase1(s, i, gblk):
        nb = nbs[s]
        j = i + 2
        xt = xtok[gblk % 2]
        idxu = idxu2[gblk % 2]
        gts = gts2[gblk % 2]
        if i == 0:
            yield from seg_prologue(s)
        jn = i + 2 + LOOK
        if jn < nb + 4:
            kv_stage(s, jn)
            yield
        dma_load(xTq, xTq.ap, d_xT[s][j])
        dma_load(ropeq, ropeq.ap, d_rope[s][j])
        S.op("sp", lambda e, a=xt.ap, b=d_xtok[s][i]: e.dma_start(out=a, in_=b),
             reads=[], writes=[xt], dma=True, chan=xt.chan)
        for half in range(2):
            ps = nxt("mm", ps_mm)
            tokmajor(xTq, [d_win[half * 4 + q] for q in range(4)], ps)
            act(tok_tmp, tok_tmp.ap[:, half * 512:(half + 1) * 512], ps, ps.ap, ACT.Copy)
            yield
        rope(8, ropeq)
        for half in range(2):
            ps = nxt("mm", ps_mm)
            for q in range(4):
                h = half * 4 + q
                transpose(ps, ps.ap[:, q * 128:(q + 1) * 128], tok_tmp.ap[:, h * 128:(h + 1) * 128],
                          [tok_tmp])
            act(QT, QT.ap[:, half * 512:(half + 1) * 512], ps, ps.ap, ACT.Copy)
        yield
        for grp, t0 in ((2, 12), (3, 24)):
            ps = nxt("mm", ps_mm)
            featmajor(xTq, [d_win[t0 + q] for q in range(4)], ps)
            act(QT, QT.ap[:, grp * 512:(grp + 1) * 512], ps, ps.ap, ACT.Copy)
            yield
        aslot = 0 if i == 0 else (2 if i == nb - 1 else 1)
        am_off = aslot * 384
        heads = []
        for h in range(8):
            hk = h // 4
            kl = []
            for d in (-1, 0, 1):
                sl = (j + d) % KVR
                kl.append((kTa[sl].ap[:, hk * 128:(hk + 1) * 128], kTa[sl],
                           va[sl].ap[:, hk * 129:(hk + 1) * 129], va[sl]))
            heads.append(dict(q_ap=QT.ap[:, h * 128:(h + 1) * 128], keys=kl, col=h,
                              mask=lambda jj, o=am_off: (amask.ap[:, o + jj * 128:o + (jj + 1) * 128], amask)))
        if i == 0:
            bslot, dl = 0, list(range(-2, 4))
        elif i == nb - 1:
            bslot, dl = 4, list(range(-3, 3))
        else:
            bslot = 1 if i == 1 else (3 if i == nb - 2 else 2)
            dl = list(range(-2, 3))
        d0 = dl[0] + 3
        for h in range(4):
            kl = []
            for d in dl:
                sl = (j + d) % KVR
                kl.append((kTb[sl].ap[:, h * 128:(h + 1) * 128], kTb[sl],
                           vb[sl].ap[:, h * 129:(h + 1) * 129], vb[sl]))
            bo = (h % 2) * 1024
            heads.append(dict(q_ap=QT.ap[:, (8 + h) * 128:(9 + h) * 128], keys=kl, col=8 + h,
                              pre=lambda bo=bo, h=h: dma_load(B4, B4.ap[:, bo:bo + 896],
                                                              d_btab[s * 5 + bslot][:, h * 896:(h + 1) * 896]),
                              mask=lambda jj, bo=bo, d0=d0: (B4.ap[:, bo + (d0 + jj) * 128:bo + (d0 + jj + 1) * 128], B4)))
        for h in range(4):
            kl = []
            for mb in range(2):
                kl.append((kTm.ap[:, h * 256 + mb * 128:h * 256 + (mb + 1) * 128], kTm,
                           vm.ap[:, (mb * 4 + h) * 129:(mb * 4 + h + 1) * 129], vm))
            heads.append(dict(q_ap=QT.ap[:, (12 + h) * 128:(13 + h) * 128], keys=kl, col=12 + h, mask=None))
        yield from attn_all(heads)
        attn_finish()
        for grp in range(4):
            ps = nxt("mm", ps_mm)
            for q in range(4):
                c = grp * 4 + q
                transpose(ps, ps.ap[:, q * 128:(q + 1) * 128], o_tok.ap[:, c * 128:(c + 1) * 128], [o_tok])
            act(oT, oT.ap[:, grp * 512:(grp + 1) * 512], ps, ps.ap, ACT.Copy)
        yield
        for ec in range(16):
            psg = nxt("mm", ps_mm)
            featmajor(xTq, [d_win[28 + b * 16 + ec] for b in range(3)], psg)
            for b in range(3):
                act(g_sb, g_sb.ap[:, b * 128:(b + 1) * 128], psg, psg.ap[:, b * 128:(b + 1) * 128],
                    ACT.Sigmoid, extra_reads=[bgate], bias=bgate.ap[:, b * 16 + ec:b * 16 + ec + 1])
            psp = nxt("mm", ps_mm)
            for b, (wd, kcs, xoff) in enumerate(((d_wpa[ec], 8, 0), (d_wpb[ec], 4, 8), (d_wpm[ec], 4, 12))):
                wt = w_next(wd)
                pairs = [(wt.ap[:, kc * 128:(kc + 1) * 128], oT.ap[:, (xoff + kc) * 128:(xoff + kc + 1) * 128])
                         for kc in range(kcs)]
                mm(psp, psp.ap[:, b * 128:(b + 1) * 128], pairs, [oT, wt])
            dve(lambda e, pp=psp: e.tensor_tensor(gm_sb.ap, pp.ap[:, 0:384], g_sb.ap, ALU.mult),
                [psp, g_sb], [gm_sb])
            mo = mT.ap[:, ec * 128:(ec + 1) * 128]
            dve(lambda e, mo=mo: e.tensor_reduce(mo, sb_ap(gm_sb.ap, 0, [[1, 128], [128, 3]]), AX.X, ALU.add),
                [gm_sb], [mT])
            yield
        for grp in range(4):
            ps = nxt("mm", ps_mm)
            tokmajor(mT, [d_wout[grp * 4 + q] for q in range(4)], ps)
            xs = xt.ap[:, grp * 512:(grp + 1) * 512]
            dve(lambda e, xs=xs, ps=ps: e.scalar_tensor_tensor(xs, xs, ALPHA, ps.ap, ALU.mult, ALU.add),
                [xt, ps], [xt])
            yield
        layernorm(xt, 0, 1, B4, B1, stats)
        yield
        x1T = xTq
        for grp in range(4):
            ps = nxt("mm", ps_mm)
            for q in range(4):
                c = grp * 4 + q
                transpose(ps, ps.ap[:, q * 128:(q + 1) * 128], xt.ap[:, c * 128:(c + 1) * 128], [xt])
            act(x1T, x1T.ap[:, grp * 512:(grp + 1) * 512], ps, ps.ap, ACT.Copy)
        yield
        qTp = QT
        for grp in range(4):
            ps = nxt("mm", ps_mm)
            featmajor(x1T, [d_wpq[grp * 4 + q] for q in range(4)], ps)
            act(qTp, qTp.ap[:, grp * 512:(grp + 1) * 512], ps, ps.ap, ACT.Copy)
            yield
        s_sb = B4
        for grp in range(4):
            ps = nxt("mm", ps_mm)
            for q in range(4):
                g = grp * 4 + q
                side = g % 2
                mm(ps, ps.ap[:, q * 128:(q + 1) * 128],
                   [(qTp.ap[:, g * 128:(g + 1) * 128], keysT.ap[:, side * 128:(side + 1) * 128])],
                   [qTp, keysT])
            act(s_sb, s_sb.ap[:, grp * 512:(grp + 1) * 512], ps, ps.ap, ACT.Copy)
        yield
        for g in range(16):
            top16(s_sb, s_sb.ap[:, g * 128:(g + 1) * 128], 128, s2.ap[:, 0:128],
                  tv, tv.ap[:, g * 16:(g + 1) * 16], ti, ti.ap[:, g * 16:(g + 1) * 16])
            if g % 2 == 1:
                yield
        cand = B1
        dve(lambda e: e.tensor_tensor(sb_ap(cand.ap, 0, [[256, 8], [16, 16], [1, 16]]),
                                      sb_ap(tv.ap, 0, [[32, 8], [1, 16], [0, 16]]),
                                      sb_ap(tv.ap, 16, [[32, 8], [0, 16], [1, 16]]), ALU.add),
            [tv], [cand])
        for h in range(8):
            top16(cand, cand.ap[:, h * 256:(h + 1) * 256], 256, s2.ap[:, 0:256],
                  topv, topv.ap[:, h * 16:(h + 1) * 16], topi, topi.ap[:, h * 16:(h + 1) * 16])
            if h % 2 == 1:
                yield
        dve(lambda e: e.tensor_tensor(ai.ap, topi.ap, sb_ap(c4.ap, 0, [[0, 128]]), ALU.logical_shift_right),
            [topi, c4], [ai])
        dve(lambda e: e.tensor_copy(af.ap, ai.ap), [ai], [af])
        dve(lambda e: e.tensor_tensor(ai.ap, topi.ap, sb_ap(c15.ap, 0, [[0, 128]]), ALU.bitwise_and),
            [topi, c15], [ai])
        dve(lambda e: e.tensor_copy(bf.ap, ai.ap), [ai], [bf])
        dve(lambda e: e.tensor_copy(tif.ap, ti.ap), [ti], [tif])
        yield
        oh = B4
        for (pf, off, dst) in ((af, 0, isel1), (bf, 16, isel2)):
            dve(lambda e, pf=pf: e.tensor_tensor(sb_ap(oh.ap, 0, [[16, 128], [1, 16]]),
                                                 sb_ap(pf.ap, 0, [[1, 128], [0, 16]]),
                                                 sb_ap(iota16.ap, 0, [[0, 128], [1, 16]]), ALU.is_equal),
                [pf, iota16], [oh])
            dve(lambda e, off=off: e.tensor_tensor(sb_ap(oh.ap, 0, [[256, 8], [16, 16], [1, 16]]),
                                                   sb_ap(oh.ap, 0, [[256, 8], [16, 16], [1, 16]]),
                                                   sb_ap(tif.ap, off, [[32, 8], [0, 16], [1, 16]]), ALU.mult),
                [oh, tif], [oh])
            dve(lambda e, dst=dst: e.tensor_reduce(dst.ap, sb_ap(oh.ap, 0, [[16, 128], [1, 16]]),
                                                   AX.X, ALU.add), [oh], [dst])
            yield
        dve(lambda e: e.scalar_tensor_tensor(idxf.ap, isel1.ap, 128.0, isel2.ap, ALU.mult, ALU.add),
            [isel1, isel2], [idxf])
        dve(lambda e: e.tensor_copy(idxu.ap, idxf.ap), [idxf], [idxu])
        dve(lambda e: e.tensor_tensor(sb_ap(gts.ap, 0, [[16, 8], [1, 16]]),
                                      sb_ap(topv.ap, 0, [[16, 8], [1, 16]]),
                                      sb_ap(topv.ap, 0, [[16, 8], [0, 16]]), ALU.subtract), [topv], [gts])
        act(gts, gts.ap, gts, gts.ap, ACT.Exp)
        dve(lambda e: e.tensor_reduce(gsum.ap[:, 0:8], sb_ap(gts.ap, 0, [[16, 8], [1, 16]]), AX.X, ALU.add),
            [gts], [gsum])
        dve(lambda e: e.reciprocal(gsum.ap[:, 8:16], gsum.ap[:, 0:8]), [gsum], [gsum])
        dve(lambda e: e.tensor_tensor(sb_ap(gts.ap, 0, [[16, 8], [1, 16]]),
                                      sb_ap(gts.ap, 0, [[16, 8], [1, 16]]),
                                      sb_ap(gsum.ap, 8, [[1, 8], [0, 16]]), ALU.mult), [gts, gsum], [gts])
        yield

    def phase2(s, i, gblk):
        xt = xtok[gblk % 2]
        idxu = idxu2[gblk % 2]
        gts = gts2[gblk % 2]

        def gather(buf, table, col):
            return S.op("pool", lambda e, b=buf, c=col: e.indirect_dma_start(
                out=b.ap, out_offset=None, in_=table,
                in_offset=bass.IndirectOffsetOnAxis(ap=idxu.ap[:, c:c + 1], axis=0)),
                reads=[idxu], writes=[buf], dma=True, chan=gchan[id(buf)])

        for c in range(128):
            ub = gbuf[c % NG]
            gather(ub, d_pu, c)
            dve(lambda e, ub=ub, c=c: e.scalar_tensor_tensor(
                ub.ap, ub.ap, 1.0, xt.ap, ALU.mult, ALU.mult, accum_out=a_sb.ap[:, c:c + 1]),
                [ub, xt], [ub, a_sb])
            yield
        act(w_sb, w_sb.ap, a_sb, a_sb.ap, ACT.Gelu)
        dve(lambda e: e.tensor_tensor(w_sb.ap, w_sb.ap, gts.ap, ALU.mult), [w_sb, gts], [w_sb])
        dve(lambda e: e.tensor_scalar(xt.ap, xt.ap, ALPHA, None, ALU.mult), [xt], [xt])
        for c in range(128):
            vbf = gbuf[(128 + c) % NG]
            gather(vbf, d_pv, c)
            dve(lambda e, vbf=vbf, c=c: e.scalar_tensor_tensor(
                xt.ap, vbf.ap, w_sb.ap[:, c:c + 1], xt.ap, ALU.mult, ALU.add),
                [vbf, w_sb, xt], [xt])
            yield
        layernorm(xt, 2, 3, gbuf[0], gbuf[1], stats2)
        st = S.op("sp", lambda e, a=d_y[s][i], b=xt.ap: e.dma_start(out=a, in_=b),
                  reads=[xt], writes=[], dma=True, chan=xtok_st[gblk % 2])
        S.store_ops.append(st)
        yield

    blocks = [(s, i) for s in range(nseg) for i in range(nbs[s])]
    for _ in phase1(blocks[0][0], blocks[0][1], 0):
        pass
    for g, (s, i) in enumerate(blocks):
        p2 = phase2(s, i, g)
        p1 = phase1(blocks[g + 1][0], blocks[g + 1][1], g + 1) if g + 1 < len(blocks) else None
        alive2, alive1 = True, p1 is not None
        while alive2 or alive1:
            for _ in range(P2_STEPS):
                if alive2 and next(p2, "done") == "done":
                    alive2 = False
            if alive1 and next(p1, "done") == "done":
                alive1 = False
    assert wstate["used"] == len(wseq), (wstate, len(wseq))

    S.emit(nc, es)
    es.close()
    return nc


def _tile_w(W, kcn):
    K, N = W.shape
    assert K == kcn * 128
    return np.ascontiguousarray(W.reshape(kcn, 128, N // 128, 128).transpose(2, 1, 0, 3)).reshape(
        N // 128, 128, kcn * 128)


def _btab(rpb, gi, nbt):
    rows = 2 * nbt
    wr = min(8, rows)
    k = np.arange(128)
    q = np.arange(128)
    kr2, kc = k // 64, k % 64
    qr2, qc = q // 64, q % 64
    cs = np.clip(qc - 8, 0, 64 - 16)
    col_ok = (kc[:, None] >= cs[None, :]) & (kc[:, None] < cs[None, :] + 16)
    dc = np.clip(kc[:, None] - qc[None, :], -15, 15) + 15
    out = np.full((128, 4, 7, 128), NEG, np.float32)
    for d in range(-3, 4):
        gk = gi + d
        krg = 2 * gk + kr2
        r = 2 * gi + qr2
        rs = np.clip(r - wr // 2, 0, rows - wr)
        ok = (krg[:, None] >= 0) & (krg[:, None] < rows) & (krg[:, None] >= rs[None, :]) \
            & (krg[:, None] < rs[None, :] + wr) & col_ok
        dr = np.clip(krg[:, None] - r[None, :] + 7, 0, 14)
        vals = rpb[:, dr, dc]
        out[:, :, d + 3, :] = np.where(ok[:, None, :], vals.transpose(1, 0, 2), np.float32(NEG))
    return out.reshape(128, 4 * 7 * 128)


def _amask(gi, nbt):
    k = np.arange(128)[:, None]
    q = np.arange(128)[None, :]
    out = np.zeros((128, 3, 128), np.float32)
    for jj, d in enumerate((-1, 0, 1)):
        ok = (np.abs(d * 128 + k - q) <= 128) & (0 <= gi + d < nbt)
        out[:, jj, :] = np.where(ok, np.float32(0.0), np.float32(NEG))
    return out.reshape(128, 384)


def _rope_tab(pos):
    half = 16
    inv = (np.float32(ROPE_THETA) ** (-np.arange(half, dtype=np.float32) * np.float32(2.0) / np.float32(32))).astype(np.float32)
    ang = pos.astype(np.float32)[:, None] * inv[None, :]
    c = np.cos(ang).astype(np.float32)
    s = np.sin(ang).astype(np.float32)
    return np.concatenate([np.tile(c, (1, 8)), np.tile(s, (1, 8))], axis=1)


def prep_shared(inp):
    w_in = np.asarray(inp["w_in"][0], np.float32)
    sh = {
        "win_t": _tile_w(w_in, 16),
        "wpa_t": _tile_w(np.asarray(inp["w_proj_a"][0], np.float32), 8),
        "wpb_t": _tile_w(np.asarray(inp["w_proj_b"][0], np.float32), 4),
        "wpm_t": _tile_w(np.asarray(inp["w_proj_m"][0], np.float32), 4),
        "wout_t": _tile_w(np.asarray(inp["w_out"][0], np.float32), 16),
        "wpq_t": _tile_w(np.asarray(inp["w_peer_q"][0], np.float32), 16),
        "wmkv_t": _tile_w(np.asarray(inp["w_mem_kv"][0], np.float32), 16),
        "bgate": np.ascontiguousarray(np.asarray(inp["b_gate"][0], np.float32).reshape(48, 128).T),
        "sink": np.asarray(inp["a_sink"], np.float32).reshape(1, 8),
        "lnp": np.stack([np.asarray(inp[k][0], np.float32) for k in ("ln1_g", "ln1_b", "ln2_g", "ln2_b")]),
        "ident": np.eye(128, dtype=np.float32),
        "identS": (np.eye(128, dtype=np.float32) * np.float32(1.0 / SCALE)).astype(np.float32),
        "iota16": np.tile(np.arange(16, dtype=np.float32)[None, :], (128, 1)),
        "keysT": np.ascontiguousarray(np.concatenate(
            [np.asarray(inp["peer_keys1"][0], np.float32).T, np.asarray(inp["peer_keys2"][0], np.float32).T], axis=1)),
        "peer_u": np.asarray(inp["peer_u"][0], np.float32),
        "peer_v": np.asarray(inp["peer_v"][0], np.float32),
    }
    return sh


def prep_segment(xseq, tok0, ntok, mem, rpb):
    T = xseq.shape[0]
    nbt = T // 128
    nb = ntok // 128
    g0 = tok0 // 128
    xpad = np.zeros(((nb + 4) * 128, D), np.float32)
    lo, hi = tok0 - 256, tok0 + ntok + 256
    slo, shi = max(lo, 0), min(hi, T)
    xpad[slo - lo:shi - lo] = xseq[slo:shi]
    xT = np.ascontiguousarray(xpad.reshape(nb + 4, 128, KC, 128).transpose(0, 3, 2, 1)).reshape(nb + 4, 128, D)
    xtok = np.ascontiguousarray(xseq[tok0:tok0 + ntok].reshape(nb, 128, D))
    pos = np.arange(lo, hi)
    rope = _rope_tab(np.maximum(pos, 0)).reshape(nb + 4, 128, 256)
    memT = np.ascontiguousarray(mem.reshape(256, KC, 128).transpose(2, 1, 0)).reshape(128, KC * 256)
    bslots = [g0, g0 + 1, g0 + 2, g0 + nb - 2, g0 + nb - 1]
    btab = np.stack([_btab(rpb, gi, nbt) for gi in bslots])
    aslots = [g0, g0 + 1, g0 + nb - 1]
    am = np.concatenate([_amask(gi, nbt) for gi in aslots], axis=1)
    return xT, xtok, rope, memT, btab, am


def make_in_map(shared, segs):
    m = dict(shared)
    for s, (xT, xtok, rope, memT, btab, am) in enumerate(segs):
        m["xT%d" % s] = xT
        m["xtok%d" % s] = xtok
        m["rope%d" % s] = rope
    m["memT"] = np.stack([sg[3] for sg in segs])
    m["btab"] = np.concatenate([sg[4] for sg in segs], axis=0)
    m["amask"] = np.ascontiguousarray(np.concatenate([sg[5] for sg in segs], axis=1))
    return m


_NC_CACHE = {}


def kernel(x_prompt, x_sample, mem_prompt, mem_sample, w_in, b_gate, a_sink, na_rpb, w_mem_kv,
           w_proj_a, w_proj_b, w_proj_m, w_out, ln1_g, ln1_b, w_peer_q, peer_keys1, peer_keys2,
           peer_u, peer_v, ln2_g, ln2_b):
    inp = dict(w_in=w_in, b_gate=b_gate, a_sink=a_sink, w_mem_kv=w_mem_kv, w_proj_a=w_proj_a,
               w_proj_b=w_proj_b, w_proj_m=w_proj_m, w_out=w_out, ln1_g=ln1_g, ln1_b=ln1_b,
               w_peer_q=w_peer_q, peer_keys1=peer_keys1, peer_keys2=peer_keys2, peer_u=peer_u,
               peer_v=peer_v, ln2_g=ln2_g, ln2_b=ln2_b)
    n = 8
    shared = prep_shared(inp)
    rpb = np.asarray(na_rpb[0], np.float32)
    xs = np.asarray(x_sample, np.float32)
    xp = np.asarray(x_prompt, np.float32)[0]
    ms = np.asarray(mem_sample, np.float32)
    mp = np.asarray(mem_prompt, np.float32)[0]
    TP = xp.shape[0] // n
    in_maps = []
    for c in range(n):
        seg0 = prep_segment(xs[c], 0, xs.shape[1], ms[c], rpb)
        seg1 = prep_segment(xp, c * TP, TP, mp, rpb)
        in_maps.append(make_in_map(shared, [seg0, seg1]))
    nbs = (xs.shape[1] // 128, TP // 128)
    nc = build_program(list(nbs))
    res = run_bass_kernel_spmd(nc, in_maps, core_ids=list(range(n)))
    y_s = np.stack([np.asarray(res.results[c]["y0"]).reshape(-1, D) for c in range(n)]).astype(np.float32)
    y_p = np.concatenate([np.asarray(res.results[c]["y1"]).reshape(-1, D) for c in range(n)], axis=0)[None]
    return (y_p.astype(np.float32), y_s)
```

```python
import math
from contextlib import ExitStack

import numpy as np
import concourse.bass as bass
import concourse.mybir as mybir
from concourse.bass_utils import run_bass_kernel_spmd

F32 = mybir.dt.float32
U32 = mybir.dt.uint32
ALU = mybir.AluOpType
ACT = mybir.ActivationFunctionType
AX = mybir.AxisListType

D = 2048
KC = 16
HD = 128
IN_W = 9728
N_EXP = 16384
ALPHA = 2.0 ** 0.25
LN_EPS = 1e-5
SCALE = HD ** -0.5
NEG = -30000.0
ROPE_THETA = 500000.0

NW = 5
NG = 6
KVR = 6
LOOK = 3
STOP = 0
WQ = "pool"
P2_STEPS = 5


class Tile:
    def __init__(self, name, ap=None):
        self.name = name
        self.ap = ap
        self.writer = None
        self.readers = []
        self.chan = None
        self.excl = False


class Op:
    __slots__ = ("eng", "fn", "deps", "dma", "chan", "needed", "val", "ninst")

    def __init__(self, eng, fn, dma=False, chan=None):
        self.eng = eng
        self.fn = fn
        self.deps = []
        self.dma = dma
        self.chan = chan
        self.needed = False
        self.val = None
        self.ninst = 1


class Sched:
    ENGS = ("pe", "act", "dve", "pool", "sp")

    def __init__(self):
        self.prog = {e: [] for e in self.ENGS}
        self.all_ops = []
        self.nchan = 0
        self.store_ops = []

    def new_chan(self):
        self.nchan += 1
        return self.nchan - 1

    def op(self, eng, fn, reads=(), writes=(), dma=False, chan=None, ninst=1):
        o = Op(eng, fn, dma, chan)
        o.ninst = ninst
        deps = set()
        for t in reads:
            if t.writer is not None:
                deps.add(t.writer)
            if t.excl:
                for r in t.readers:
                    if r.eng != eng:
                        deps.add(r)
        for t in writes:
            if t.writer is not None:
                deps.add(t.writer)
            for r in t.readers:
                deps.add(r)
        deps.discard(o)
        o.deps = list(deps)
        for t in reads:
            t.readers.append(o)
        for t in writes:
            t.writer = o
            t.readers = []
        self.prog[eng].append(o)
        self.all_ops.append(o)
        return o

    def emit(self, nc, es):
        for e in self.ENGS:
            for o in self.prog[e]:
                for d in o.deps:
                    if d.eng == "pe" and o.eng == "pe" and not d.dma:
                        continue
                    d.needed = True
        esem = {e: es.enter_context(nc.semaphore("sem_" + e)) for e in ("pe", "act", "dve", "pool")}
        csem = [es.enter_context(nc.semaphore("ch%d" % i)) for i in range(self.nchan)]
        cnt = {e: 0 for e in esem}
        ccnt = [0] * self.nchan
        for o in self.all_ops:
            if o.dma:
                ccnt[o.chan] += 16 * o.ninst
                o.val = (csem[o.chan], ccnt[o.chan])
        for e in ("pe", "act", "dve", "pool"):
            for o in self.prog[e]:
                if not o.dma and o.needed:
                    cnt[e] += 1
                    o.val = (esem[e], cnt[e])
        final = [(csem[o.chan], o.val[1]) for o in self.store_ops]
        engobj = {"pe": "tensor", "act": "scalar", "dve": "vector", "pool": "gpsimd", "sp": "sync"}

        with nc.Block() as block:
            def body(ename):
                def run(eng):
                    waited = {}
                    for o in self.prog[ename]:
                        need = {}
                        for d in o.deps:
                            if d.eng == "pe" and ename == "pe" and not d.dma:
                                continue
                            s, v = d.val
                            k = id(s)
                            if k not in need or need[k][1] < v:
                                need[k] = (s, v)
                        for k, (s, v) in need.items():
                            if waited.get(k, 0) >= v:
                                continue
                            eng.wait_ge(s, v)
                            waited[k] = v
                        insts = o.fn(eng)
                        if o.dma:
                            if not isinstance(insts, (list, tuple)):
                                insts = [insts]
                            assert len(insts) == o.ninst
                            for ins in insts:
                                ins.then_inc(o.val[0], 16)
                        elif o.needed:
                            insts.then_inc(o.val[0], 1)
                    if ename == "sp":
                        for s, v in final:
                            eng.wait_ge(s, v)
                return run

            block.tensor(body("pe"))
            block.scalar(body("act"))
            block.vector(body("dve"))
            block.gpsimd(body("pool"))
            block.sync(body("sp"))


def sb_ap(t, offset, dims):
    full = t.ap
    return bass.AP(t.tensor, t.offset + offset, [[full[0][0], full[0][1]]] + [list(d) for d in dims])


def build_program(nbs):
    nc = bass.Bass("TRN2", target_bir_lowering=False)
    S = Sched()
    es = ExitStack()
    nseg = len(nbs)

    def din(name, shape, dt=F32):
        return nc.dram_tensor(name, list(shape), dt, kind="ExternalInput").ap()

    d_xT = [din("xT%d" % s, [nbs[s] + 4, 128, D]) for s in range(nseg)]
    d_xtok = [din("xtok%d" % s, [nbs[s], 128, D]) for s in range(nseg)]
    d_rope = [din("rope%d" % s, [nbs[s] + 4, 128, 256]) for s in range(nseg)]
    d_memT = din("memT", [nseg, 128, KC * 256])
    d_btab = din("btab", [nseg * 5, 128, 4 * 7 * 128])
    d_amask = din("amask", [128, nseg * 3 * 384])
    d_win = din("win_t", [76, 128, D])
    d_wpa = din("wpa_t", [16, 128, 1024])
    d_wpb = din("wpb_t", [16, 128, 512])
    d_wpm = din("wpm_t", [16, 128, 512])
    d_wout = din("wout_t", [16, 128, D])
    d_wpq = din("wpq_t", [16, 128, D])
    d_wmkv = din("wmkv_t", [8, 128, D])
    d_bg = din("bgate", [128, 48])
    d_sink = din("sink", [1, 8])
    d_ln = din("lnp", [4, D])
    d_ident = din("ident", [128, 128])
    d_identS = din("identS", [128, 128])
    d_iota = din("iota16", [128, 16])
    d_keysT = din("keysT", [128, 256])
    d_pu = din("peer_u", [N_EXP if STOP < 10 else 128, D])
    d_pv = din("peer_v", [N_EXP if STOP < 10 else 128, D])
    d_y = [nc.dram_tensor("y%d" % s, [nbs[s], 128, D], F32, kind="ExternalOutput").ap()
           for s in range(nseg)]

    def sb(name, shape, dt=F32, dma=False):
        t = es.enter_context(nc.sbuf_tensor("sb_" + name, list(shape), dt))
        tl = Tile(name, t[:])
        if dma:
            tl.chan = S.new_chan()
        return tl

    B1 = sb("B1", [128, D], dma=True)
    B2 = sb("B2", [128, D], dma=True)
    B3 = sb("B3", [128, D], dma=True)
    B4 = sb("B4", [128, D], dma=True)
    xTkv, xTq, QT, oT, mT = B1, B2, B3, B3, B1
    gbuf = [sb("gbuf%d" % i, [128, D], dma=True) for i in range(NG)]
    gchan = {id(t): S.new_chan() for t in gbuf}
    xtok = [sb("xtok%d" % i, [128, D], dma=True) for i in range(2)]
    xtok_st = [S.new_chan() for _ in range(2)]
    wring = [sb("w%d" % i, [128, D], dma=True) for i in range(NW)]
    kTa = [sb("kTa%d" % i, [128, 2 * 128]) for i in range(KVR)]
    va = [sb("va%d" % i, [128, 2 * 129]) for i in range(KVR)]
    kTb = [sb("kTb%d" % i, [128, 4 * 128]) for i in range(KVR)]
    vb = [sb("vb%d" % i, [128, 4 * 129]) for i in range(KVR)]
    kTm = sb("kTm", [128, 4 * 256])
    vm = sb("vm", [128, 2 * 4 * 129])
    identS = sb("identS", [128, 128], dma=True)
    dens = sb("dens", [128, 32])
    amask = sb("amask", [128, 3 * 384], dma=True)
    ropekv = sb("ropekv", [128, 256], dma=True)
    ropeq = sb("ropeq", [128, 256], dma=True)
    ident = sb("ident", [128, 128], dma=True)
    iota16 = sb("iota16", [128, 16], dma=True)
    keysT = sb("keysT", [128, 256], dma=True)
    bgate = sb("bgate", [128, 48], dma=True)
    sink = sb("sink", [128, 8], dma=True)
    sinke = sb("sinke", [128, 8])
    tok_tmp = B4
    o_tok = B1
    p_sb = [sb("p_sb%d" % i, [128, 512]) for i in range(2)]
    rtmp = p_sb[0]
    g_sb = sb("g_sb", [128, 384])
    gm_sb = sb("gm_sb", [128, 384])
    small = sb("small", [128, 64])
    stats = sb("stats", [128, 32])
    stats2 = sb("stats2", [128, 32])
    tv = sb("tv", [128, 256])
    ti = sb("ti", [128, 256], U32)
    tif = sb("tif", [128, 256])
    s2 = gm_sb
    topv = sb("topv", [128, 128])
    topi = sb("topi", [128, 128], U32)
    ai = sb("ai", [128, 128], U32)
    af = sb("af", [128, 128])
    bf = sb("bf", [128, 128])
    c4 = sb("c4", [128, 2], U32)
    c15 = sb("c15", [128, 2], U32)
    isel1 = sb("isel1", [128, 128])
    isel2 = sb("isel2", [128, 128])
    idxf = sb("idxf", [128, 128])
    idxu2 = [sb("idxu%d" % i, [128, 128], U32) for i in range(2)]
    gts2 = [sb("gts%d" % i, [128, 128]) for i in range(2)]
    a_sb = sb("a_sb", [128, 128])
    w_sb = sb("w_sb", [128, 128])
    gsum = sb("gsum", [128, 16])

    psum = []
    for i in range(8):
        t = es.enter_context(nc.psum_tensor("ps%d" % i, [128, 512], F32))
        psum.append(Tile("ps%d" % i, t[:]))
        psum[-1].excl = True
    ps_mm = psum[0:4]
    ps_s = psum[4:6]
    ps_o = psum[6:8]
    rr = {"mm": 0, "s": 0, "o": 0, "p": 0, "w": 0, "bt": 0}

    def nxt(key, lst):
        v = lst[rr[key] % len(lst)]
        rr[key] += 1
        return v

    def dma_load(dst, dst_ap, src_ap, reads=()):
        return S.op("sp", lambda e, a=dst_ap, b=src_ap: e.dma_start(out=a, in_=b),
                    reads=reads, writes=[dst], dma=True, chan=dst.chan)

    def bcast_rows(dram, row, ncols):
        return bass.AP(dram.tensor, dram.offset + row * ncols, [[0, 128], [1, ncols]])

    def mm(ps, out_ap, pairs, reads):
        n = len(pairs)
        last = None
        for k, (l, r) in enumerate(pairs):
            last = S.op("pe", lambda e, o=out_ap, l=l, r=r, st=(k == 0), sp=(k == n - 1):
                        e.matmul(o, l, r, start=st, stop=sp),
                        reads=reads, writes=[ps])
        return last

    def transpose(ps, out_ap, in_ap, reads):
        return S.op("pe", lambda e, o=out_ap, i=in_ap: e.transpose(o, i, ident.ap),
                    reads=list(reads) + [ident], writes=[ps])

    def act(out_t, out_ap, in_t, in_ap, func, extra_reads=(), **kw):
        return S.op("act", lambda e, o=out_ap, i=in_ap: e.activation(o, i, func, **kw),
                    reads=[in_t] + list(extra_reads), writes=[out_t])

    def dve(fn, reads, writes):
        return S.op("dve", fn, reads=reads, writes=writes)

    wseq = []
    wstate = {"issued": 0, "used": 0}

    def w_issue_upto(k):
        while wstate["issued"] < min(k, len(wseq)):
            i = wstate["issued"]
            slot = wring[i % NW]
            src = wseq[i]
            n = src.shape[-1]
            S.op(WQ, lambda e, a=slot.ap[:, 0:n], b=src: e.dma_start(out=a, in_=b),
                 reads=[], writes=[slot], dma=True, chan=slot.chan)
            wstate["issued"] += 1

    def w_next(expect):
        i = wstate["used"]
        if STOP:
            wseq.append(expect)
        assert wseq[i].offset == expect.offset and wseq[i].tensor.name == expect.tensor.name, (i,)
        w_issue_upto(i + NW)
        wstate["used"] += 1
        return wring[i % NW]

    def seq_kv():
        return [d_win[t] for t in (8, 9, 10, 11, 20, 21, 22, 23, 16, 17, 18, 19)]

    def seq_main():
        l = [d_win[t] for t in list(range(0, 8)) + [12, 13, 14, 15, 24, 25, 26, 27]]
        for ec in range(16):
            l += [d_win[28 + b * 16 + ec] for b in range(3)]
            l += [d_wpa[ec], d_wpb[ec], d_wpm[ec]]
        l += [d_wout[ct] for ct in range(16)]
        l += [d_wpq[ct] for ct in range(16)]
        return l

    for s in range(nseg):
        wseq += [d_wmkv[t] for t in range(8)]
        for j in range(min(LOOK + 2, nbs[s] + 4)):
            wseq += seq_kv()
        for i in range(nbs[s]):
            if i + 2 + LOOK < nbs[s] + 4:
                wseq += seq_kv()
            wseq += seq_main()

    if STOP:
        wseq.clear()
    dma_load(ident, ident.ap, d_ident)
    dma_load(identS, identS.ap, d_identS)
    dma_load(iota16, iota16.ap, d_iota)
    dma_load(keysT, keysT.ap, d_keysT)
    dma_load(bgate, bgate.ap, d_bg)
    dma_load(sink, sink.ap, bcast_rows(d_sink, 0, 8))
    act(sinke, sinke.ap, sink, sink.ap, ACT.Exp)
    dve(lambda e: e.memset(c4.ap, 4), [], [c4])
    dve(lambda e: e.memset(c15.ap, 15), [], [c15])
    for r in range(KVR):
        dve(lambda e, t=va[r]: e.memset(t.ap, 1.0), [], [va[r]])
        dve(lambda e, t=vb[r]: e.memset(t.ap, 1.0), [], [vb[r]])
    dve(lambda e: e.memset(vm.ap, 1.0), [], [vm])

    def rope(nh, rtab):
        x1 = sb_ap(tok_tmp.ap, 0, [[128, nh], [1, 16]])
        x2 = sb_ap(tok_tmp.ap, 16, [[128, nh], [1, 16]])
        cs = sb_ap(rtab.ap, 0, [[16, nh], [1, 16]])
        sn = sb_ap(rtab.ap, 128, [[16, nh], [1, 16]])
        t = [sb_ap(rtmp.ap, k * 128, [[16, nh], [1, 16]]) for k in range(4)]
        dve(lambda e: e.tensor_tensor(t[0], x1, cs, ALU.mult), [tok_tmp, rtab], [rtmp])
        dve(lambda e: e.tensor_tensor(t[1], x2, sn, ALU.mult), [tok_tmp, rtab], [rtmp])
        dve(lambda e: e.tensor_tensor(t[2], x2, cs, ALU.mult), [tok_tmp, rtab], [rtmp])
        dve(lambda e: e.tensor_tensor(t[3], x1, sn, ALU.mult), [tok_tmp, rtab], [rtmp])
        dve(lambda e: e.tensor_tensor(x1, t[0], t[1], ALU.subtract), [rtmp], [tok_tmp])
        dve(lambda e: e.tensor_tensor(x2, t[2], t[3], ALU.add), [rtmp], [tok_tmp])

    def tokmajor(xT_t, wtiles_dram, ps):
        for q, wd in enumerate(wtiles_dram):
            wt = w_next(wd)
            pairs = [(xT_t.ap[:, kc * 128:(kc + 1) * 128], wt.ap[:, kc * 128:(kc + 1) * 128])
                     for kc in range(KC)]
            mm(ps, ps.ap[:, q * 128:(q + 1) * 128], pairs, [xT_t, wt])

    def featmajor(xT_t, wtiles_dram, ps, kcs=KC, xoff=0):
        for q, wd in enumerate(wtiles_dram):
            wt = w_next(wd)
            pairs = [(wt.ap[:, kc * 128:(kc + 1) * 128],
                      xT_t.ap[:, (xoff + kc) * 128:(xoff + kc + 1) * 128]) for kc in range(kcs)]
            mm(ps, ps.ap[:, q * 128:(q + 1) * 128], pairs, [xT_t, wt])

    def kv_stage(s, j):
        if STOP in (10, 20):
            return
        slot = j % KVR
        dma_load(xTkv, xTkv.ap, d_xT[s][j])
        dma_load(ropekv, ropekv.ap, d_rope[s][j])
        ps = nxt("mm", ps_mm)
        tokmajor(xTkv, [d_win[8], d_win[9], d_win[10], d_win[11]], ps)
        act(tok_tmp, tok_tmp.ap[:, 0:256], ps, ps.ap[:, 0:256], ACT.Copy)
        dve(lambda e, o=sb_ap(va[slot].ap, 0, [[129, 2], [1, 128]]),
            i=sb_ap(ps.ap, 256, [[128, 2], [1, 128]]): e.tensor_copy(o, i), [ps], [va[slot]])
        if STOP == 21:
            return
        rope(2, ropekv)
        ps2 = nxt("mm", ps_mm)
        for h in range(2):
            transpose(ps2, ps2.ap[:, h * 128:(h + 1) * 128], tok_tmp.ap[:, h * 128:(h + 1) * 128],
                      [tok_tmp])
        act(kTa[slot], kTa[slot].ap, ps2, ps2.ap[:, 0:256], ACT.Copy)
        if STOP == 22:
            return
        ps = nxt("mm", ps_mm)
        tokmajor(xTkv, [d_win[20 + q] for q in range(4)], ps)
        dve(lambda e, o=sb_ap(vb[slot].ap, 0, [[129, 4], [1, 128]]),
            i=sb_ap(ps.ap, 0, [[128, 4], [1, 128]]): e.tensor_copy(o, i), [ps], [vb[slot]])
        if STOP == 23:
            return
        ps = nxt("mm", ps_mm)
        featmajor(xTkv, [d_win[16 + q] for q in range(4)], ps)
        act(kTb[slot], kTb[slot].ap, ps, ps.ap, ACT.Copy)

    def attn_all(heads):
        chunks = []
        for hd in heads:
            nk = len(hd["keys"])
            done = 0
            first = True
            while done < nk:
                n = min(3, nk - done)
                chunks.append((hd, done, n, first, done + n == nk))
                first = False
                done += n
        state = {}

        def emit_S(ck):
            hd, j0, n, first, last = ck
            if first and hd.get("pre") is not None:
                hd["pre"]()
            pss = nxt("s", ps_s)
            for jj in range(n):
                kT_ap, kt_t, _, _ = hd["keys"][j0 + jj]
                o = pss.ap[:, jj * 128:(jj + 1) * 128]
                if hd["mask"] is None:
                    mm(pss, o, [(kT_ap, hd["q_ap"])], [kt_t, QT])
                else:
                    m_ap, m_t = hd["mask"](j0 + jj)
                    S.op("pe", lambda e, o=o, l=kT_ap, r=hd["q_ap"]: e.matmul(o, l, r, start=True, stop=False),
                         reads=[kt_t, QT], writes=[pss])
                    S.op("pe", lambda e, o=o, r=m_ap: e.matmul(o, identS.ap, r, start=False, stop=True),
                         reads=[identS, m_t], writes=[pss])
            p = nxt("p", p_sb)
            act(p, p.ap[:, 0:n * 128], pss, pss.ap[:, 0:n * 128], ACT.Exp, scale=SCALE)
            state[id(ck)] = p

        def emit_PV(ck):
            hd, j0, n, first, last = ck
            p = state.pop(id(ck))
            if first:
                hd["po"] = nxt("o", ps_o)
            po = hd["po"]
            for jj in range(n):
                _, _, v_ap, v_t = hd["keys"][j0 + jj]
                S.op("pe", lambda e, o=po.ap[:, 0:129], l=p.ap[:, jj * 128:(jj + 1) * 128], r=v_ap,
                     st=(first and jj == 0), sp=(last and jj == n - 1): e.matmul(o, l, r, start=st, stop=sp),
                     reads=[p, v_t], writes=[po])
            if last:
                c = hd["col"]
                act(o_tok, o_tok.ap[:, c * 128:(c + 1) * 128], po, po.ap[:, 0:128], ACT.Copy)
                act(dens, dens.ap[:, c:c + 1], po, po.ap[:, 128:129], ACT.Copy)

        emit_S(chunks[0])
        for k, ck in enumerate(chunks):
            if k + 1 < len(chunks):
                emit_S(chunks[k + 1])
            emit_PV(ck)
            if ck[4]:
                yield

    def attn_finish():
        dve(lambda e: e.tensor_tensor(dens.ap[:, 0:8], dens.ap[:, 0:8], sinke.ap, ALU.add), [dens, sinke], [dens])
        dve(lambda e: e.reciprocal(dens.ap[:, 16:32], dens.ap[:, 0:16]), [dens], [dens])
        dve(lambda e: e.tensor_tensor(sb_ap(o_tok.ap, 0, [[128, 16], [1, 128]]),
                                      sb_ap(o_tok.ap, 0, [[128, 16], [1, 128]]),
                                      sb_ap(dens.ap, 16, [[1, 16], [0, 128]]), ALU.mult), [o_tok, dens], [o_tok])

    def layernorm(xt, grow, brow, lnA, lnB, stats):
        dma_load(lnA, lnA.ap, bcast_rows(d_ln, grow, D))
        dma_load(lnB, lnB.ap, bcast_rows(d_ln, brow, D))
        for q in range(4):
            dve(lambda e, q=q: e.bn_stats(stats.ap[:, q * 6:(q + 1) * 6], xt.ap[:, q * 512:(q + 1) * 512]),
                [xt], [stats])
        dve(lambda e: e.bn_aggr(stats.ap[:, 24:26], stats.ap[:, 0:24]), [stats], [stats])
        dve(lambda e: e.tensor_scalar(stats.ap[:, 26:27], stats.ap[:, 25:26], LN_EPS, None, ALU.add),
            [stats], [stats])
        act(stats, stats.ap[:, 27:28], stats, stats.ap[:, 26:27], ACT.Sqrt)
        dve(lambda e: e.reciprocal(stats.ap[:, 28:29], stats.ap[:, 27:28]), [stats], [stats])
        dve(lambda e: e.tensor_scalar(xt.ap, xt.ap, stats.ap[:, 24:25], stats.ap[:, 28:29],
                                      ALU.subtract, ALU.mult), [xt, stats], [xt])
        dve(lambda e: e.tensor_tensor(xt.ap, xt.ap, lnA.ap, ALU.mult), [xt, lnA], [xt])
        dve(lambda e: e.tensor_tensor(xt.ap, xt.ap, lnB.ap, ALU.add), [xt, lnB], [xt])

    def top16(src_t, src_ap, n, scratch_ap, outv_t, outv_ap, outi_t, outi_ap):
        dve(lambda e: e.max(outv_ap[:, 0:8], src_ap), [src_t], [outv_t])
        dve(lambda e: e.max_index(outi_ap[:, 0:8], outv_ap[:, 0:8], src_ap), [src_t, outv_t], [outi_t])
        dve(lambda e: e.match_replace(scratch_ap, outv_ap[:, 0:8], src_ap, -3.0e38),
            [src_t, outv_t], [s2])
        dve(lambda e: e.max(outv_ap[:, 8:16], scratch_ap), [s2], [outv_t])
        dve(lambda e: e.max_index(outi_ap[:, 8:16], outv_ap[:, 8:16], scratch_ap), [s2, outv_t], [outi_t])

    def seg_prologue(s):
        nb = nbs[s]
        halves = (xTkv, xTq)
        dma_load(amask, amask.ap, d_amask[:, s * 1152:(s + 1) * 1152])
        for mb in range(2):
            src = bass.AP(d_memT.tensor, d_memT.offset + s * 128 * KC * 256 + mb * 128,
                          [[KC * 256, 128], [256, KC], [1, 128]])
            dma_load(halves[mb], sb_ap(halves[mb].ap, 0, [[128, KC], [1, 128]]), src)
        for t in range(8):
            wt = w_next(d_wmkv[t])
            if t < 4:
                ps = nxt("mm", ps_mm)
                for mb in range(2):
                    pairs = [(wt.ap[:, kc * 128:(kc + 1) * 128], halves[mb].ap[:, kc * 128:(kc + 1) * 128])
                             for kc in range(KC)]
                    mm(ps, ps.ap[:, mb * 128:(mb + 1) * 128], pairs, [halves[mb], wt])
                act(kTm, kTm.ap[:, t * 256:(t + 1) * 256], ps, ps.ap[:, 0:256], ACT.Copy)
            else:
                h = t - 4
                ps = nxt("mm", ps_mm)
                for mb in range(2):
                    pairs = [(halves[mb].ap[:, kc * 128:(kc + 1) * 128], wt.ap[:, kc * 128:(kc + 1) * 128])
                             for kc in range(KC)]
                    mm(ps, ps.ap[:, mb * 128:(mb + 1) * 128], pairs, [halves[mb], wt])
                for mb in range(2):
                    o = vm.ap[:, (mb * 4 + h) * 129:(mb * 4 + h) * 129 + 128]
                    act(vm, o, ps, ps.ap[:, mb * 128:(mb + 1) * 128], ACT.Copy)
            yield
        for j in range(min(LOOK + 2, nb + 4)):
            kv_stage(s, j)
            yield

    def phase1(s, i, gblk):
        nb = nbs[s]
        j = i + 2
        xt = xtok[gblk % 2]
        idxu = idxu2[gblk % 2]
        gts = gts2[gblk % 2]
        if i == 0:
            yield from seg_prologue(s)
        jn = i + 2 + LOOK
        if jn < nb + 4:
            kv_stage(s, jn)
            yield
        dma_load(xTq, xTq.ap, d_xT[s][j])
        dma_load(ropeq, ropeq.ap, d_rope[s][j])
        S.op("sp", lambda e, a=xt.ap, b=d_xtok[s][i]: e.dma_start(out=a, in_=b),
             reads=[], writes=[xt], dma=True, chan=xt.chan)
        for half in range(2):
            ps = nxt("mm", ps_mm)
            tokmajor(xTq, [d_win[half * 4 + q] for q in range(4)], ps)
            act(tok_tmp, tok_tmp.ap[:, half * 512:(half + 1) * 512], ps, ps.ap, ACT.Copy)
            yield
        rope(8, ropeq)
        for half in range(2):
            ps = nxt("mm", ps_mm)
            for q in range(4):
                h = half * 4 + q
                transpose(ps, ps.ap[:, q * 128:(q + 1) * 128], tok_tmp.ap[:, h * 128:(h + 1) * 128],
                          [tok_tmp])
            act(QT, QT.ap[:, half * 512:(half + 1) * 512], ps, ps.ap, ACT.Copy)
        yield
        for grp, t0 in ((2, 12), (3, 24)):
            ps = nxt("mm", ps_mm)
            featmajor(xTq, [d_win[t0 + q] for q in range(4)], ps)
            act(QT, QT.ap[:, grp * 512:(grp + 1) * 512], ps, ps.ap, ACT.Copy)
            yield
        aslot = 0 if i == 0 else (2 if i == nb - 1 else 1)
        am_off = aslot * 384
        heads = []
        for h in range(8):
            hk = h // 4
            kl = []
            for d in (-1, 0, 1):
                sl = (j + d) % KVR
                kl.append((kTa[sl].ap[:, hk * 128:(hk + 1) * 128], kTa[sl],
                           va[sl].ap[:, hk * 129:(hk + 1) * 129], va[sl]))
            heads.append(dict(q_ap=QT.ap[:, h * 128:(h + 1) * 128], keys=kl, col=h,
                              mask=lambda jj, o=am_off: (amask.ap[:, o + jj * 128:o + (jj + 1) * 128], amask)))
        if i == 0:
            bslot, dl = 0, list(range(-2, 4))
        elif i == nb - 1:
            bslot, dl = 4, list(range(-3, 3))
        else:
            bslot = 1 if i == 1 else (3 if i == nb - 2 else 2)
            dl = list(range(-2, 3))
        d0 = dl[0] + 3
        for h in range(4):
            kl = []
            for d in dl:
                sl = (j + d) % KVR
                kl.append((kTb[sl].ap[:, h * 128:(h + 1) * 128], kTb[sl],
                           vb[sl].ap[:, h * 129:(h + 1) * 129], vb[sl]))
            bo = (h % 2) * 1024
            heads.append(dict(q_ap=QT.ap[:, (8 + h) * 128:(9 + h) * 128], keys=kl, col=8 + h,
                              pre=lambda bo=bo, h=h: dma_load(B4, B4.ap[:, bo:bo + 896],
                                                              d_btab[s * 5 + bslot][:, h * 896:(h + 1) * 896]),
                              mask=lambda jj, bo=bo, d0=d0: (B4.ap[:, bo + (d0 + jj) * 128:bo + (d0 + jj + 1) * 128], B4)))
        for h in range(4):
            kl = []
            for mb in range(2):
                kl.append((kTm.ap[:, h * 256 + mb * 128:h * 256 + (mb + 1) * 128], kTm,
                           vm.ap[:, (mb * 4 + h) * 129:(mb * 4 + h + 1) * 129], vm))
            heads.append(dict(q_ap=QT.ap[:, (12 + h) * 128:(13 + h) * 128], keys=kl, col=12 + h, mask=None))
        yield from attn_all(heads)
        attn_finish()
        for grp in range(4):
            ps = nxt("mm", ps_mm)
            for q in range(4):
                c = grp * 4 + q
                transpose(ps, ps.ap[:, q * 128:(q + 1) * 128], o_tok.ap[:, c * 128:(c + 1) * 128], [o_tok])
            act(oT, oT.ap[:, grp * 512:(grp + 1) * 512], ps, ps.ap, ACT.Copy)
        yield
        for ec in range(16):
            psg = nxt("mm", ps_mm)
            featmajor(xTq, [d_win[28 + b * 16 + ec] for b in range(3)], psg)
            for b in range(3):
                act(g_sb, g_sb.ap[:, b * 128:(b + 1) * 128], psg, psg.ap[:, b * 128:(b + 1) * 128],
                    ACT.Sigmoid, extra_reads=[bgate], bias=bgate.ap[:, b * 16 + ec:b * 16 + ec + 1])
            psp = nxt("mm", ps_mm)
            for b, (wd, kcs, xoff) in enumerate(((d_wpa[ec], 8, 0), (d_wpb[ec], 4, 8), (d_wpm[ec], 4, 12))):
                wt = w_next(wd)
                pairs = [(wt.ap[:, kc * 128:(kc + 1) * 128], oT.ap[:, (xoff + kc) * 128:(xoff + kc + 1) * 128])
                         for kc in range(kcs)]
                mm(psp, psp.ap[:, b * 128:(b + 1) * 128], pairs, [oT, wt])
            dve(lambda e, pp=psp: e.tensor_tensor(gm_sb.ap, pp.ap[:, 0:384], g_sb.ap, ALU.mult),
                [psp, g_sb], [gm_sb])
            mo = mT.ap[:, ec * 128:(ec + 1) * 128]
            dve(lambda e, mo=mo: e.tensor_reduce(mo, sb_ap(gm_sb.ap, 0, [[1, 128], [128, 3]]), AX.X, ALU.add),
                [gm_sb], [mT])
            yield
        for grp in range(4):
            ps = nxt("mm", ps_mm)
            tokmajor(mT, [d_wout[grp * 4 + q] for q in range(4)], ps)
            xs = xt.ap[:, grp * 512:(grp + 1) * 512]
            dve(lambda e, xs=xs, ps=ps: e.scalar_tensor_tensor(xs, xs, ALPHA, ps.ap, ALU.mult, ALU.add),
                [xt, ps], [xt])
            yield
        layernorm(xt, 0, 1, B4, B1, stats)
        yield
        x1T = xTq
        for grp in range(4):
            ps = nxt("mm", ps_mm)
            for q in range(4):
                c = grp * 4 + q
                transpose(ps, ps.ap[:, q * 128:(q + 1) * 128], xt.ap[:, c * 128:(c + 1) * 128], [xt])
            act(x1T, x1T.ap[:, grp * 512:(grp + 1) * 512], ps, ps.ap, ACT.Copy)
        yield
        qTp = QT
        for grp in range(4):
            ps = nxt("mm", ps_mm)
            featmajor(x1T, [d_wpq[grp * 4 + q] for q in range(4)], ps)
            act(qTp, qTp.ap[:, grp * 512:(grp + 1) * 512], ps, ps.ap, ACT.Copy)
            yield
        s_sb = B4
        for grp in range(4):
            ps = nxt("mm", ps_mm)
            for q in range(4):
                g = grp * 4 + q
                side = g % 2
                mm(ps, ps.ap[:, q * 128:(q + 1) * 128],
                   [(qTp.ap[:, g * 128:(g + 1) * 128], keysT.ap[:, side * 128:(side + 1) * 128])],
                   [qTp, keysT])
            act(s_sb, s_sb.ap[:, grp * 512:(grp + 1) * 512], ps, ps.ap, ACT.Copy)
        yield
        for g in range(16):
            top16(s_sb, s_sb.ap[:, g * 128:(g + 1) * 128], 128, s2.ap[:, 0:128],
                  tv, tv.ap[:, g * 16:(g + 1) * 16], ti, ti.ap[:, g * 16:(g + 1) * 16])
            if g % 2 == 1:
                yield
        cand = B1
        dve(lambda e: e.tensor_tensor(sb_ap(cand.ap, 0, [[256, 8], [16, 16], [1, 16]]),
                                      sb_ap(tv.ap, 0, [[32, 8], [1, 16], [0, 16]]),
                                      sb_ap(tv.ap, 16, [[32, 8], [0, 16], [1, 16]]), ALU.add),
            [tv], [cand])
        for h in range(8):
            top16(cand, cand.ap[:, h * 256:(h + 1) * 256], 256, s2.ap[:, 0:256],
                  topv, topv.ap[:, h * 16:(h + 1) * 16], topi, topi.ap[:, h * 16:(h + 1) * 16])
            if h % 2 == 1:
                yield
        dve(lambda e: e.tensor_tensor(ai.ap, topi.ap, sb_ap(c4.ap, 0, [[0, 128]]), ALU.logical_shift_right),
            [topi, c4], [ai])
        dve(lambda e: e.tensor_copy(af.ap, ai.ap), [ai], [af])
        dve(lambda e: e.tensor_tensor(ai.ap, topi.ap, sb_ap(c15.ap, 0, [[0, 128]]), ALU.bitwise_and),
            [topi, c15], [ai])
        dve(lambda e: e.tensor_copy(bf.ap, ai.ap), [ai], [bf])
        dve(lambda e: e.tensor_copy(tif.ap, ti.ap), [ti], [tif])
        yield
        oh = B4
        for (pf, off, dst) in ((af, 0, isel1), (bf, 16, isel2)):
            dve(lambda e, pf=pf: e.tensor_tensor(sb_ap(oh.ap, 0, [[16, 128], [1, 16]]),
                                                 sb_ap(pf.ap, 0, [[1, 128], [0, 16]]),
                                                 sb_ap(iota16.ap, 0, [[0, 128], [1, 16]]), ALU.is_equal),
                [pf, iota16], [oh])
            dve(lambda e, off=off: e.tensor_tensor(sb_ap(oh.ap, 0, [[256, 8], [16, 16], [1, 16]]),
                                                   sb_ap(oh.ap, 0, [[256, 8], [16, 16], [1, 16]]),
                                                   sb_ap(tif.ap, off, [[32, 8], [0, 16], [1, 16]]), ALU.mult),
                [oh, tif], [oh])
            dve(lambda e, dst=dst: e.tensor_reduce(dst.ap, sb_ap(oh.ap, 0, [[16, 128], [1, 16]]),
                                                   AX.X, ALU.add), [oh], [dst])
            yield
        dve(lambda e: e.scalar_tensor_tensor(idxf.ap, isel1.ap, 128.0, isel2.ap, ALU.mult, ALU.add),
            [isel1, isel2], [idxf])
        dve(lambda e: e.tensor_copy(idxu.ap, idxf.ap), [idxf], [idxu])
        dve(lambda e: e.tensor_tensor(sb_ap(gts.ap, 0, [[16, 8], [1, 16]]),
                                      sb_ap(topv.ap, 0, [[16, 8], [1, 16]]),
                                      sb_ap(topv.ap, 0, [[16, 8], [0, 16]]), ALU.subtract), [topv], [gts])
        act(gts, gts.ap, gts, gts.ap, ACT.Exp)
        dve(lambda e: e.tensor_reduce(gsum.ap[:, 0:8], sb_ap(gts.ap, 0, [[16, 8], [1, 16]]), AX.X, ALU.add),
            [gts], [gsum])
        dve(lambda e: e.reciprocal(gsum.ap[:, 8:16], gsum.ap[:, 0:8]), [gsum], [gsum])
        dve(lambda e: e.tensor_tensor(sb_ap(gts.ap, 0, [[16, 8], [1, 16]]),
                                      sb_ap(gts.ap, 0, [[16, 8], [1, 16]]),
                                      sb_ap(gsum.ap, 8, [[1, 8], [0, 16]]), ALU.mult), [gts, gsum], [gts])
        yield

    def phase2(s, i, gblk):
        xt = xtok[gblk % 2]
        idxu = idxu2[gblk % 2]
        gts = gts2[gblk % 2]

        def gather(buf, table, col):
            return S.op("pool", lambda e, b=buf, c=col: e.indirect_dma_start(
                out=b.ap, out_offset=None, in_=table,
                in_offset=bass.IndirectOffsetOnAxis(ap=idxu.ap[:, c:c + 1], axis=0)),
                reads=[idxu], writes=[buf], dma=True, chan=gchan[id(buf)])

        for c in range(128):
            ub = gbuf[c % NG]
            gather(ub, d_pu, c)
            dve(lambda e, ub=ub, c=c: e.scalar_tensor_tensor(
                ub.ap, ub.ap, 1.0, xt.ap, ALU.mult, ALU.mult, accum_out=a_sb.ap[:, c:c + 1]),
                [ub, xt], [ub, a_sb])
            yield
        act(w_sb, w_sb.ap, a_sb, a_sb.ap, ACT.Gelu)
        dve(lambda e: e.tensor_tensor(w_sb.ap, w_sb.ap, gts.ap, ALU.mult), [w_sb, gts], [w_sb])
        dve(lambda e: e.tensor_scalar(xt.ap, xt.ap, ALPHA, None, ALU.mult), [xt], [xt])
        for c in range(128):
            vbf = gbuf[(128 + c) % NG]
            gather(vbf, d_pv, c)
            dve(lambda e, vbf=vbf, c=c: e.scalar_tensor_tensor(
                xt.ap, vbf.ap, w_sb.ap[:, c:c + 1], xt.ap, ALU.mult, ALU.add),
                [vbf, w_sb, xt], [xt])
            yield
        layernorm(xt, 2, 3, gbuf[0], gbuf[1], stats2)
        st = S.op("sp", lambda e, a=d_y[s][i], b=xt.ap: e.dma_start(out=a, in_=b),
                  reads=[xt], writes=[], dma=True, chan=xtok_st[gblk % 2])
        S.store_ops.append(st)
        yield

    blocks = [(s, i) for s in range(nseg) for i in range(nbs[s])]
    for _ in phase1(blocks[0][0], blocks[0][1], 0):
        pass
    for g, (s, i) in enumerate(blocks):
        p2 = phase2(s, i, g)
        p1 = phase1(blocks[g + 1][0], blocks[g + 1][1], g + 1) if g + 1 < len(blocks) else None
        alive2, alive1 = True, p1 is not None
        while alive2 or alive1:
            for _ in range(P2_STEPS):
                if alive2 and next(p2, "done") == "done":
                    alive2 = False
            if alive1 and next(p1, "done") == "done":
                alive1 = False
    assert wstate["used"] == len(wseq), (wstate, len(wseq))

    S.emit(nc, es)
    es.close()
    return nc


def _tile_w(W, kcn):
    K, N = W.shape
    assert K == kcn * 128
    return np.ascontiguousarray(W.reshape(kcn, 128, N // 128, 128).transpose(2, 1, 0, 3)).reshape(
        N // 128, 128, kcn * 128)


def _btab(rpb, gi, nbt):
    rows = 2 * nbt
    wr = min(8, rows)
    k = np.arange(128)
    q = np.arange(128)
    kr2, kc = k // 64, k % 64
    qr2, qc = q // 64, q % 64
    cs = np.clip(qc - 8, 0, 64 - 16)
    col_ok = (kc[:, None] >= cs[None, :]) & (kc[:, None] < cs[None, :] + 16)
    dc = np.clip(kc[:, None] - qc[None, :], -15, 15) + 15
    out = np.full((128, 4, 7, 128), NEG, np.float32)
    for d in range(-3, 4):
        gk = gi + d
        krg = 2 * gk + kr2
        r = 2 * gi + qr2
        rs = np.clip(r - wr // 2, 0, rows - wr)
        ok = (krg[:, None] >= 0) & (krg[:, None] < rows) & (krg[:, None] >= rs[None, :]) \
            & (krg[:, None] < rs[None, :] + wr) & col_ok
        dr = np.clip(krg[:, None] - r[None, :] + 7, 0, 14)
        vals = rpb[:, dr, dc]
        out[:, :, d + 3, :] = np.where(ok[:, None, :], vals.transpose(1, 0, 2), np.float32(NEG))
    return out.reshape(128, 4 * 7 * 128)


def _amask(gi, nbt):
    k = np.arange(128)[:, None]
    q = np.arange(128)[None, :]
    out = np.zeros((128, 3, 128), np.float32)
    for jj, d in enumerate((-1, 0, 1)):
        ok = (np.abs(d * 128 + k - q) <= 128) & (0 <= gi + d < nbt)
        out[:, jj, :] = np.where(ok, np.float32(0.0), np.float32(NEG))
    return out.reshape(128, 384)


def _rope_tab(pos):
    half = 16
    inv = (np.float32(ROPE_THETA) ** (-np.arange(half, dtype=np.float32) * np.float32(2.0) / np.float32(32))).astype(np.float32)
    ang = pos.astype(np.float32)[:, None] * inv[None, :]
    c = np.cos(ang).astype(np.float32)
    s = np.sin(ang).astype(np.float32)
    return np.concatenate([np.tile(c, (1, 8)), np.tile(s, (1, 8))], axis=1)


def prep_shared(inp):
    w_in = np.asarray(inp["w_in"][0], np.float32)
    sh = {
        "win_t": _tile_w(w_in, 16),
        "wpa_t": _tile_w(np.asarray(inp["w_proj_a"][0], np.float32), 8),
        "wpb_t": _tile_w(np.asarray(inp["w_proj_b"][0], np.float32), 4),
        "wpm_t": _tile_w(np.asarray(inp["w_proj_m"][0], np.float32), 4),
        "wout_t": _tile_w(np.asarray(inp["w_out"][0], np.float32), 16),
        "wpq_t": _tile_w(np.asarray(inp["w_peer_q"][0], np.float32), 16),
        "wmkv_t": _tile_w(np.asarray(inp["w_mem_kv"][0], np.float32), 16),
        "bgate": np.ascontiguousarray(np.asarray(inp["b_gate"][0], np.float32).reshape(48, 128).T),
        "sink": np.asarray(inp["a_sink"], np.float32).reshape(1, 8),
        "lnp": np.stack([np.asarray(inp[k][0], np.float32) for k in ("ln1_g", "ln1_b", "ln2_g", "ln2_b")]),
        "ident": np.eye(128, dtype=np.float32),
        "identS": (np.eye(128, dtype=np.float32) * np.float32(1.0 / SCALE)).astype(np.float32),
        "iota16": np.tile(np.arange(16, dtype=np.float32)[None, :], (128, 1)),
        "keysT": np.ascontiguousarray(np.concatenate(
            [np.asarray(inp["peer_keys1"][0], np.float32).T, np.asarray(inp["peer_keys2"][0], np.float32).T], axis=1)),
        "peer_u": np.asarray(inp["peer_u"][0], np.float32),
        "peer_v": np.asarray(inp["peer_v"][0], np.float32),
    }
    return sh


def prep_segment(xseq, tok0, ntok, mem, rpb):
    T = xseq.shape[0]
    nbt = T // 128
    nb = ntok // 128
    g0 = tok0 // 128
    xpad = np.zeros(((nb + 4) * 128, D), np.float32)
    lo, hi = tok0 - 256, tok0 + ntok + 256
    slo, shi = max(lo, 0), min(hi, T)
    xpad[slo - lo:shi - lo] = xseq[slo:shi]
    xT = np.ascontiguousarray(xpad.reshape(nb + 4, 128, KC, 128).transpose(0, 3, 2, 1)).reshape(nb + 4, 128, D)
    xtok = np.ascontiguousarray(xseq[tok0:tok0 + ntok].reshape(nb, 128, D))
    pos = np.arange(lo, hi)
    rope = _rope_tab(np.maximum(pos, 0)).reshape(nb + 4, 128, 256)
    memT = np.ascontiguousarray(mem.reshape(256, KC, 128).transpose(2, 1, 0)).reshape(128, KC * 256)
    bslots = [g0, g0 + 1, g0 + 2, g0 + nb - 2, g0 + nb - 1]
    btab = np.stack([_btab(rpb, gi, nbt) for gi in bslots])
    aslots = [g0, g0 + 1, g0 + nb - 1]
    am = np.concatenate([_amask(gi, nbt) for gi in aslots], axis=1)
    return xT, xtok, rope, memT, btab, am


def make_in_map(shared, segs):
    m = dict(shared)
    for s, (xT, xtok, rope, memT, btab, am) in enumerate(segs):
        m["xT%d" % s] = xT
        m["xtok%d" % s] = xtok
        m["rope%d" % s] = rope
    m["memT"] = np.stack([sg[3] for sg in segs])
    m["btab"] = np.concatenate([sg[4] for sg in segs], axis=0)
    m["amask"] = np.ascontiguousarray(np.concatenate([sg[5] for sg in segs], axis=1))
    return m


_NC_CACHE = {}


def kernel(x_prompt, x_sample, mem_prompt, mem_sample, w_in, b_gate, a_sink, na_rpb, w_mem_kv,
           w_proj_a, w_proj_b, w_proj_m, w_out, ln1_g, ln1_b, w_peer_q, peer_keys1, peer_keys2,
           peer_u, peer_v, ln2_g, ln2_b):
    inp = dict(w_in=w_in, b_gate=b_gate, a_sink=a_sink, w_mem_kv=w_mem_kv, w_proj_a=w_proj_a,
               w_proj_b=w_proj_b, w_proj_m=w_proj_m, w_out=w_out, ln1_g=ln1_g, ln1_b=ln1_b,
               w_peer_q=w_peer_q, peer_keys1=peer_keys1, peer_keys2=peer_keys2, peer_u=peer_u,
               peer_v=peer_v, ln2_g=ln2_g, ln2_b=ln2_b)
    n = 8
    shared = prep_shared(inp)
    rpb = np.asarray(na_rpb[0], np.float32)
    xs = np.asarray(x_sample, np.float32)
    xp = np.asarray(x_prompt, np.float32)[0]
    ms = np.asarray(mem_sample, np.float32)
    mp = np.asarray(mem_prompt, np.float32)[0]
    TP = xp.shape[0] // n
    in_maps = []
    for c in range(n):
        seg0 = prep_segment(xs[c], 0, xs.shape[1], ms[c], rpb)
        seg1 = prep_segment(xp, c * TP, TP, mp, rpb)
        in_maps.append(make_in_map(shared, [seg0, seg1]))
    nbs = (xs.shape[1] // 128, TP // 128)
    nc = build_program(list(nbs))
    res = run_bass_kernel_spmd(nc, in_maps, core_ids=list(range(n)))
    y_s = np.stack([np.asarray(res.results[c]["y0"]).reshape(-1, D) for c in range(n)]).astype(np.float32)
    y_p = np.concatenate([np.asarray(res.results[c]["y1"]).reshape(-1, D) for c in range(n)], axis=0)[None]
    return (y_p.astype(np.float32), y_s)
```
